# Optimizing a Trainium2 kernel written in Bass

```python
import math
import jax, jax.numpy as jnp
from jax import lax
import numpy as np

D_MODEL = 1024
BATCH = 32
SEQ = 2048
DEPTH = 2

N_MEM = 256
MIX_WIDTH = D_MODEL
HGRN_WIDTH = D_MODEL // 2
HGRN_HEAD_DIM = 128
HGRN_HEADS = HGRN_WIDTH // HGRN_HEAD_DIM
HGRN_CHUNK = 64
ATTN_WIDTH = MIX_WIDTH - HGRN_WIDTH
ATTN_HEAD_DIM = 64
ATTN_HEADS = ATTN_WIDTH // ATTN_HEAD_DIM
DILATED_PATTERNS = ((128, 1), (512, 4), (2048, 16))
ATTN_BLOCK = 64
MEM_HEADS = 4
MEM_HEAD_DIM = D_MODEL // MEM_HEADS
D_FF = 2816
FFN_RES = 0.5
EPS = 1e-6
IN_SPLITS = (HGRN_WIDTH, HGRN_WIDTH, HGRN_WIDTH, HGRN_WIDTH, HGRN_WIDTH,
             ATTN_WIDTH, ATTN_WIDTH, ATTN_WIDTH)
IN_COLS = sum(IN_SPLITS)

kernel_name = "hybrid_hgrn2_dilated_attn_macaron_encoder"


def rms_norm(x, g):
    xf = x.astype(jnp.float32)
    y = xf * lax.rsqrt(jnp.mean(xf * xf, axis=-1, keepdims=True) + EPS)
    return (y * g.astype(jnp.float32)).astype(x.dtype)


def swiglu(x, w_gate, w_up, w_down):
    return (jax.nn.silu(x @ w_gate) * (x @ w_up)) @ w_down


def alibi_slopes(n_heads):
    return jnp.asarray(np.array([2.0 ** (-8.0 * (h + 1) / n_heads) for h in range(n_heads)],
                                dtype=np.float32))


def hgrn2_chunk_scan(q, k, v, log_f):
    B, S, H, Dk = q.shape
    Dv = v.shape[-1]
    n_chunks = S // HGRN_CHUNK

    def to_chunks(a):
        return a.reshape(B, n_chunks, HGRN_CHUNK, H, a.shape[-1]).transpose(1, 0, 3, 2, 4)

    xs = tuple(to_chunks(a) for a in (q, k, v, log_f))
    causal_in_chunk = jnp.tril(jnp.ones((HGRN_CHUNK, HGRN_CHUNK), dtype=bool))[:, :, None]

    def step(state, inp):
        qc, kc, vc, gc = inp
        A = jnp.cumsum(gc, axis=2)
        diff = A[:, :, :, None, :] - A[:, :, None, :, :]
        decay = jnp.exp(jnp.where(causal_in_chunk, diff, -jnp.inf))
        scores = jnp.einsum('bhtk,bhsk,bhtsk->bhts', qc, kc, decay)
        o = (jnp.einsum('bhts,bhsv->bhtv', scores, vc)
             + jnp.einsum('bhtk,bhkv->bhtv', qc * jnp.exp(A), state))
        A_last = A[:, :, -1:, :]
        state = (jnp.exp(A_last[:, :, 0, :])[..., None] * state
                 + jnp.einsum('bhsk,bhsv->bhkv', kc * jnp.exp(A_last - A), vc))
        return state, o

    s0 = jnp.zeros((B, H, Dk, Dv), jnp.float32)
    _, o = lax.scan(step, s0, xs)
    return o.transpose(1, 0, 3, 2, 4).reshape(B, S, H, Dv)


def hgrn2_forget(z, lb):
    log_f = jnp.logaddexp(jnp.log(lb), jnp.log1p(-lb) + jax.nn.log_sigmoid(z))
    one_minus_f = (1.0 - lb) * jax.nn.sigmoid(-z)
    return log_f, one_minus_f


def hgrn2_mixer(q, i, z_fwd, z_bwd, g, lb_fwd, lb_bwd, out_gain):
    B, S, _ = q.shape
    heads = lambda a: a.astype(jnp.float32).reshape(B, S, HGRN_HEADS, HGRN_HEAD_DIM)
    qh, ih = heads(q), heads(i)
    lbf = lb_fwd.astype(jnp.float32).reshape(HGRN_HEADS, HGRN_HEAD_DIM)
    lbb = lb_bwd.astype(jnp.float32).reshape(HGRN_HEADS, HGRN_HEAD_DIM)
    logf_f, k_f = hgrn2_forget(heads(z_fwd), lbf)
    logf_b, k_b = hgrn2_forget(heads(z_bwd), lbb)
    o_f = hgrn2_chunk_scan(qh, k_f, ih, logf_f)
    flip = lambda a: jnp.flip(a, axis=1)
    o_b = flip(hgrn2_chunk_scan(flip(qh), flip(k_b), flip(ih), flip(logf_b)))
    o = o_f + o_b
    o = o * lax.rsqrt(jnp.mean(o * o, axis=-1, keepdims=True) + EPS)
    o = o * out_gain.astype(jnp.float32).reshape(HGRN_HEADS, HGRN_HEAD_DIM)
    o = o.reshape(B, S, HGRN_WIDTH) * jax.nn.silu(g.astype(jnp.float32))
    return o.astype(q.dtype)


def dilated_branch(q, k, v, window, dil, slopes):
    B, S, H, Dh = q.shape
    half = window // (2 * dil)
    blk = ATTN_BLOCK
    L = S // dil
    nb = -(-L // blk)
    Lp = nb * blk

    def split(a):
        return a.reshape(B, L, dil, H, Dh).transpose(0, 2, 3, 1, 4)

    qs, ks, vs = split(q), split(k), split(v)
    qb = jnp.pad(qs, ((0, 0), (0, 0), (0, 0), (0, Lp - L), (0, 0))).reshape(B, dil, H, nb, blk, Dh)

    def band(a):
        ap = jnp.pad(a, ((0, 0), (0, 0), (0, 0), (blk, Lp - L + blk), (0, 0)))
        ap = ap.reshape(B, dil, H, nb + 2, blk, Dh)
        return jnp.concatenate([ap[:, :, :, :-2], ap[:, :, :, 1:-1], ap[:, :, :, 2:]], axis=4)

    kw, vw = band(ks), band(vs)
    qpos = jnp.arange(Lp).reshape(nb, blk)
    kpos = (jnp.arange(nb)[:, None] - 1) * blk + jnp.arange(3 * blk)[None, :]
    rel = kpos[:, None, :] - qpos[:, :, None]
    valid = (jnp.abs(rel) <= half) & (kpos[:, None, :] >= 0) & ((kpos[:, None, :] < L) | (rel == 0))
    bias = -(slopes[:, None, None, None] * (dil * jnp.abs(rel)).astype(jnp.float32))

    s = jnp.einsum('brhnqe,brhnke->brhnqk', qb, kw) * (1.0 / math.sqrt(Dh)) + bias[None, None]
    s = jnp.where(valid, s, -jnp.inf)
    m = jnp.max(s, axis=-1, keepdims=True)
    p = jnp.exp(s - m)
    l = jnp.sum(p, axis=-1, keepdims=True)
    o = jnp.einsum('brhnqk,brhnke->brhnqe', p, vw) / l
    lse = (m + jnp.log(l))[..., 0]
    o = o.reshape(B, dil, H, Lp, Dh)[:, :, :, :L].transpose(0, 3, 1, 2, 4).reshape(B, S, H, Dh)
    lse = lse.reshape(B, dil, H, Lp)[:, :, :, :L].transpose(0, 3, 1, 2).reshape(B, S, H)
    return o, lse


def dilated_attention(q, k, v):
    B, S, _ = q.shape
    heads = lambda a: a.astype(jnp.float32).reshape(B, S, ATTN_HEADS, ATTN_HEAD_DIM)
    qh, kh, vh = heads(q), heads(k), heads(v)
    slopes = alibi_slopes(ATTN_HEADS)
    outs, lses = [], []
    for window, dil in DILATED_PATTERNS:
        o, lse = dilated_branch(qh, kh, vh, window, dil, slopes)
        outs.append(o)
        lses.append(lse)
    w = jax.nn.softmax(jnp.stack(lses, axis=0), axis=0)
    o = jnp.sum(w[..., None] * jnp.stack(outs, axis=0), axis=0)
    return o.reshape(B, S, ATTN_WIDTH).astype(q.dtype)


def memory_cross_attention(hn, memn, w_q, w_kv, w_o):
    B, S, _ = hn.shape
    M = memn.shape[1]
    q = (hn @ w_q).astype(jnp.float32).reshape(B, S, MEM_HEADS, MEM_HEAD_DIM)
    k, v = jnp.split((memn @ w_kv).astype(jnp.float32), 2, axis=-1)
    k = k.reshape(B, M, MEM_HEADS, MEM_HEAD_DIM)
    v = v.reshape(B, M, MEM_HEADS, MEM_HEAD_DIM)
    s = jnp.einsum('bshe,bmhe->bhsm', q, k) * (1.0 / math.sqrt(MEM_HEAD_DIM))
    p = jax.nn.softmax(s, axis=-1)
    o = jnp.einsum('bhsm,bmhe->bshe', p, v).reshape(B, S, D_MODEL).astype(hn.dtype)
    return o @ w_o


def setup_inputs(seed: int = 0) -> dict:
    key = jax.random.key(seed)
    ks = jax.random.split(key, 24)
    nrm = lambda k, shape, scale: jax.random.normal(k, shape, jnp.float32) * scale
    gain = lambda k, shape: 1.0 + 0.05 * jax.random.normal(k, shape, jnp.float32)
    D, F = D_MODEL, D_FF
    return {
        "x": nrm(ks[0], (BATCH, SEQ, D), 1.0),
        "mem": nrm(ks[1], (BATCH, N_MEM, D), 1.0),
        "ln_ffn1": gain(ks[2], (DEPTH, D)),
        "ffn1_w_gate": nrm(ks[3], (DEPTH, D, F), D ** -0.5),
        "ffn1_w_up": nrm(ks[4], (DEPTH, D, F), D ** -0.5),
        "ffn1_w_down": nrm(ks[5], (DEPTH, F, D), F ** -0.5),
        "ln_mix": gain(ks[6], (DEPTH, D)),
        "w_in": nrm(ks[7], (DEPTH, D, IN_COLS), D ** -0.5),
        "hgrn_lb_logits": nrm(ks[8], (DEPTH, 2, HGRN_WIDTH), 0.5),
        "hgrn_out_norm": gain(ks[9], (DEPTH, HGRN_WIDTH)),
        "w_out": nrm(ks[10], (DEPTH, MIX_WIDTH, D), MIX_WIDTH ** -0.5),
        "ln_xq": gain(ks[11], (DEPTH, D)),
        "ln_mem": gain(ks[12], (DEPTH, D)),
        "w_xq": nrm(ks[13], (DEPTH, D, D), D ** -0.5),
        "w_xkv": nrm(ks[14], (DEPTH, D, 2 * D), D ** -0.5),
        "w_xo": nrm(ks[15], (DEPTH, D, D), D ** -0.5),
        "ln_ffn2": gain(ks[16], (DEPTH, D)),
        "ffn2_w_gate": nrm(ks[17], (DEPTH, D, F), D ** -0.5),
        "ffn2_w_up": nrm(ks[18], (DEPTH, D, F), D ** -0.5),
        "ffn2_w_down": nrm(ks[19], (DEPTH, F, D), F ** -0.5),
        "ln_final": gain(ks[20], (D,)),
    }


def reference(x, mem, ln_ffn1, ffn1_w_gate, ffn1_w_up, ffn1_w_down, ln_mix, w_in,
              hgrn_lb_logits, hgrn_out_norm, w_out, ln_xq, ln_mem, w_xq, w_xkv, w_xo,
              ln_ffn2, ffn2_w_gate, ffn2_w_up, ffn2_w_down, ln_final):
    lb_all = jnp.cumsum(jax.nn.softmax(hgrn_lb_logits.astype(jnp.float32), axis=0), axis=0)
    lb_all = lb_all - lb_all[0:1]
    offsets = np.cumsum(IN_SPLITS)[:-1].tolist()
    h = x
    for l in range(DEPTH):
        h = h + FFN_RES * swiglu(rms_norm(h, ln_ffn1[l]), ffn1_w_gate[l], ffn1_w_up[l], ffn1_w_down[l])
        u = rms_norm(h, ln_mix[l])
        proj = u @ w_in[l]
        q_h, i_h, zf_h, zb_h, g_h, q_a, k_a, v_a = jnp.split(proj, offsets, axis=-1)
        y_h = hgrn2_mixer(q_h, i_h, zf_h, zb_h, g_h, lb_all[l, 0], lb_all[l, 1], hgrn_out_norm[l])
        y_a = dilated_attention(q_a, k_a, v_a)
        h = h + jnp.concatenate([y_h, y_a], axis=-1) @ w_out[l]
        h = h + memory_cross_attention(rms_norm(h, ln_xq[l]), rms_norm(mem, ln_mem[l]),
                                       w_xq[l], w_xkv[l], w_xo[l])
        h = h + FFN_RES * swiglu(rms_norm(h, ln_ffn2[l]), ffn2_w_gate[l], ffn2_w_up[l], ffn2_w_down[l])
    return rms_norm(h, ln_final)
```

```python
import numpy as np
import concourse.bass as bass
import concourse.mybir as mybir
from concourse.bass_utils import run_bass_kernel_spmd

F32 = mybir.dt.float32
BF16 = mybir.dt.bfloat16
ALU = mybir.AluOpType
AF = mybir.ActivationFunctionType

D = 1024
S = 2048
NMEM = 256
DFF = 2816
DEPTH = 2
KC = 8
NT = 4
TW = 512
EPS = 1e-6
NCORES = 8
WNAMES = [("ffn1_w_gate", [DEPTH, D, DFF]), ("ffn1_w_up", [DEPTH, D, DFF]), ("ffn1_w_down", [DEPTH, DFF, D]),
          ("w_in", [DEPTH, D, 4096]), ("w_out", [DEPTH, D, D]), ("w_xq", [DEPTH, D, D]),
          ("w_xkv", [DEPTH, D, 2 * D]), ("w_xo", [DEPTH, D, D]),
          ("ffn2_w_gate", [DEPTH, D, DFF]), ("ffn2_w_up", [DEPTH, D, DFF]), ("ffn2_w_down", [DEPTH, DFF, D])]
CV_FFN1, CV_MIX, CV_XQ, CV_MEM, CV_FFN2, CV_FIN, CV_ON, CV_LB, NCV = 0, 16, 32, 48, 64, 80, 88, 96, 112
CT_ID, CT_ONE, CT_MF, CT_MB, CT_VAL, CT_V0, CT_REL, CT_R0, CT_SCAN, NCT = 0, 128, 256, 384, 512, 768, 896, 1152, 1280, 1792
EPOCH = 50000
SAME_ENG_SYNC = True


class Node:
    __slots__ = ("id", "eng", "fn", "deps", "slot", "sig", "cnt")


class Prog:
    ENGS = ["pe", "act", "dve", "pool", "sp"]

    def __init__(self, nc):
        self.nc = nc
        self.nodes = []
        self.lastw = {}
        self.readers = {}
        self.fence = {}
        self.last_on = {}
        self.last_dma = {}

    def op(self, eng, fn, r=(), w=(), slot=None):
        n = Node()
        n.id = len(self.nodes)
        n.eng = eng
        n.fn = fn
        n.slot = slot
        n.sig = slot is not None
        n.cnt = 0
        deps = set()
        w = list(w) + [k for k in r if k[0] == "ps" and k not in w]
        for k in r:
            if k in self.lastw:
                deps.add(self.lastw[k])
        for k in w:
            if k in self.lastw:
                deps.add(self.lastw[k])
            deps.update(self.readers.get(k, ()))
        if eng in self.fence:
            deps.update(self.fence.pop(eng))
        n.deps = deps
        for k in r:
            self.readers.setdefault(k, []).append(n.id)
        for k in w:
            self.lastw[k] = n.id
            self.readers[k] = []
        self.nodes.append(n)
        if slot is None:
            self.last_on[eng] = n.id
        else:
            self.last_dma[slot] = n.id
        return n.id

    def barrier(self):
        ids = set(self.last_on.values()) | set(self.last_dma.values())
        for e in self.ENGS:
            self.fence[e] = set(ids) | self.fence.get(e, set())

    def emit(self):
        nc = self.nc
        nodes = self.nodes
        for n in nodes:
            for d in n.deps:
                nodes[d].sig = True
        cnt = {}
        for n in nodes:
            if not n.sig:
                continue
            key = ("d", n.slot) if n.slot is not None else ("e", n.eng)
            cnt[key] = cnt.get(key, 0) + 1
            n.cnt = cnt[key]
        sems = {}

        def sem_for(key, c):
            ep = (c - 1) // EPOCH
            k = (key, ep)
            if k not in sems:
                sems[k] = nc.alloc_semaphore("s_%s_%s_%d" % (key[0], key[1], ep))
            return sems[k], (c - 1) % EPOCH + 1

        bname = {"pe": "tensor", "act": "scalar", "dve": "vector", "pool": "gpsimd", "sp": "sync"}
        with nc.Block() as block:
            for eng in self.ENGS:
                mine = [n for n in nodes if n.eng == eng]

                def body(e, mine=mine, eng=eng):
                    seen = {}
                    for n in mine:
                        need = {}
                        for d in n.deps:
                            dn = nodes[d]
                            if dn.slot is not None:
                                key = ("d", dn.slot)
                            else:
                                if dn.eng == eng and (eng == "pe" or not SAME_ENG_SYNC):
                                    continue
                                key = ("e", dn.eng)
                            if dn.cnt > need.get(key, 0):
                                need[key] = dn.cnt
                        for key, c in need.items():
                            if seen.get(key, 0) >= c:
                                continue
                            seen[key] = c
                            sm, v = sem_for(key, c)
                            e.wait_ge(sm, v * (16 if key[0] == "d" else 1))
                        if n.fn is None:
                            continue
                        ins = None
                        for m_, kw_ in n.fn:
                            ins = getattr(e, m_)(**kw_)
                        if n.sig:
                            key = ("d", n.slot) if n.slot is not None else ("e", n.eng)
                            sm, v = sem_for(key, n.cnt)
                            ins.then_inc(sm, 16 if n.slot is not None else 1)

                getattr(block, bname[eng])(body)


def I(m, **kw):
    return (m, kw)


def build(nseq, stop_after=None):
    nc = bass.Bass("TRN2", target_bir_lowering=False)
    P = Prog(nc)
    xT = nc.dram_tensor("xT", [nseq, D, S], F32, kind="ExternalInput").ap()
    memT = nc.dram_tensor("memT", [nseq, D, NMEM], F32, kind="ExternalInput").ap()
    Wd = {nm: nc.dram_tensor(nm, shp, F32, kind="ExternalInput").ap() for nm, shp in WNAMES}
    cvec_d = nc.dram_tensor("cvec", [128, NCV], F32, kind="ExternalInput").ap()
    ctab_d = nc.dram_tensor("ctab", [128, NCT], F32, kind="ExternalInput").ap()
    outT = nc.dram_tensor("outT", [nseq, D, S], F32, kind="ExternalOutput").ap()

    hbuf = nc.alloc_sbuf_tensor("hbuf", [128, KC * S], F32).ap()
    ubuf = nc.alloc_sbuf_tensor("ubuf", [128, KC * S], BF16).ap()
    ybuf = nc.alloc_sbuf_tensor("ybuf", [128, KC * S], BF16).ap()
    wpool = nc.alloc_sbuf_tensor("wpool", [128, 4 * 4096], BF16).ap()
    cb = nc.alloc_sbuf_tensor("cb", [128, 896], BF16).ap()
    cf = nc.alloc_sbuf_tensor("cf", [128, 896], F32).ap()
    cv = nc.alloc_sbuf_tensor("cv", [128, NCV + 48], F32).ap()
    ARENA = 10560
    arena = nc.alloc_sbuf_tensor("arena", [128, ARENA], F32).ap()
    PS = [nc.alloc_psum_tensor("ps%d" % b, [128, TW], F32).ap() for b in range(8)]

    h3 = hbuf.rearrange("p (k t) -> p k t", k=KC)
    u3 = ubuf.rearrange("p (k t) -> p k t", k=KC)
    y3 = ybuf.rearrange("p (k t) -> p k t", k=KC)
    ident = cb[:, 0:128]
    ones = cb[:, 128:256]
    mfb = [cb[:, 256:384], cb[:, 384:512]]
    valid = cb[:, 512:768]
    valid0 = cb[:, 768:896]
    relabs = cf[:, 0:256]
    rel0 = cf[:, 256:384]
    scanmask = cf[:, 384:896]
    CV_LBV, CV_OML, CV_TMP = NCV, NCV + 16, NCV + 32

    def AF32(off, n):
        assert off + n <= ARENA
        return arena[:, off:off + n]

    def ABF(off, n):
        assert off + n // 2 <= ARENA
        return arena[:, off:off + n // 2].bitcast(BF16)

    nsq = ABF(8960, 2 * TW)
    nrs = AF32(9472, TW)

    def tsl_(tt):
        return slice(tt * TW, (tt + 1) * TW)

    MAXP = 5
    wstate = {"next": 0, "nslots": 4}

    def slot_view(s):
        if s < 4:
            return wpool[:, s * 4096:(s + 1) * 4096]
        return ybuf[:, (s - 4) * 4096:(s - 3) * 4096]

    def wkeys(s):
        return [("w", s, i) for i in range(MAXP)]

    def wload(parts):
        s = wstate["next"] % wstate["nslots"]
        wstate["next"] += 1
        sv = slot_view(s)
        for i, (dst_fn, src) in enumerate(parts):
            wk = [("w", s, i)]
            if i == 0:
                wk += [("w", s, j) for j in range(len(parts), MAXP)]
            P.op("pool", [I("dma_start", out=dst_fn(sv), in_=src)], w=wk, slot="w%d" % s)
        return s

    def v8(sv):
        return sv.rearrange("p (k c) -> p k c", k=8)

    def wsrc(Wl, c0, n):
        return Wl[:, c0:c0 + n].rearrange("(kc p) c -> p kc c", p=128)

    psr = {"a": [0, 1], "b": [2, 3], "c": [4, 5], "d": [6], "e": [7]}
    psn = {k: 0 for k in psr}
    psq = {"n": 0}

    def psum(cls):
        b = psr[cls][psn[cls] % len(psr[cls])]
        psn[cls] += 1
        return b

    P.op("sp", [I("dma_start", out=cv[:, 0:NCV], in_=cvec_d)], w=[("cv",)], slot="c0")
    P.op("sp", [I("dma_start", out=cf, in_=ctab_d[:, CT_REL:NCT])], w=[("cf",)], slot="c2")
    P.op("pool", [I("dma_start", out=cb, in_=ctab_d[:, 0:CT_REL])], w=[("cb",)], slot="c1")
    T0 = CV_TMP
    P.op("act", [I("activation", out=cv[:, T0:T0 + 16], in_=cv[:, CV_LB:CV_LB + 16], func=AF.Exp)], r=[("cv",)], w=[("cvt",)])
    P.op("dve", [I("tensor_tensor", out=cv[:, T0:T0 + 8], in0=cv[:, T0:T0 + 8], in1=cv[:, T0 + 8:T0 + 16], op=ALU.add)], r=[("cvt",)], w=[("cvt",)])
    P.op("dve", [I("reciprocal", out=cv[:, T0:T0 + 8], in_=cv[:, T0:T0 + 8])], r=[("cvt",)], w=[("cvt",)])
    P.op("dve", [I("memset", ap=cv[:, CV_LBV:CV_LBV + 8], constant=0.0)], w=[("lb0",)])
    P.op("dve", [I("tensor_tensor", out=cv[:, CV_LBV + 8:CV_LBV + 16], in0=cv[:, T0 + 8:T0 + 16], in1=cv[:, T0:T0 + 8], op=ALU.mult)], r=[("cvt",)], w=[("lb1",)])
    P.op("dve", [I("tensor_scalar", out=cv[:, CV_OML:CV_OML + 16], in0=cv[:, CV_LBV:CV_LBV + 16], scalar1=-1.0, scalar2=1.0, op0=ALU.mult, op1=ALU.add)],
         r=[("lb0",), ("lb1",)], w=[("lbd",)])
    CONSTR = [("cv",), ("cf",), ("cb",), ("lbd",), ("lb0",), ("lb1",)]

    hkey = lambda kc, tt: ("h", kc, tt)
    ukey = lambda kc, tt: ("u", kc, tt)
    ykey = lambda kc, tt: ("y", kc, tt)

    def norm(src3, srckey, gcol, dst3, dstkey, ntiles, tw, dn=D):
        for tt in range(ntiles):
            tsl = slice(tt * tw, (tt + 1) * tw)
            b = psum("d")
            for kc in range(KC):
                j = kc % 2
                sq = nsq[:, j * TW:j * TW + tw]
                P.op("act", [I("activation", out=sq, in_=src3[:, kc, tsl], func=AF.Square)], r=[srckey(kc, tt)], w=[("nsq", j)])
                P.op("pe", [I("matmul", out=PS[b][:, 0:tw], lhsT=ones, rhs=sq, start=(kc == 0), stop=(kc == KC - 1))], r=[("nsq", j), ("cb",)], w=[("ps", b)])
            P.op("act", [I("activation", out=nrs[:, 0:tw], in_=PS[b][:, 0:tw], func=AF.Ln, bias=EPS, scale=1.0 / dn)], r=[("ps", b)], w=[("nrs",)])
            P.op("act", [I("activation", out=nrs[:, 0:tw], in_=nrs[:, 0:tw], func=AF.Exp, scale=-0.5)], r=[("nrs",)], w=[("nrs",)])
            for kc in range(KC):
                P.op("dve", [I("scalar_tensor_tensor", out=dst3[:, kc, tsl], in0=src3[:, kc, tsl], scalar=cv[:, gcol + kc:gcol + kc + 1],
                               in1=nrs[:, 0:tw], op0=ALU.mult, op1=ALU.mult)],
                     r=[srckey(kc, tt), ("nrs",), ("cv",)], w=[dstkey(kc, tt)])

    def mm_group(b, lhs, rhs, rkeys, n=TW, col0=0):
        nk = len(lhs)
        P.op("pe", [I("matmul", out=PS[b][:, col0:col0 + n], lhsT=lhs[k], rhs=rhs[k], start=(k == 0), stop=(k == nk - 1)) for k in range(nk)],
             r=rkeys, w=[("ps", b)])

    def add_to_h(b, dc, tt, scale):
        tsl = tsl_(tt)
        P.op("dve", [I("scalar_tensor_tensor", out=h3[:, dc, tsl], in0=PS[b], scalar=scale, in1=h3[:, dc, tsl], op0=ALU.mult, op1=ALU.add)],
             r=[("ps", b), hkey(dc, tt)], w=[hkey(dc, tt)])

    def proj(src3, skey, Wl, ncols, epi, pcls="a"):
        for g0 in range(0, ncols, 512):
            gw = min(512, ncols - g0)
            s = wload([(lambda sv: v8(sv)[:, :, 0:gw], wsrc(Wl, g0, gw))])
            w8 = v8(slot_view(s))
            for tt in range(NT):
                tsl = tsl_(tt)
                for cc in range(gw // 128):
                    b = psum(pcls)
                    mm_group(b, [w8[:, k, cc * 128:(cc + 1) * 128] for k in range(KC)], [src3[:, k, tsl] for k in range(KC)],
                             wkeys(s) + [skey(k, tt) for k in range(KC)])
                    epi(g0 // 128 + cc, tt, b)

    def ffn(l, which):
        pre = "ffn%d_" % which
        gcol = (CV_FFN1 if which == 1 else CV_FFN2) + l * 8
        P.barrier()
        wstate["nslots"] = 8
        norm(h3, hkey, gcol, u3, ukey, NT, TW)
        Wg, Wu, Wdn = Wd[pre + "w_gate"][l], Wd[pre + "w_up"][l], Wd[pre + "w_down"][l]
        act = [ABF(0, 2048).rearrange("p (f t) -> p f t", f=4), ABF(1024, 2048).rearrange("p (f t) -> p f t", f=4)]
        sg = [AF32(2048, 512), AF32(2560, 512)]
        ai = 0
        si = 0
        for f0 in range(0, DFF, 512):
            gw = min(512, DFF - f0)
            nf = gw // 128
            sgt = wload([(lambda sv: v8(sv)[:, :, 0:gw], wsrc(Wg, f0, gw))])
            sup = wload([(lambda sv: v8(sv)[:, :, 0:gw], wsrc(Wu, f0, gw))])
            sdn = wload([(lambda sv: sv.rearrange("p (f d) -> p f d", f=4)[:, 0:nf, :], Wdn[f0:f0 + gw, :].rearrange("(f p) d -> p f d", p=128))])
            wg8, wu8 = v8(slot_view(sgt)), v8(slot_view(sup))
            wd4 = slot_view(sdn).rearrange("p (f d) -> p f d", f=4)
            for tt in range(NT):
                tsl = tsl_(tt)
                a = act[ai % 2]
                akey = ("act", ai % 2)
                ai += 1
                ukeys = [ukey(k, tt) for k in range(KC)]
                urhs = [u3[:, k, tsl] for k in range(KC)]
                for fc in range(nf):
                    bg = psum("a")
                    mm_group(bg, [wg8[:, k, fc * 128:(fc + 1) * 128] for k in range(KC)], urhs, wkeys(sgt) + ukeys)
                    bu = psum("b")
                    mm_group(bu, [wu8[:, k, fc * 128:(fc + 1) * 128] for k in range(KC)], urhs, wkeys(sup) + ukeys)
                    sgb = sg[si % 2]
                    sk = ("sg", si % 2)
                    si += 1
                    P.op("act", [I("activation", out=sgb, in_=PS[bg], func=AF.Silu)], r=[("ps", bg)], w=[sk])
                    P.op("dve", [I("tensor_tensor", out=a[:, fc, :], in0=sgb, in1=PS[bu], op=ALU.mult)], r=[sk, ("ps", bu)], w=[akey + (fc,)])
                for dc in range(KC):
                    bd = psum("c")
                    mm_group(bd, [wd4[:, k, dc * 128:(dc + 1) * 128] for k in range(nf)], [a[:, k, :] for k in range(nf)],
                             wkeys(sdn) + [akey + (fc,) for fc in range(nf)])
                    add_to_h(bd, dc, tt, 0.5)
        wstate["nslots"] = 4

    def run_streams(gens):
        gens = list(gens)
        while gens:
            for g in list(gens):
                try:
                    next(g)
                except StopIteration:
                    gens.remove(g)

    ext = ybuf[:, 8192:16384].bitcast(F32)

    def EF32(off, n):
        return ext[:, off:off + n]

    def EBF(off, n):
        return ext[:, off:off + n // 2].bitcast(BF16)

    def load_hgrn(l, hd):
        Win = Wd["w_in"][l]
        col = lambda j: j * 512 + hd * 128
        sA = wload([((lambda sv, j=j: v8(sv)[:, :, j * 128:(j + 1) * 128]), wsrc(Win, col(j), 128)) for j in range(4)])
        sB = wload([(lambda sv: v8(sv)[:, :, 0:128], wsrc(Win, col(4), 128))])
        return (sA, sB)

    def load_attn(l, j):
        Win = Wd["w_in"][l]
        cols = [2560 + j * 128, 3072 + j * 128, 3584 + j * 128]
        return wload([((lambda sv, i=i: v8(sv)[:, :, i * 128:(i + 1) * 128]), wsrc(Win, cols[i], 128)) for i in range(3)])

    def hgrn_head(l, hd, slots):
        sA, sB = slots
        wA, wB = v8(slot_view(sA)), v8(slot_view(sB))
        q_sb = ABF(0, 2048)
        i_tok = ABF(1024, 2048).rearrange("p (t v) -> p t v", t=16)
        o_acc = AF32(2048, 2048)
        Bs = [[AF32(4096 + i * 512, 512) for i in range(5)], [EF32(i * 512, 512) for i in range(5)]]
        QK = [[ABF(6656 + i * 256, 512) for i in range(4)], [EBF(2560 + i * 256, 512) for i in range(4)]]
        SCM = [ABF(7680, 512), EBF(3584, 512)]
        KLT = [ABF(7936, 512).rearrange("p (c k) -> p c k", c=4), EBF(3840, 512).rearrange("p (c k) -> p c k", c=4)]
        STATE = [AF32(8192, 128), AF32(8320, 128)]
        STB = [[ABF(8448, 128), ABF(8512, 128)], [ABF(8576, 128), ABF(8640, 128)]]
        AE = [AF32(8704, 8), AF32(8712, 8)]
        TL = [AF32(8720, 8), AF32(8728, 8)]
        lbc = CV_LBV + l * 8
        omc = CV_OML + l * 8
        allu = lambda tt: [ukey(k, tt) for k in range(KC)]
        for tt in range(NT):
            tsl = tsl_(tt)
            P.op("dve", [I("memset", ap=o_acc[:, tsl], constant=0.0)], w=[("oacc", tt)])
            b = psum("a")
            mm_group(b, [wA[:, k, 0:128] for k in range(KC)], [u3[:, k, tsl] for k in range(KC)], wkeys(sA) + allu(tt))
            P.op("act", [I("copy", out=q_sb[:, tsl], in_=PS[b])], r=[("ps", b)], w=[("q", tt)])
            b2 = psum("b")
            ins = []
            for t4 in range(4):
                for k in range(KC):
                    ins.append(I("matmul", out=PS[b2][:, t4 * 128:(t4 + 1) * 128], lhsT=u3[:, k, tt * TW + t4 * 128:tt * TW + (t4 + 1) * 128], rhs=wA[:, k, 128:256],
                                 start=(k == 0), stop=(k == KC - 1)))
            P.op("pe", ins, r=wkeys(sA) + allu(tt), w=[("ps", b2)])
            P.op("dve", [I("tensor_copy", out=i_tok[:, tt * 4:(tt + 1) * 4, :], in_=PS[b2].rearrange("p (t v) -> p t v", t=4))], r=[("ps", b2)], w=[("itok", tt)])

        def dir_stream(dr):
            B = Bs[dr]
            B3 = [x.rearrange("p (c t) -> p c t", t=64) for x in B]
            qa, qm, km, kl = QK[dr]
            scm, kl_tok, state, state_bf, aE, tl = SCM[dr], KLT[dr], STATE[dr], STB[dr], AE[dr], TL[dr]
            tl3 = tl.rearrange("p (c o) -> p c o", o=1)
            K = lambda nm, *a: (nm, dr) + a
            lb_ap = cv[:, lbc + dr * 4 + hd:lbc + dr * 4 + hd + 1]
            om_ap = cv[:, omc + dr * 4 + hd:omc + dr * 4 + hd + 1]
            P.op("dve", [I("memset", ap=state, constant=0.0)], w=[K("st")])
            P.op("dve", [I("memset", ap=state_bf[0], constant=0.0)], w=[K("stb", 0)])
            yield
            sbi = 0
            order = list(range(NT)) if dr == 0 else list(range(NT - 1, -1, -1))
            edge = 63 if dr == 0 else 0
            for tg in order:
                tsl = tsl_(tg)
                bz = psum("a")
                mm_group(bz, [wA[:, k, (2 + dr) * 128:(3 + dr) * 128] for k in range(KC)], [u3[:, k, tsl] for k in range(KC)], wkeys(sA) + allu(tg))
                P.op("act", [I("activation", out=B[0], in_=PS[bz], func=AF.Sigmoid)], r=[("ps", bz)], w=[K("B", 0)])
                yield
                P.op("dve", [I("tensor_scalar", out=B[0], in0=B[0], scalar1=om_ap, scalar2=lb_ap, op0=ALU.mult, op1=ALU.add)], r=[K("B", 0)] + CONSTR, w=[K("B", 0)])
                yield
                P.op("act", [I("activation", out=B[1], in_=B[0], func=AF.Identity, bias=1.0, scale=-1.0)], r=[K("B", 0)], w=[K("B", 1)])
                P.op("act", [I("activation", out=B[0], in_=B[0], func=AF.Ln)], r=[K("B", 0)], w=[K("B", 0)])
                yield
                P.op("dve", [I("tensor_tensor_scan", out=B[2], data0=scanmask, data1=B[0], initial=0.0, op0=ALU.mult, op1=ALU.add)], r=[K("B", 0), ("cf",)], w=[K("B", 2)])
                yield
                if dr == 1:
                    P.op("dve", [I("tensor_copy", out=tl3, in_=B3[2][:, :, 63:64])], r=[K("B", 2)], w=[K("tl")])
                    P.op("dve", [I("tensor_tensor", out=B[0], in0=B[0], in1=B[2], op=ALU.subtract)], r=[K("B", 0), K("B", 2)], w=[K("B", 0)])
                    yield
                    P.op("dve", [I("tensor_tensor", out=B3[2], in0=B3[0], in1=tl3.broadcast_to([128, 8, 64]), op=ALU.add)], r=[K("B", 0), K("tl")], w=[K("B", 2)])
                    yield
                P.op("dve", [I("tensor_copy", out=tl3, in_=B3[2][:, :, edge:edge + 1])], r=[K("B", 2)], w=[K("tl")])
                P.op("act", [I("activation", out=aE, in_=tl, func=AF.Exp)], r=[K("tl")], w=[K("aE")])
                P.op("dve", [I("tensor_tensor", out=B3[0], in0=B3[2], in1=B3[2][:, :, 32:33].broadcast_to([128, 8, 64]), op=ALU.subtract)], r=[K("B", 2)], w=[K("B", 0)])
                yield
                P.op("dve", [I("tensor_tensor", out=B3[3], in0=tl3.broadcast_to([128, 8, 64]), in1=B3[2], op=ALU.subtract)], r=[K("B", 2), K("tl")], w=[K("B", 3)])
                yield
                P.op("act", [I("activation", out=B[2], in_=B[2], func=AF.Exp)], r=[K("B", 2)], w=[K("B", 2)])
                P.op("act", [I("activation", out=B[4], in_=B[0], func=AF.Exp)], r=[K("B", 0)], w=[K("B", 4)])
                yield
                P.op("act", [I("activation", out=B[0], in_=B[0], func=AF.Exp, scale=-1.0)], r=[K("B", 0)], w=[K("B", 0)])
                P.op("act", [I("activation", out=B[3], in_=B[3], func=AF.Exp)], r=[K("B", 3)], w=[K("B", 3)])
                yield
                P.op("dve", [I("tensor_tensor", out=qm, in0=q_sb[:, tsl], in1=B[4], op=ALU.mult)], r=[("q", tg), K("B", 4)], w=[K("qm")])
                yield
                P.op("dve", [I("tensor_tensor", out=km, in0=B[1], in1=B[0], op=ALU.mult)], r=[K("B", 1), K("B", 0)], w=[K("km")])
                yield
                bs = psum("b")
                P.op("pe", [I("matmul", out=PS[bs][:, cp * 128:(cp + 1) * 128], lhsT=km[:, cp * 128:(cp + 1) * 128], rhs=qm[:, cp * 128:(cp + 1) * 128], start=True, stop=True)
                            for cp in range(4)], r=[K("km"), K("qm")], w=[("ps", bs)])
                P.op("dve", [I("tensor_tensor", out=kl, in0=B[1], in1=B[3], op=ALU.mult)], r=[K("B", 1), K("B", 3)], w=[K("kl")])
                yield
                P.op("dve", [I("tensor_tensor", out=scm.rearrange("p (c t) -> p c t", c=4), in0=PS[bs].rearrange("p (c t) -> p c t", c=4),
                               in1=mfb[dr].rearrange("p (o t) -> p o t", o=1).broadcast_to([128, 4, 128]), op=ALU.mult)],
                     r=[("ps", bs), ("cb",)], w=[K("scm")])
                yield
                P.op("dve", [I("tensor_tensor", out=qa, in0=q_sb[:, tsl], in1=B[2], op=ALU.mult)], r=[("q", tg), K("B", 2)], w=[K("qa")])
                yield
                bt = psum("b")
                ptb = PS[bt].bitcast(BF16)
                P.op("pe", [I("transpose", out=ptb[:, cp * 128:(cp + 1) * 128], in_=kl[:, cp * 128:(cp + 1) * 128], identity=ident) for cp in range(4)],
                     r=[K("kl"), ("cb",)], w=[("ps", bt)])
                P.op("act", [I("copy", out=kl_tok, in_=ptb[:, 0:512].rearrange("p (c k) -> p c k", c=4))], r=[("ps", bt)], w=[K("kltok")])
                yield
                bo = psum("c")
                corder = list(range(8)) if dr == 0 else list(range(7, -1, -1))
                for c in corder:
                    cp, hf = c // 2, c % 2
                    rows = slice(hf * 64, hf * 64 + 64)
                    tile = tg * 4 + cp
                    sb_cur = state_bf[sbi % 2]
                    sb_nxt = state_bf[(sbi + 1) % 2]
                    kcur, knxt = K("stb", sbi % 2), K("stb", (sbi + 1) % 2)
                    sbi += 1
                    pq = PS[6 + dr][:, 0:128]
                    P.op("pe", [I("matmul", out=pq, lhsT=kl_tok[rows, cp, :], rhs=i_tok[rows, tile, :], start=True, stop=True)],
                         r=[K("kltok"), ("itok", tg)], w=[("ps", 6 + dr)])
                    P.op("pe", [I("matmul", out=PS[bo][:, c * 64:(c + 1) * 64], lhsT=i_tok[rows, tile, :], rhs=scm[rows, cp * 128 + hf * 64:cp * 128 + hf * 64 + 64], start=True, stop=False),
                                I("matmul", out=PS[bo][:, c * 64:(c + 1) * 64], lhsT=sb_cur, rhs=qa[:, c * 64:(c + 1) * 64], start=False, stop=True)],
                         r=[("itok", tg), K("scm"), kcur, K("qa")], w=[("ps", bo)])
                    P.op("dve", [I("scalar_tensor_tensor", out=sb_nxt, in0=state, scalar=aE[:, c:c + 1], in1=pq, op0=ALU.mult, op1=ALU.add),
                                 I("scalar_tensor_tensor", out=state, in0=state, scalar=aE[:, c:c + 1], in1=pq, op0=ALU.mult, op1=ALU.add)],
                         r=[K("st"), K("aE"), ("ps", 6 + dr)], w=[K("st"), knxt])
                    yield
                P.op("dve", [I("tensor_tensor", out=o_acc[:, tsl], in0=o_acc[:, tsl], in1=PS[bo], op=ALU.add)], r=[("ps", bo), ("oacc", tg)], w=[("oacc", tg)])
                yield

        run_streams([dir_stream(0), dir_stream(1)])
        B = Bs[0]
        osq = ABF(4096, 512)
        K0 = lambda i: ("B", 0, i)
        gcol = CV_ON + l * 4 + hd
        for tg in range(NT):
            tsl = tsl_(tg)
            P.op("act", [I("activation", out=osq, in_=o_acc[:, tsl], func=AF.Square)], r=[("oacc", tg)], w=[K0(0)])
            bn = psum("d")
            P.op("pe", [I("matmul", out=PS[bn], lhsT=ones, rhs=osq, start=True, stop=True)], r=[K0(0), ("cb",)], w=[("ps", bn)])
            P.op("act", [I("activation", out=B[1], in_=PS[bn], func=AF.Ln, bias=EPS, scale=1.0 / 128)], r=[("ps", bn)], w=[K0(1)])
            P.op("act", [I("activation", out=B[1], in_=B[1], func=AF.Exp, scale=-0.5)], r=[K0(1)], w=[K0(1)])
            bg = psum("a")
            mm_group(bg, [wB[:, k, 0:128] for k in range(KC)], [u3[:, k, tsl] for k in range(KC)], wkeys(sB) + allu(tg))
            P.op("act", [I("activation", out=B[2], in_=PS[bg], func=AF.Silu)], r=[("ps", bg)], w=[K0(2)])
            P.op("dve", [I("scalar_tensor_tensor", out=B[3], in0=o_acc[:, tsl], scalar=cv[:, gcol:gcol + 1], in1=B[1], op0=ALU.mult, op1=ALU.mult)],
                 r=[("oacc", tg), K0(1)] + CONSTR, w=[K0(3)])
            P.op("dve", [I("tensor_tensor", out=y3[:, hd, tsl], in0=B[3], in1=B[2], op=ALU.mult)], r=[K0(3), K0(2)], w=[ykey(hd, tg)])

    def attn_pair(l, j, sA):
        wA = v8(slot_view(sA))
        qkv = [ABF(i * 1024, 2048) for i in range(3)]
        vtok = ABF(3072, 3072).rearrange("p (t v) -> p t v", t=16)
        acc = AF32(4608, 4096).rearrange("p (a t) -> p a t", a=2)
        NS = 2
        pTs = [ABF(8704 + i * 512, 1024) for i in range(NS)]
        Wp = [ABF(9728 + i * 256, 512) for i in range(2)]
        dtmp = AF32(9216, 512)
        sbanks = [(2, 3), (6, 7)]
        pvbanks = [(4, 0), (5, 1)]
        allu = lambda tt: [ukey(k, tt) for k in range(KC)]
        for i in range(3):
            for tt in range(NT):
                tsl = tsl_(tt)
                b = psum("a")
                mm_group(b, [wA[:, k, i * 128:(i + 1) * 128] for k in range(KC)], [u3[:, k, tsl] for k in range(KC)], wkeys(sA) + allu(tt))
                P.op("act", [I("copy", out=qkv[i][:, tsl], in_=PS[b])], r=[("ps", b)], w=[("qkv", i, tt)])
        qT, kT, vT = qkv
        qk_keys = [("qkv", i, tt) for i in range(2) for tt in range(NT)]
        v_keys = [("qkv", 2, tt) for tt in range(NT)]
        for tt in range(NT):
            P.op("dve", [I("memset", ap=acc[:, :, tsl_(tt)], constant=0.0)], w=[("acc", tt)])
        P.op("dve", [I("memset", ap=vtok[:, :, 64:128], constant=1.0)], w=[("vones",)])
        wi = 0
        for dil in (1, 4, 16):
            L = S // dil
            nkt = L // 128
            Wc = Wp[wi % 2]
            wkey = ("Wp", wi % 2)
            wi += 1
            for hh in range(2):
                c = (2.0 ** (-(2 * j + hh + 1))) * dil
                P.op("act", [I("activation", out=Wc[:, hh * 256:(hh + 1) * 256], in_=relabs, func=AF.Exp, scale=-c)], r=[("cf",)], w=[wkey + (hh,)])
                P.op("dve", [I("tensor_tensor", out=Wc[:, hh * 256:(hh + 1) * 256], in0=Wc[:, hh * 256:(hh + 1) * 256], in1=valid, op=ALU.mult)],
                     r=[wkey + (hh,), ("cb",)], w=[wkey + (hh,)])
            wkeys_ = [wkey + (0,), wkey + (1,)]
            for r_ in range(dil):
                for kt0 in range(0, nkt, 4):
                    nk = min(4, nkt - kt0)
                    bt = psum("e")
                    ptb = PS[bt].bitcast(BF16)
                    ins = []
                    for q in range(nk):
                        t0 = r_ + dil * 128 * (kt0 + q)
                        ins.append(I("transpose", out=ptb[:, q * 128:(q + 1) * 128], in_=vT[:, t0:t0 + 127 * dil + 1:dil], identity=ident))
                    P.op("pe", ins, r=v_keys + [("cb",)], w=[("ps", bt)])
                    ti = r_ * nkt + kt0
                    pt4 = ptb[:, 0:nk * 128].rearrange("p (t h e) -> p t h e", h=2, e=64)
                    P.op("act", [I("copy", out=vtok[:, ti:ti + nk, 0:64], in_=pt4[:, :, 0, :]),
                                 I("copy", out=vtok[:, ti:ti + nk, 128:192], in_=pt4[:, :, 1, :])], r=[("ps", bt)], w=[("vtok", ti)])
            vt_keys = [("vtok", r_ * nkt + kt0) for r_ in range(dil) for kt0 in range(0, nkt, 4)]
            units = []
            for r_ in range(dil):
                for kt in range(nkt):
                    x0 = 64 if kt == 0 else 0
                    x1 = 192 if kt == nkt - 1 else 256
                    base = 128 * kt - 64
                    a, bnd = base + x0, base + x1
                    pieces = []
                    if dil == 1:
                        for u in range((a + 64) // 512, (bnd - 1 + 64) // 512 + 1):
                            ma, mb = max(a, 512 * u - 64), min(bnd, 512 * u + 448)
                            pieces.append((("b", u), ma + 64 - 512 * u, ma - base, mb - base))
                    elif dil == 4:
                        pieces.append((("b", r_), a, x0, x1))
                    else:
                        pieces.append((("b", r_ // 4), (r_ % 4) * 128 + a, x0, x1))
                    units.append(dict(r=r_, kt=kt, x0=x0, x1=x1, base=base, pieces=pieces))
            inst_order = []
            contrib = {}
            for ui, un in enumerate(units):
                for pi_, pc in enumerate(un["pieces"]):
                    if pc[0] not in contrib:
                        contrib[pc[0]] = []
                        inst_order.append(pc[0])
                    contrib[pc[0]].append((ui, pi_))
            inst_idx = {k: n for n, k in enumerate(inst_order)}
            upairs = [units[i:i + 2] for i in range(0, len(units), 2)]

            def evac(inst, hh, dil=dil, L=L):
                b = pvbanks[hh][inst_idx[inst] % 2]
                if dil == 1:
                    u = inst[1]
                    m0, m1 = max(0, 512 * u - 64), min(L, 512 * u + 448)
                    c0 = m0 + 64 - 512 * u
                    dst = acc[:, hh, m0:m1]
                    src = PS[b][:, c0:c0 + (m1 - m0)]
                    akeys = [("acc", t) for t in range(m0 // TW, (m1 - 1) // TW + 1)]
                elif dil == 4:
                    dst = acc[:, hh, inst[1]:inst[1] + 4 * 511 + 1:4]
                    src = PS[b]
                    akeys = [("acc", t) for t in range(NT)]
                else:
                    g = inst[1]
                    dst = acc[:, hh, :].rearrange("p (m r) -> p r m", r=16)[:, 4 * g:4 * g + 4, :]
                    src = PS[b].rearrange("p (r m) -> p r m", r=4)
                    akeys = [("acc", t) for t in range(NT)]
                P.op("dve", [I("tensor_tensor", out=dst, in0=dst, in1=src, op=ALU.add)], r=[("ps", b)] + akeys, w=akeys)

            def blk_stream(si, hh, dil=dil, nkt=nkt, Wc=Wc, wkey=wkey, vt_keys=vt_keys, units=units, upairs=upairs, contrib=contrib, inst_idx=inst_idx):
                p_ = pTs[si][:, hh * 512:(hh + 1) * 512]
                pkey = ("pT", si, hh)
                rows = slice(hh * 64, hh * 64 + 64)
                sb = sbanks[si][hh]
                for up in upairs[si::NS]:
                    ncol = 256 * len(up)
                    ins = []
                    for slot, un in enumerate(up):
                        k0 = un["r"] + dil * 128 * un["kt"]
                        q0 = un["r"] + dil * (un["base"] + un["x0"])
                        nq = un["x1"] - un["x0"]
                        ins.append(I("matmul", out=PS[sb][:, slot * 256 + un["x0"]:slot * 256 + un["x1"]], lhsT=kT[rows, k0:k0 + 127 * dil + 1:dil],
                                     rhs=qT[rows, q0:q0 + (nq - 1) * dil + 1:dil], start=True, stop=True))
                    P.op("pe", ins, r=qk_keys, w=[("ps", sb)])
                    P.op("act", [I("activation", out=p_[:, 0:ncol], in_=PS[sb][:, 0:ncol], func=AF.Exp, scale=0.125)], r=[("ps", sb)], w=[pkey])
                    yield
                    nsl = len(up)
                    pv = p_.rearrange("p (s x) -> p s x", s=2)[:, 0:nsl, :]
                    wv = Wc[:, hh * 256:(hh + 1) * 256].rearrange("p (o x) -> p o x", o=1).broadcast_to([128, nsl, 256])
                    P.op("dve", [I("tensor_tensor", out=pv, in0=pv, in1=wv, op=ALU.mult)], r=[pkey, wkey + (hh,)], w=[pkey])
                    yield
                    ins = []
                    wb = set()
                    closing = []
                    for slot, un in enumerate(up):
                        ui = units.index(un)
                        tile = un["r"] * nkt + un["kt"]
                        for pi_, (inst, col0, xa, xb) in enumerate(un["pieces"]):
                            first = contrib[inst][0] == (ui, pi_)
                            last = contrib[inst][-1] == (ui, pi_)
                            b = pvbanks[hh][inst_idx[inst] % 2]
                            wb.add(b)
                            ins.append(I("matmul", out=PS[b][:, col0:col0 + (xb - xa)], lhsT=vtok[:, tile, hh * 64:hh * 64 + 128],
                                         rhs=p_[:, slot * 256 + xa:slot * 256 + xb], start=first, stop=last, skip_group_check=True))
                            if last:
                                closing.append(inst)
                    P.op("pe", ins, r=[pkey] + vt_keys + [("vones",)], w=[("ps", b) for b in sorted(wb)])
                    for inst in closing:
                        evac(inst, hh)
                    yield

            run_streams([blk_stream(si, hh) for si in range(NS) for hh in range(2)])
        for tt in range(NT):
            tsl = tsl_(tt)
            dk = [("pT", 1, 0), ("pT", 1, 1)]
            P.op("act", [I("activation", out=dtmp[0:64, :], in_=acc[64:128, 0, tsl], func=AF.Ln), I("activation", out=dtmp[64:128, :], in_=acc[0:64, 1, tsl], func=AF.Ln)],
                 r=[("acc", tt)], w=dk)
            P.op("act", [I("activation", out=dtmp, in_=dtmp, func=AF.Exp, scale=-1.0)], r=dk, w=dk)
            P.op("dve", [I("tensor_tensor", out=y3[0:64, 4 + j, tsl], in0=acc[0:64, 0, tsl], in1=dtmp[0:64, :], op=ALU.mult),
                         I("tensor_tensor", out=y3[64:128, 4 + j, tsl], in0=acc[64:128, 1, tsl], in1=dtmp[64:128, :], op=ALU.mult)],
                 r=[("acc", tt)] + dk, w=[ykey(4 + j, tt)])

    def mixer(l):
        P.barrier()
        norm(h3, hkey, CV_MIX + l * 8, u3, ukey, NT, TW)
        slots = load_hgrn(l, 0)
        for hd in range(4):
            nxt = load_hgrn(l, hd + 1) if hd < 3 else load_attn(l, 0)
            hgrn_head(l, hd, slots)
            slots = nxt
        P.barrier()
        for j in range(4):
            nxt = load_attn(l, j + 1) if j < 3 else None
            attn_pair(l, j, slots)
            slots = nxt
        P.barrier()
        proj(y3, ykey, Wd["w_out"][l], D, lambda dc, tt, b: add_to_h(b, dc, tt, 1.0))

    def xattn(l, s):
        P.barrier()
        memf = AF32(0, 2048).rearrange("p (k m) -> p k m", k=KC)
        memn = ABF(2048, 2048).rearrange("p (k m) -> p k m", k=KC)
        kTm = ABF(3072, 2048).rearrange("p (k m) -> p k m", k=KC)
        vm = ABF(4096, 2048).rearrange("p (t c) -> p t c", t=2)
        pT = [ABF(5120 + i * 256, 512) for i in range(4)]
        rden = AF32(6144, 512)
        P.op("sp", [I("dma_start", out=memf, in_=memT[s].rearrange("(k p) m -> p k m", p=128))], w=[("memf",)], slot="mem")
        norm(memf, lambda kc, tt: ("memf",), CV_MEM + l * 8, memn, lambda kc, tt: ("memn", kc), 1, NMEM)
        norm(h3, hkey, CV_XQ + l * 8, u3, ukey, NT, TW)
        Wkv = Wd["w_xkv"][l]
        mkeys = [("memn", k) for k in range(KC)]
        for g in range(4):
            sw = wload([(lambda sv: v8(sv), wsrc(Wkv, g * 512, 512))])
            w8 = v8(slot_view(sw))
            if g < 2:
                for mh in range(2):
                    b = psum("a")
                    ins = []
                    for cc in range(4):
                        for k in range(KC):
                            ins.append(I("matmul", out=PS[b][:, cc * 128:cc * 128 + 128], lhsT=w8[:, k, cc * 128:(cc + 1) * 128], rhs=memn[:, k, mh * 128:(mh + 1) * 128],
                                         start=(k == 0), stop=(k == KC - 1)))
                    P.op("pe", ins, r=wkeys(sw) + mkeys, w=[("ps", b)])
                    P.op("act", [I("copy", out=kTm[:, g * 4:(g + 1) * 4, mh * 128:(mh + 1) * 128], in_=PS[b].rearrange("p (c m) -> p c m", c=4))],
                         r=[("ps", b)], w=[("kTm", g, mh)])
            else:
                for mt in range(2):
                    b = psum("a")
                    mm_group(b, [memn[:, k, mt * 128:(mt + 1) * 128] for k in range(KC)], [w8[:, k, :] for k in range(KC)], wkeys(sw) + mkeys)
                    P.op("act", [I("copy", out=vm[:, mt, (g - 2) * 512:(g - 1) * 512], in_=PS[b])], r=[("ps", b)], w=[("vm", g, mt)])
        ktkeys = [("kTm", g, mh) for g in range(2) for mh in range(2)]
        vmkeys = [("vm", g, mt) for g in (2, 3) for mt in range(2)]

        def epi_q(c, tt, b):
            P.op("act", [I("copy", out=y3[:, c, tsl_(tt)], in_=PS[b])], r=[("ps", b)], w=[ykey(c, tt)])
        proj(u3, ukey, Wd["w_xq"][l], D, epi_q)
        pi = 0
        for tt in range(NT):
            tsl = tsl_(tt)
            for m in range(4):
                pk = []
                for mt in range(2):
                    b = psum("b")
                    mm_group(b, [kTm[:, 2 * m + k, mt * 128:(mt + 1) * 128] for k in range(2)], [y3[:, 2 * m + k, tsl] for k in range(2)],
                             ktkeys + [ykey(2 * m, tt), ykey(2 * m + 1, tt)])
                    p_ = pT[pi % 4]
                    pkey = ("xp", pi % 4)
                    pi += 1
                    P.op("act", [I("activation", out=p_, in_=PS[b], func=AF.Exp, scale=1.0 / 16.0)], r=[("ps", b)], w=[pkey])
                    pk.append((p_, pkey))
                bd = psum("d")
                mm_group(bd, [ones, ones], [pk[0][0], pk[1][0]], [pk[0][1], pk[1][1], ("cb",)])
                P.op("act", [I("activation", out=rden, in_=PS[bd], func=AF.Ln)], r=[("ps", bd)], w=[("rden",)])
                P.op("act", [I("activation", out=rden, in_=rden, func=AF.Exp, scale=-1.0)], r=[("rden",)], w=[("rden",)])
                for cc in range(2):
                    bo = psum("c")
                    mm_group(bo, [vm[:, k, (2 * m + cc) * 128:(2 * m + cc + 1) * 128] for k in range(2)], [pk[0][0], pk[1][0]],
                             [pk[0][1], pk[1][1]] + vmkeys)
                    P.op("dve", [I("tensor_tensor", out=u3[:, 2 * m + cc, tsl], in0=PS[bo], in1=rden, op=ALU.mult)],
                         r=[("ps", bo), ("rden",)], w=[ukey(2 * m + cc, tt)])
        proj(u3, ukey, Wd["w_xo"][l], D, lambda dc, tt, b: add_to_h(b, dc, tt, 1.0))

    stages = []
    for l in range(DEPTH):
        stages += [("ffn1", l), ("mixer", l), ("xattn", l), ("ffn2", l)]
    if stop_after is not None:
        stages = stages[:stop_after]
    for s in range(nseq):
        for kc in range(KC):
            P.op("sp", [I("dma_start", out=h3[:, kc, :], in_=xT[s, kc * 128:(kc + 1) * 128, :])], w=[hkey(kc, tt) for tt in range(NT)], slot="x%d" % kc)
        for st, l in stages:
            if st == "ffn1":
                ffn(l, 1)
            elif st == "mixer":
                mixer(l)
            elif st == "xattn":
                xattn(l, s)
            else:
                ffn(l, 2)
        if stop_after is None:
            norm(h3, hkey, CV_FIN, h3, hkey, NT, TW)
        for kc in range(KC):
            P.op("sp", [I("dma_start", out=outT[s, kc * 128:(kc + 1) * 128, :], in_=h3[:, kc, :])], r=[hkey(kc, tt) for tt in range(NT)], slot="o%d" % kc)
    P.barrier()
    P.op("sp", None)
    P.emit()
    return nc


def host_tables():
    ct = np.zeros((128, NCT), np.float32)
    ct[:, CT_ID:CT_ID + 128] = np.eye(128, dtype=np.float32)
    ct[:, CT_ONE:CT_ONE + 128] = 1.0
    s = np.arange(128)[:, None]
    t = np.arange(128)[None, :]
    same = (s // 64) == (t // 64)
    ct[:, CT_MF:CT_MF + 128] = (same & (s <= t)).astype(np.float32)
    ct[:, CT_MB:CT_MB + 128] = (same & (s >= t)).astype(np.float32)
    j = np.arange(128)[:, None]
    i = np.arange(128)[None, :]
    xq = np.arange(256)[None, :]
    rel = np.abs(j + 64 - xq).astype(np.float32)
    ct[:, CT_VAL:CT_VAL + 256] = (rel <= 64).astype(np.float32)
    ct[:, CT_REL:CT_REL + 256] = np.minimum(rel, 80.0)
    r0 = np.abs(j - i).astype(np.float32)
    ct[:, CT_V0:CT_V0 + 128] = (r0 <= 64).astype(np.float32)
    ct[:, CT_R0:CT_R0 + 128] = np.minimum(r0, 80.0)
    sm = np.ones((128, 512), np.float32)
    sm[:, ::64] = 0.0
    ct[:, CT_SCAN:CT_SCAN + 512] = sm
    return ct


def pack_cvec(inp):
    cvv = np.zeros((128, NCV), np.float32)

    def put(col, a, inner):
        a = np.asarray(a, np.float32)
        lead = int(np.prod(a.shape[:-1])) if a.ndim > 1 else 1
        a = a.reshape(lead, inner, 128)
        cvv[:, col:col + lead * inner] = a.transpose(2, 0, 1).reshape(128, lead * inner)
    put(CV_FFN1, inp["ln_ffn1"], 8)
    put(CV_MIX, inp["ln_mix"], 8)
    put(CV_XQ, inp["ln_xq"], 8)
    put(CV_MEM, inp["ln_mem"], 8)
    put(CV_FFN2, inp["ln_ffn2"], 8)
    put(CV_FIN, inp["ln_final"], 8)
    put(CV_ON, inp["hgrn_out_norm"], 4)
    put(CV_LB, inp["hgrn_lb_logits"], 4)
    return cvv


_CACHE = {}


def run(inputs, nseq, core_ids, stop_after=None):
    key = (nseq, stop_after)
    if key not in _CACHE:
        _CACHE[key] = build(nseq, stop_after)
    nc = _CACHE[key]
    x = np.asarray(inputs["x"], np.float32)
    mem = np.asarray(inputs["mem"], np.float32)
    ct = host_tables()
    cvv = pack_cvec(inputs)
    wts = {nm: np.ascontiguousarray(np.asarray(inputs[nm], np.float32)) for nm, _ in WNAMES}
    in_maps = []
    for ci in range(len(core_ids)):
        sl = slice(ci * nseq, (ci + 1) * nseq)
        m = {"xT": np.ascontiguousarray(x[sl].transpose(0, 2, 1)), "memT": np.ascontiguousarray(mem[sl].transpose(0, 2, 1)),
             "cvec": cvv, "ctab": ct}
        m.update(wts)
        in_maps.append(m)
    res = run_bass_kernel_spmd(nc, in_maps, core_ids=core_ids)
    out = np.concatenate([np.asarray(r["outT"]).transpose(0, 2, 1) for r in res.results], axis=0)
    return np.ascontiguousarray(out.astype(np.float32))


def kernel(**inputs):
    B = inputs["x"].shape[0]
    return run(inputs, B // NCORES, list(range(NCORES)))
```

```python
import numpy as np
import concourse.bass as bass
import concourse.mybir as mybir
from concourse.bass_utils import run_bass_kernel_spmd

F32 = mybir.dt.float32
BF16 = mybir.dt.bfloat16
ALU = mybir.AluOpType
AF = mybir.ActivationFunctionType

D = 1024
S = 2048
NMEM = 256
DFF = 2816
DEPTH = 2
KC = 8
NT = 4
TW = 512
EPS = 1e-6
NCORES = 8
WNAMES = [("ffn1_w_gate", [DEPTH, D, DFF]), ("ffn1_w_up", [DEPTH, D, DFF]), ("ffn1_w_down", [DEPTH, DFF, D]),
          ("w_in", [DEPTH, D, 4096]), ("w_out", [DEPTH, D, D]), ("w_xq", [DEPTH, D, D]),
          ("w_xkv", [DEPTH, D, 2 * D]), ("w_xo", [DEPTH, D, D]),
          ("ffn2_w_gate", [DEPTH, D, DFF]), ("ffn2_w_up", [DEPTH, D, DFF]), ("ffn2_w_down", [DEPTH, DFF, D])]
CV_FFN1, CV_MIX, CV_XQ, CV_MEM, CV_FFN2, CV_FIN, CV_ON, CV_LB, NCV = 0, 16, 32, 48, 64, 80, 88, 96, 112
CT_ID, CT_ONE, CT_MF, CT_MB, CT_VAL, CT_V0, CT_REL, CT_R0, CT_SCAN, NCT = 0, 128, 256, 384, 512, 768, 896, 1152, 1280, 1792
EPOCH = 50000
SAME_ENG_SYNC = True


class Node:
    __slots__ = ("id", "eng", "fn", "deps", "slot", "sig", "cnt")


class Prog:
    ENGS = ["pe", "act", "dve", "pool", "sp"]

    def __init__(self, nc):
        self.nc = nc
        self.nodes = []
        self.lastw = {}
        self.readers = {}
        self.fence = {}
        self.last_on = {}
        self.last_dma = {}

    def op(self, eng, fn, r=(), w=(), slot=None):
        n = Node()
        n.id = len(self.nodes)
        n.eng = eng
        n.fn = fn
        n.slot = slot
        n.sig = slot is not None
        n.cnt = 0
        deps = set()
        w = list(w) + [k for k in r if k[0] == "ps" and k not in w]
        for k in r:
            if k in self.lastw:
                deps.add(self.lastw[k])
        for k in w:
            if k in self.lastw:
                deps.add(self.lastw[k])
            deps.update(self.readers.get(k, ()))
        if eng in self.fence:
            deps.update(self.fence.pop(eng))
        n.deps = deps
        for k in r:
            self.readers.setdefault(k, []).append(n.id)
        for k in w:
            self.lastw[k] = n.id
            self.readers[k] = []
        self.nodes.append(n)
        if slot is None:
            self.last_on[eng] = n.id
        else:
            self.last_dma[slot] = n.id
        return n.id

    def barrier(self):
        ids = set(self.last_on.values()) | set(self.last_dma.values())
        for e in self.ENGS:
            self.fence[e] = set(ids) | self.fence.get(e, set())

    def emit(self):
        nc = self.nc
        nodes = self.nodes
        for n in nodes:
            for d in n.deps:
                nodes[d].sig = True
        cnt = {}
        for n in nodes:
            if not n.sig:
                continue
            key = ("d", n.slot) if n.slot is not None else ("e", n.eng)
            cnt[key] = cnt.get(key, 0) + 1
            n.cnt = cnt[key]
        sems = {}

        def sem_for(key, c):
            ep = (c - 1) // EPOCH
            k = (key, ep)
            if k not in sems:
                sems[k] = nc.alloc_semaphore("s_%s_%s_%d" % (key[0], key[1], ep))
            return sems[k], (c - 1) % EPOCH + 1

        bname = {"pe": "tensor", "act": "scalar", "dve": "vector", "pool": "gpsimd", "sp": "sync"}
        with nc.Block() as block:
            for eng in self.ENGS:
                mine = [n for n in nodes if n.eng == eng]

                def body(e, mine=mine, eng=eng):
                    seen = {}
                    for n in mine:
                        need = {}
                        for d in n.deps:
                            dn = nodes[d]
                            if dn.slot is not None:
                                key = ("d", dn.slot)
                            else:
                                if dn.eng == eng and (eng == "pe" or not SAME_ENG_SYNC):
                                    continue
                                key = ("e", dn.eng)
                            if dn.cnt > need.get(key, 0):
                                need[key] = dn.cnt
                        for key, c in need.items():
                            if seen.get(key, 0) >= c:
                                continue
                            seen[key] = c
                            sm, v = sem_for(key, c)
                            e.wait_ge(sm, v * (16 if key[0] == "d" else 1))
                        if n.fn is None:
                            continue
                        ins = None
                        for m_, kw_ in n.fn:
                            ins = getattr(e, m_)(**kw_)
                        if n.sig:
                            key = ("d", n.slot) if n.slot is not None else ("e", n.eng)
                            sm, v = sem_for(key, n.cnt)
                            ins.then_inc(sm, 16 if n.slot is not None else 1)

                getattr(block, bname[eng])(body)


def I(m, **kw):
    return (m, kw)


def build(nseq, stop_after=None):
    nc = bass.Bass("TRN2", target_bir_lowering=False)
    P = Prog(nc)
    xT = nc.dram_tensor("xT", [nseq, D, S], F32, kind="ExternalInput").ap()
    memT = nc.dram_tensor("memT", [nseq, D, NMEM], F32, kind="ExternalInput").ap()
    Wd = {nm: nc.dram_tensor(nm, shp, F32, kind="ExternalInput").ap() for nm, shp in WNAMES}
    cvec_d = nc.dram_tensor("cvec", [128, NCV], F32, kind="ExternalInput").ap()
    ctab_d = nc.dram_tensor("ctab", [128, NCT], F32, kind="ExternalInput").ap()
    outT = nc.dram_tensor("outT", [nseq, D, S], F32, kind="ExternalOutput").ap()

    hbuf = nc.alloc_sbuf_tensor("hbuf", [128, KC * S], F32).ap()
    ubuf = nc.alloc_sbuf_tensor("ubuf", [128, KC * S], BF16).ap()
    ybuf = nc.alloc_sbuf_tensor("ybuf", [128, KC * S], BF16).ap()
    wpool = nc.alloc_sbuf_tensor("wpool", [128, 4 * 4096], BF16).ap()
    cb = nc.alloc_sbuf_tensor("cb", [128, 896], BF16).ap()
    cf = nc.alloc_sbuf_tensor("cf", [128, 896], F32).ap()
    cv = nc.alloc_sbuf_tensor("cv", [128, NCV + 48], F32).ap()
    ARENA = 10560
    arena = nc.alloc_sbuf_tensor("arena", [128, ARENA], F32).ap()
    PS = [nc.alloc_psum_tensor("ps%d" % b, [128, TW], F32).ap() for b in range(8)]

    h3 = hbuf.rearrange("p (k t) -> p k t", k=KC)
    u3 = ubuf.rearrange("p (k t) -> p k t", k=KC)
    y3 = ybuf.rearrange("p (k t) -> p k t", k=KC)
    ident = cb[:, 0:128]
    ones = cb[:, 128:256]
    mfb = [cb[:, 256:384], cb[:, 384:512]]
    valid = cb[:, 512:768]
    valid0 = cb[:, 768:896]
    relabs = cf[:, 0:256]
    rel0 = cf[:, 256:384]
    scanmask = cf[:, 384:896]
    CV_LBV, CV_OML, CV_TMP = NCV, NCV + 16, NCV + 32

    def AF32(off, n):
        assert off + n <= ARENA
        return arena[:, off:off + n]

    def ABF(off, n):
        assert off + n // 2 <= ARENA
        return arena[:, off:off + n // 2].bitcast(BF16)

    nsq = ABF(8960, 2 * TW)
    nrs = AF32(9472, TW)

    def tsl_(tt):
        return slice(tt * TW, (tt + 1) * TW)

    MAXP = 5
    wstate = {"next": 0, "nslots": 4}

    def slot_view(s):
        if s < 4:
            return wpool[:, s * 4096:(s + 1) * 4096]
        return ybuf[:, (s - 4) * 4096:(s - 3) * 4096]

    def wkeys(s):
        return [("w", s, i) for i in range(MAXP)]

    def wload(parts):
        s = wstate["next"] % wstate["nslots"]
        wstate["next"] += 1
        sv = slot_view(s)
        for i, (dst_fn, src) in enumerate(parts):
            wk = [("w", s, i)]
            if i == 0:
                wk += [("w", s, j) for j in range(len(parts), MAXP)]
            P.op("pool", [I("dma_start", out=dst_fn(sv), in_=src)], w=wk, slot="w%d" % s)
        return s

    def v8(sv):
        return sv.rearrange("p (k c) -> p k c", k=8)

    def wsrc(Wl, c0, n):
        return Wl[:, c0:c0 + n].rearrange("(kc p) c -> p kc c", p=128)

    psr = {"a": [0, 1], "b": [2, 3], "c": [4, 5], "d": [6], "e": [7]}
    psn = {k: 0 for k in psr}
    psq = {"n": 0}

    def psum(cls):
        b = psr[cls][psn[cls] % len(psr[cls])]
        psn[cls] += 1
        return b

    P.op("sp", [I("dma_start", out=cv[:, 0:NCV], in_=cvec_d)], w=[("cv",)], slot="c0")
    P.op("sp", [I("dma_start", out=cf, in_=ctab_d[:, CT_REL:NCT])], w=[("cf",)], slot="c2")
    P.op("pool", [I("dma_start", out=cb, in_=ctab_d[:, 0:CT_REL])], w=[("cb",)], slot="c1")
    T0 = CV_TMP
    P.op("act", [I("activation", out=cv[:, T0:T0 + 16], in_=cv[:, CV_LB:CV_LB + 16], func=AF.Exp)], r=[("cv",)], w=[("cvt",)])
    P.op("dve", [I("tensor_tensor", out=cv[:, T0:T0 + 8], in0=cv[:, T0:T0 + 8], in1=cv[:, T0 + 8:T0 + 16], op=ALU.add)], r=[("cvt",)], w=[("cvt",)])
    P.op("dve", [I("reciprocal", out=cv[:, T0:T0 + 8], in_=cv[:, T0:T0 + 8])], r=[("cvt",)], w=[("cvt",)])
    P.op("dve", [I("memset", ap=cv[:, CV_LBV:CV_LBV + 8], constant=0.0)], w=[("lb0",)])
    P.op("dve", [I("tensor_tensor", out=cv[:, CV_LBV + 8:CV_LBV + 16], in0=cv[:, T0 + 8:T0 + 16], in1=cv[:, T0:T0 + 8], op=ALU.mult)], r=[("cvt",)], w=[("lb1",)])
    P.op("dve", [I("tensor_scalar", out=cv[:, CV_OML:CV_OML + 16], in0=cv[:, CV_LBV:CV_LBV + 16], scalar1=-1.0, scalar2=1.0, op0=ALU.mult, op1=ALU.add)],
         r=[("lb0",), ("lb1",)], w=[("lbd",)])
    CONSTR = [("cv",), ("cf",), ("cb",), ("lbd",), ("lb0",), ("lb1",)]

    hkey = lambda kc, tt: ("h", kc, tt)
    ukey = lambda kc, tt: ("u", kc, tt)
    ykey = lambda kc, tt: ("y", kc, tt)

    def norm(src3, srckey, gcol, dst3, dstkey, ntiles, tw, dn=D):
        for tt in range(ntiles):
            tsl = slice(tt * tw, (tt + 1) * tw)
            b = psum("d")
            for kc in range(KC):
                j = kc % 2
                sq = nsq[:, j * TW:j * TW + tw]
                P.op("act", [I("activation", out=sq, in_=src3[:, kc, tsl], func=AF.Square)], r=[srckey(kc, tt)], w=[("nsq", j)])
                P.op("pe", [I("matmul", out=PS[b][:, 0:tw], lhsT=ones, rhs=sq, start=(kc == 0), stop=(kc == KC - 1))], r=[("nsq", j), ("cb",)], w=[("ps", b)])
            P.op("act", [I("activation", out=nrs[:, 0:tw], in_=PS[b][:, 0:tw], func=AF.Ln, bias=EPS, scale=1.0 / dn)], r=[("ps", b)], w=[("nrs",)])
            P.op("act", [I("activation", out=nrs[:, 0:tw], in_=nrs[:, 0:tw], func=AF.Exp, scale=-0.5)], r=[("nrs",)], w=[("nrs",)])
            for kc in range(KC):
                P.op("dve", [I("scalar_tensor_tensor", out=dst3[:, kc, tsl], in0=src3[:, kc, tsl], scalar=cv[:, gcol + kc:gcol + kc + 1],
                               in1=nrs[:, 0:tw], op0=ALU.mult, op1=ALU.mult)],
                     r=[srckey(kc, tt), ("nrs",), ("cv",)], w=[dstkey(kc, tt)])

    def mm_group(b, lhs, rhs, rkeys, n=TW, col0=0):
        nk = len(lhs)
        P.op("pe", [I("matmul", out=PS[b][:, col0:col0 + n], lhsT=lhs[k], rhs=rhs[k], start=(k == 0), stop=(k == nk - 1)) for k in range(nk)],
             r=rkeys, w=[("ps", b)])

    def add_to_h(b, dc, tt, scale):
        tsl = tsl_(tt)
        P.op("dve", [I("scalar_tensor_tensor", out=h3[:, dc, tsl], in0=PS[b], scalar=scale, in1=h3[:, dc, tsl], op0=ALU.mult, op1=ALU.add)],
             r=[("ps", b), hkey(dc, tt)], w=[hkey(dc, tt)])

    def proj(src3, skey, Wl, ncols, epi, pcls="a"):
        for g0 in range(0, ncols, 512):
            gw = min(512, ncols - g0)
            s = wload([(lambda sv: v8(sv)[:, :, 0:gw], wsrc(Wl, g0, gw))])
            w8 = v8(slot_view(s))
            for tt in range(NT):
                tsl = tsl_(tt)
                for cc in range(gw // 128):
                    b = psum(pcls)
                    mm_group(b, [w8[:, k, cc * 128:(cc + 1) * 128] for k in range(KC)], [src3[:, k, tsl] for k in range(KC)],
                             wkeys(s) + [skey(k, tt) for k in range(KC)])
                    epi(g0 // 128 + cc, tt, b)

    def ffn(l, which):
        pre = "ffn%d_" % which
        gcol = (CV_FFN1 if which == 1 else CV_FFN2) + l * 8
        P.barrier()
        wstate["nslots"] = 8
        norm(h3, hkey, gcol, u3, ukey, NT, TW)
        Wg, Wu, Wdn = Wd[pre + "w_gate"][l], Wd[pre + "w_up"][l], Wd[pre + "w_down"][l]
        act = [ABF(0, 2048).rearrange("p (f t) -> p f t", f=4), ABF(1024, 2048).rearrange("p (f t) -> p f t", f=4)]
        sg = [AF32(2048, 512), AF32(2560, 512)]
        ai = 0
        si = 0
        for f0 in range(0, DFF, 512):
            gw = min(512, DFF - f0)
            nf = gw // 128
            sgt = wload([(lambda sv: v8(sv)[:, :, 0:gw], wsrc(Wg, f0, gw))])
            sup = wload([(lambda sv: v8(sv)[:, :, 0:gw], wsrc(Wu, f0, gw))])
            sdn = wload([(lambda sv: sv.rearrange("p (f d) -> p f d", f=4)[:, 0:nf, :], Wdn[f0:f0 + gw, :].rearrange("(f p) d -> p f d", p=128))])
            wg8, wu8 = v8(slot_view(sgt)), v8(slot_view(sup))
            wd4 = slot_view(sdn).rearrange("p (f d) -> p f d", f=4)
            for tt in range(NT):
                tsl = tsl_(tt)
                a = act[ai % 2]
                akey = ("act", ai % 2)
                ai += 1
                ukeys = [ukey(k, tt) for k in range(KC)]
                urhs = [u3[:, k, tsl] for k in range(KC)]
                for fc in range(nf):
                    bg = psum("a")
                    mm_group(bg, [wg8[:, k, fc * 128:(fc + 1) * 128] for k in range(KC)], urhs, wkeys(sgt) + ukeys)
                    bu = psum("b")
                    mm_group(bu, [wu8[:, k, fc * 128:(fc + 1) * 128] for k in range(KC)], urhs, wkeys(sup) + ukeys)
                    sgb = sg[si % 2]
                    sk = ("sg", si % 2)
                    si += 1
                    P.op("act", [I("activation", out=sgb, in_=PS[bg], func=AF.Silu)], r=[("ps", bg)], w=[sk])
                    P.op("dve", [I("tensor_tensor", out=a[:, fc, :], in0=sgb, in1=PS[bu], op=ALU.mult)], r=[sk, ("ps", bu)], w=[akey + (fc,)])
                for dc in range(KC):
                    bd = psum("c")
                    mm_group(bd, [wd4[:, k, dc * 128:(dc + 1) * 128] for k in range(nf)], [a[:, k, :] for k in range(nf)],
                             wkeys(sdn) + [akey + (fc,) for fc in range(nf)])
                    add_to_h(bd, dc, tt, 0.5)
        wstate["nslots"] = 4

    def run_streams(gens):
        gens = list(gens)
        while gens:
            for g in list(gens):
                try:
                    next(g)
                except StopIteration:
                    gens.remove(g)

    ext = ybuf[:, 8192:16384].bitcast(F32)

    def EF32(off, n):
        return ext[:, off:off + n]

    def EBF(off, n):
        return ext[:, off:off + n // 2].bitcast(BF16)

    def load_hgrn(l, hd):
        Win = Wd["w_in"][l]
        col = lambda j: j * 512 + hd * 128
        sA = wload([((lambda sv, j=j: v8(sv)[:, :, j * 128:(j + 1) * 128]), wsrc(Win, col(j), 128)) for j in range(4)])
        sB = wload([(lambda sv: v8(sv)[:, :, 0:128], wsrc(Win, col(4), 128))])
        return (sA, sB)

    def load_attn(l, j):
        Win = Wd["w_in"][l]
        cols = [2560 + j * 128, 3072 + j * 128, 3584 + j * 128]
        return wload([((lambda sv, i=i: v8(sv)[:, :, i * 128:(i + 1) * 128]), wsrc(Win, cols[i], 128)) for i in range(3)])

    def hgrn_head(l, hd, slots):
        sA, sB = slots
        wA, wB = v8(slot_view(sA)), v8(slot_view(sB))
        q_sb = ABF(0, 2048)
        i_tok = ABF(1024, 2048).rearrange("p (t v) -> p t v", t=16)
        o_acc = AF32(2048, 2048)
        Bs = [[AF32(4096 + i * 512, 512) for i in range(5)], [EF32(i * 512, 512) for i in range(5)]]
        QK = [[ABF(6656 + i * 256, 512) for i in range(4)], [EBF(2560 + i * 256, 512) for i in range(4)]]
        SCM = [ABF(7680, 512), EBF(3584, 512)]
        KLT = [ABF(7936, 512).rearrange("p (c k) -> p c k", c=4), EBF(3840, 512).rearrange("p (c k) -> p c k", c=4)]
        STATE = [[AF32(8192, 128), AF32(9984, 128)], [AF32(8320, 128), AF32(10112, 128)]]
        STB = [[ABF(8448, 128), ABF(8512, 128)], [ABF(8576, 128), ABF(8640, 128)]]
        AE = [AF32(8704, 8), AF32(8712, 8)]
        TL = [AF32(8720, 8), AF32(8728, 8)]
        lbc = CV_LBV + l * 8
        omc = CV_OML + l * 8
        allu = lambda tt: [ukey(k, tt) for k in range(KC)]
        for tt in range(NT):
            tsl = tsl_(tt)
            P.op("dve", [I("memset", ap=o_acc[:, tsl], constant=0.0)], w=[("oacc", tt)])
            b = psum("a")
            mm_group(b, [wA[:, k, 0:128] for k in range(KC)], [u3[:, k, tsl] for k in range(KC)], wkeys(sA) + allu(tt))
            P.op("act", [I("copy", out=q_sb[:, tsl], in_=PS[b])], r=[("ps", b)], w=[("q", tt)])
            b2 = psum("b")
            ins = []
            for t4 in range(4):
                for k in range(KC):
                    ins.append(I("matmul", out=PS[b2][:, t4 * 128:(t4 + 1) * 128], lhsT=u3[:, k, tt * TW + t4 * 128:tt * TW + (t4 + 1) * 128], rhs=wA[:, k, 128:256],
                                 start=(k == 0), stop=(k == KC - 1)))
            P.op("pe", ins, r=wkeys(sA) + allu(tt), w=[("ps", b2)])
            P.op("dve", [I("tensor_copy", out=i_tok[:, tt * 4:(tt + 1) * 4, :], in_=PS[b2].rearrange("p (t v) -> p t v", t=4))], r=[("ps", b2)], w=[("itok", tt)])

        def dir_stream(dr):
            B = Bs[dr]
            B3 = [x.rearrange("p (c t) -> p c t", t=64) for x in B]
            qa, qm, km, kl = QK[dr]
            scm, kl_tok, state2, state_bf, aE, tl = SCM[dr], KLT[dr], STATE[dr], STB[dr], AE[dr], TL[dr]
            tl3 = tl.rearrange("p (c o) -> p c o", o=1)
            K = lambda nm, *a: (nm, dr) + a
            lb_ap = cv[:, lbc + dr * 4 + hd:lbc + dr * 4 + hd + 1]
            om_ap = cv[:, omc + dr * 4 + hd:omc + dr * 4 + hd + 1]
            P.op("dve", [I("memset", ap=state2[0], constant=0.0)], w=[K("st", 0)])
            P.op("dve", [I("memset", ap=state_bf[0], constant=0.0)], w=[K("stb", 0)])
            yield
            sbi = 0
            order = list(range(NT)) if dr == 0 else list(range(NT - 1, -1, -1))
            edge = 63 if dr == 0 else 0
            for tg in order:
                tsl = tsl_(tg)
                bz = psum("a")
                mm_group(bz, [wA[:, k, (2 + dr) * 128:(3 + dr) * 128] for k in range(KC)], [u3[:, k, tsl] for k in range(KC)], wkeys(sA) + allu(tg))
                P.op("act", [I("activation", out=B[0], in_=PS[bz], func=AF.Sigmoid)], r=[("ps", bz)], w=[K("B", 0)])
                yield
                P.op("dve", [I("tensor_scalar", out=B[0], in0=B[0], scalar1=om_ap, scalar2=lb_ap, op0=ALU.mult, op1=ALU.add)], r=[K("B", 0)] + CONSTR, w=[K("B", 0)])
                yield
                P.op("act", [I("activation", out=B[1], in_=B[0], func=AF.Identity, bias=1.0, scale=-1.0)], r=[K("B", 0)], w=[K("B", 1)])
                P.op("act", [I("activation", out=B[0], in_=B[0], func=AF.Ln)], r=[K("B", 0)], w=[K("B", 0)])
                yield
                P.op("dve", [I("tensor_tensor_scan", out=B[2], data0=scanmask, data1=B[0], initial=0.0, op0=ALU.mult, op1=ALU.add)], r=[K("B", 0), ("cf",)], w=[K("B", 2)])
                yield
                if dr == 1:
                    P.op("dve", [I("tensor_copy", out=tl3, in_=B3[2][:, :, 63:64])], r=[K("B", 2)], w=[K("tl")])
                    P.op("dve", [I("tensor_tensor", out=B[0], in0=B[0], in1=B[2], op=ALU.subtract)], r=[K("B", 0), K("B", 2)], w=[K("B", 0)])
                    yield
                    P.op("dve", [I("tensor_tensor", out=B3[2], in0=B3[0], in1=tl3.broadcast_to([128, 8, 64]), op=ALU.add)], r=[K("B", 0), K("tl")], w=[K("B", 2)])
                    yield
                P.op("dve", [I("tensor_copy", out=tl3, in_=B3[2][:, :, edge:edge + 1])], r=[K("B", 2)], w=[K("tl")])
                P.op("act", [I("activation", out=aE, in_=tl, func=AF.Exp)], r=[K("tl")], w=[K("aE")])
                P.op("dve", [I("tensor_tensor", out=B3[0], in0=B3[2], in1=B3[2][:, :, 32:33].broadcast_to([128, 8, 64]), op=ALU.subtract)], r=[K("B", 2)], w=[K("B", 0)])
                yield
                P.op("dve", [I("tensor_tensor", out=B3[3], in0=tl3.broadcast_to([128, 8, 64]), in1=B3[2], op=ALU.subtract)], r=[K("B", 2), K("tl")], w=[K("B", 3)])
                yield
                P.op("act", [I("activation", out=B[2], in_=B[2], func=AF.Exp)], r=[K("B", 2)], w=[K("B", 2)])
                P.op("act", [I("activation", out=B[4], in_=B[0], func=AF.Exp)], r=[K("B", 0)], w=[K("B", 4)])
                yield
                P.op("act", [I("activation", out=B[0], in_=B[0], func=AF.Exp, scale=-1.0)], r=[K("B", 0)], w=[K("B", 0)])
                P.op("act", [I("activation", out=B[3], in_=B[3], func=AF.Exp)], r=[K("B", 3)], w=[K("B", 3)])
                yield
                P.op("dve", [I("tensor_tensor", out=qm, in0=q_sb[:, tsl], in1=B[4], op=ALU.mult)], r=[("q", tg), K("B", 4)], w=[K("qm")])
                yield
                P.op("dve", [I("tensor_tensor", out=km, in0=B[1], in1=B[0], op=ALU.mult)], r=[K("B", 1), K("B", 0)], w=[K("km")])
                yield
                bs = psum("b")
                P.op("pe", [I("matmul", out=PS[bs][:, cp * 128:(cp + 1) * 128], lhsT=km[:, cp * 128:(cp + 1) * 128], rhs=qm[:, cp * 128:(cp + 1) * 128], start=True, stop=True)
                            for cp in range(4)], r=[K("km"), K("qm")], w=[("ps", bs)])
                P.op("dve", [I("tensor_tensor", out=kl, in0=B[1], in1=B[3], op=ALU.mult)], r=[K("B", 1), K("B", 3)], w=[K("kl")])
                yield
                P.op("dve", [I("tensor_tensor", out=scm.rearrange("p (c t) -> p c t", c=4), in0=PS[bs].rearrange("p (c t) -> p c t", c=4),
                               in1=mfb[dr].rearrange("p (o t) -> p o t", o=1).broadcast_to([128, 4, 128]), op=ALU.mult)],
                     r=[("ps", bs), ("cb",)], w=[K("scm")])
                yield
                P.op("dve", [I("tensor_tensor", out=qa, in0=q_sb[:, tsl], in1=B[2], op=ALU.mult)], r=[("q", tg), K("B", 2)], w=[K("qa")])
                yield
                bt = psum("b")
                ptb = PS[bt].bitcast(BF16)
                P.op("pe", [I("transpose", out=ptb[:, cp * 128:(cp + 1) * 128], in_=kl[:, cp * 128:(cp + 1) * 128], identity=ident) for cp in range(4)],
                     r=[K("kl"), ("cb",)], w=[("ps", bt)])
                P.op("act", [I("copy", out=kl_tok, in_=ptb[:, 0:512].rearrange("p (c k) -> p c k", c=4))], r=[("ps", bt)], w=[K("kltok")])
                yield
                bo = psum("c")
                corder = list(range(8)) if dr == 0 else list(range(7, -1, -1))
                for c in corder:
                    cp, hf = c // 2, c % 2
                    rows = slice(hf * 64, hf * 64 + 64)
                    tile = tg * 4 + cp
                    sb_cur = state_bf[sbi % 2]
                    sb_nxt = state_bf[(sbi + 1) % 2]
                    kcur, knxt = K("stb", sbi % 2), K("stb", (sbi + 1) % 2)
                    sbi += 1
                    pq = PS[6 + dr][:, 0:128]
                    P.op("pe", [I("matmul", out=pq, lhsT=kl_tok[rows, cp, :], rhs=i_tok[rows, tile, :], start=True, stop=True)],
                         r=[K("kltok"), ("itok", tg)], w=[("ps", 6 + dr)])
                    P.op("pe", [I("matmul", out=PS[bo][:, c * 64:(c + 1) * 64], lhsT=i_tok[rows, tile, :], rhs=scm[rows, cp * 128 + hf * 64:cp * 128 + hf * 64 + 64], start=True, stop=False),
                                I("matmul", out=PS[bo][:, c * 64:(c + 1) * 64], lhsT=sb_cur, rhs=qa[:, c * 64:(c + 1) * 64], start=False, stop=True)],
                         r=[("itok", tg), K("scm"), kcur, K("qa")], w=[("ps", bo)])
                    st_cur, st_nxt = state2[(sbi - 1) % 2], state2[sbi % 2]
                    P.op("dve", [I("scalar_tensor_tensor", out=sb_nxt, in0=st_cur, scalar=aE[:, c:c + 1], in1=pq, op0=ALU.mult, op1=ALU.add),
                                 I("scalar_tensor_tensor", out=st_nxt, in0=st_cur, scalar=aE[:, c:c + 1], in1=pq, op0=ALU.mult, op1=ALU.add)],
                         r=[K("st", (sbi - 1) % 2), K("aE"), ("ps", 6 + dr)], w=[K("st", sbi % 2), knxt])
                    yield
                P.op("dve", [I("tensor_tensor", out=o_acc[:, tsl], in0=o_acc[:, tsl], in1=PS[bo], op=ALU.add)], r=[("ps", bo), ("oacc", tg)], w=[("oacc", tg)])
                yield

        run_streams([dir_stream(0), dir_stream(1)])
        B = Bs[0]
        osq = ABF(4096, 512)
        K0 = lambda i: ("B", 0, i)
        gcol = CV_ON + l * 4 + hd
        for tg in range(NT):
            tsl = tsl_(tg)
            P.op("act", [I("activation", out=osq, in_=o_acc[:, tsl], func=AF.Square)], r=[("oacc", tg)], w=[K0(0)])
            bn = psum("d")
            P.op("pe", [I("matmul", out=PS[bn], lhsT=ones, rhs=osq, start=True, stop=True)], r=[K0(0), ("cb",)], w=[("ps", bn)])
            P.op("act", [I("activation", out=B[1], in_=PS[bn], func=AF.Ln, bias=EPS, scale=1.0 / 128)], r=[("ps", bn)], w=[K0(1)])
            P.op("act", [I("activation", out=B[1], in_=B[1], func=AF.Exp, scale=-0.5)], r=[K0(1)], w=[K0(1)])
            bg = psum("a")
            mm_group(bg, [wB[:, k, 0:128] for k in range(KC)], [u3[:, k, tsl] for k in range(KC)], wkeys(sB) + allu(tg))
            P.op("act", [I("activation", out=B[2], in_=PS[bg], func=AF.Silu)], r=[("ps", bg)], w=[K0(2)])
            P.op("dve", [I("scalar_tensor_tensor", out=B[3], in0=o_acc[:, tsl], scalar=cv[:, gcol:gcol + 1], in1=B[1], op0=ALU.mult, op1=ALU.mult)],
                 r=[("oacc", tg), K0(1)] + CONSTR, w=[K0(3)])
            P.op("dve", [I("tensor_tensor", out=y3[:, hd, tsl], in0=B[3], in1=B[2], op=ALU.mult)], r=[K0(3), K0(2)], w=[ykey(hd, tg)])

    def attn_pair(l, j, sA):
        wA = v8(slot_view(sA))
        qkv = [ABF(i * 1024, 2048) for i in range(3)]
        vtok = ABF(3072, 3072).rearrange("p (t v) -> p t v", t=16)
        acc = AF32(4608, 4096).rearrange("p (a t) -> p a t", a=2)
        NS = 2
        pTs = [ABF(8704 + i * 512, 1024) for i in range(NS)]
        Wp = [ABF(9728 + i * 256, 512) for i in range(2)]
        dtmp = AF32(9216, 512)
        sbanks = [(2, 3), (6, 7)]
        pvbanks = [(4, 0), (5, 1)]
        allu = lambda tt: [ukey(k, tt) for k in range(KC)]
        for i in range(3):
            for tt in range(NT):
                tsl = tsl_(tt)
                b = psum("a")
                mm_group(b, [wA[:, k, i * 128:(i + 1) * 128] for k in range(KC)], [u3[:, k, tsl] for k in range(KC)], wkeys(sA) + allu(tt))
                P.op("act", [I("copy", out=qkv[i][:, tsl], in_=PS[b])], r=[("ps", b)], w=[("qkv", i, tt)])
        qT, kT, vT = qkv
        qk_keys = [("qkv", i, tt) for i in range(2) for tt in range(NT)]
        v_keys = [("qkv", 2, tt) for tt in range(NT)]
        for tt in range(NT):
            P.op("dve", [I("memset", ap=acc[:, :, tsl_(tt)], constant=0.0)], w=[("acc", tt)])
        P.op("dve", [I("memset", ap=vtok[:, :, 64:128], constant=1.0)], w=[("vones",)])
        wi = 0
        for dil in (1, 4, 16):
            L = S // dil
            nkt = L // 128
            Wc = Wp[wi % 2]
            wkey = ("Wp", wi % 2)
            wi += 1
            for hh in range(2):
                c = (2.0 ** (-(2 * j + hh + 1))) * dil
                P.op("act", [I("activation", out=Wc[:, hh * 256:(hh + 1) * 256], in_=relabs, func=AF.Exp, scale=-c)], r=[("cf",)], w=[wkey + (hh,)])
                P.op("dve", [I("tensor_tensor", out=Wc[:, hh * 256:(hh + 1) * 256], in0=Wc[:, hh * 256:(hh + 1) * 256], in1=valid, op=ALU.mult)],
                     r=[wkey + (hh,), ("cb",)], w=[wkey + (hh,)])
            wkeys_ = [wkey + (0,), wkey + (1,)]
            for r_ in range(dil):
                for kt0 in range(0, nkt, 4):
                    nk = min(4, nkt - kt0)
                    bt = psum("e")
                    ptb = PS[bt].bitcast(BF16)
                    ins = []
                    for q in range(nk):
                        t0 = r_ + dil * 128 * (kt0 + q)
                        ins.append(I("transpose", out=ptb[:, q * 128:(q + 1) * 128], in_=vT[:, t0:t0 + 127 * dil + 1:dil], identity=ident))
                    P.op("pe", ins, r=v_keys + [("cb",)], w=[("ps", bt)])
                    ti = r_ * nkt + kt0
                    pt4 = ptb[:, 0:nk * 128].rearrange("p (t h e) -> p t h e", h=2, e=64)
                    P.op("act", [I("copy", out=vtok[:, ti:ti + nk, 0:64], in_=pt4[:, :, 0, :]),
                                 I("copy", out=vtok[:, ti:ti + nk, 128:192], in_=pt4[:, :, 1, :])], r=[("ps", bt)], w=[("vtok", ti)])
            vt_keys = [("vtok", r_ * nkt + kt0) for r_ in range(dil) for kt0 in range(0, nkt, 4)]
            units = []
            for r_ in range(dil):
                for kt in range(nkt):
                    x0 = 64 if kt == 0 else 0
                    x1 = 192 if kt == nkt - 1 else 256
                    base = 128 * kt - 64
                    a, bnd = base + x0, base + x1
                    pieces = []
                    if dil == 1:
                        for u in range((a + 64) // 512, (bnd - 1 + 64) // 512 + 1):
                            ma, mb = max(a, 512 * u - 64), min(bnd, 512 * u + 448)
                            pieces.append((("b", u), ma + 64 - 512 * u, ma - base, mb - base))
                    elif dil == 4:
                        pieces.append((("b", r_), a, x0, x1))
                    else:
                        pieces.append((("b", r_ // 4), (r_ % 4) * 128 + a, x0, x1))
                    units.append(dict(r=r_, kt=kt, x0=x0, x1=x1, base=base, pieces=pieces))
            inst_order = []
            contrib = {}
            for ui, un in enumerate(units):
                for pi_, pc in enumerate(un["pieces"]):
                    if pc[0] not in contrib:
                        contrib[pc[0]] = []
                        inst_order.append(pc[0])
                    contrib[pc[0]].append((ui, pi_))
            inst_idx = {k: n for n, k in enumerate(inst_order)}
            upairs = [units[i:i + 2] for i in range(0, len(units), 2)]

            def evac(inst, hh, dil=dil, L=L):
                b = pvbanks[hh][inst_idx[inst] % 2]
                if dil == 1:
                    u = inst[1]
                    m0, m1 = max(0, 512 * u - 64), min(L, 512 * u + 448)
                    c0 = m0 + 64 - 512 * u
                    dst = acc[:, hh, m0:m1]
                    src = PS[b][:, c0:c0 + (m1 - m0)]
                    akeys = [("acc", t) for t in range(m0 // TW, (m1 - 1) // TW + 1)]
                elif dil == 4:
                    dst = acc[:, hh, inst[1]:inst[1] + 4 * 511 + 1:4]
                    src = PS[b]
                    akeys = [("acc", t) for t in range(NT)]
                else:
                    g = inst[1]
                    dst = acc[:, hh, :].rearrange("p (m r) -> p r m", r=16)[:, 4 * g:4 * g + 4, :]
                    src = PS[b].rearrange("p (r m) -> p r m", r=4)
                    akeys = [("acc", t) for t in range(NT)]
                P.op("dve", [I("tensor_tensor", out=dst, in0=dst, in1=src, op=ALU.add)], r=[("ps", b)] + akeys, w=akeys)

            def blk_stream(si, hh, dil=dil, nkt=nkt, Wc=Wc, wkey=wkey, vt_keys=vt_keys, units=units, upairs=upairs, contrib=contrib, inst_idx=inst_idx):
                p_ = pTs[si][:, hh * 512:(hh + 1) * 512]
                pkey = ("pT", si, hh)
                rows = slice(hh * 64, hh * 64 + 64)
                sb = sbanks[si][hh]
                for up in upairs[si::NS]:
                    ncol = 256 * len(up)
                    ins = []
                    for slot, un in enumerate(up):
                        k0 = un["r"] + dil * 128 * un["kt"]
                        q0 = un["r"] + dil * (un["base"] + un["x0"])
                        nq = un["x1"] - un["x0"]
                        ins.append(I("matmul", out=PS[sb][:, slot * 256 + un["x0"]:slot * 256 + un["x1"]], lhsT=kT[rows, k0:k0 + 127 * dil + 1:dil],
                                     rhs=qT[rows, q0:q0 + (nq - 1) * dil + 1:dil], start=True, stop=True))
                    P.op("pe", ins, r=qk_keys, w=[("ps", sb)])
                    P.op("act", [I("activation", out=p_[:, 0:ncol], in_=PS[sb][:, 0:ncol], func=AF.Exp, scale=0.125)], r=[("ps", sb)], w=[pkey])
                    yield
                    nsl = len(up)
                    pv = p_.rearrange("p (s x) -> p s x", s=2)[:, 0:nsl, :]
                    wv = Wc[:, hh * 256:(hh + 1) * 256].rearrange("p (o x) -> p o x", o=1).broadcast_to([128, nsl, 256])
                    P.op("dve", [I("tensor_tensor", out=pv, in0=pv, in1=wv, op=ALU.mult)], r=[pkey, wkey + (hh,)], w=[pkey])
                    yield
                    ins = []
                    wb = set()
                    closing = []
                    for slot, un in enumerate(up):
                        ui = units.index(un)
                        tile = un["r"] * nkt + un["kt"]
                        for pi_, (inst, col0, xa, xb) in enumerate(un["pieces"]):
                            first = contrib[inst][0] == (ui, pi_)
                            last = contrib[inst][-1] == (ui, pi_)
                            b = pvbanks[hh][inst_idx[inst] % 2]
                            wb.add(b)
                            ins.append(I("matmul", out=PS[b][:, col0:col0 + (xb - xa)], lhsT=vtok[:, tile, hh * 64:hh * 64 + 128],
                                         rhs=p_[:, slot * 256 + xa:slot * 256 + xb], start=first, stop=last, skip_group_check=True))
                            if last:
                                closing.append(inst)
                    P.op("pe", ins, r=[pkey] + vt_keys + [("vones",)], w=[("ps", b) for b in sorted(wb)])
                    for inst in closing:
                        evac(inst, hh)
                    yield

            run_streams([blk_stream(si, hh) for si in range(NS) for hh in range(2)])
        for tt in range(NT):
            tsl = tsl_(tt)
            dk = [("pT", 1, 0), ("pT", 1, 1)]
            P.op("act", [I("activation", out=dtmp[0:64, :], in_=acc[64:128, 0, tsl], func=AF.Ln), I("activation", out=dtmp[64:128, :], in_=acc[0:64, 1, tsl], func=AF.Ln)],
                 r=[("acc", tt)], w=dk)
            P.op("act", [I("activation", out=dtmp, in_=dtmp, func=AF.Exp, scale=-1.0)], r=dk, w=dk)
            P.op("dve", [I("tensor_tensor", out=y3[0:64, 4 + j, tsl], in0=acc[0:64, 0, tsl], in1=dtmp[0:64, :], op=ALU.mult),
                         I("tensor_tensor", out=y3[64:128, 4 + j, tsl], in0=acc[64:128, 1, tsl], in1=dtmp[64:128, :], op=ALU.mult)],
                 r=[("acc", tt)] + dk, w=[ykey(4 + j, tt)])

    def mixer(l):
        P.barrier()
        norm(h3, hkey, CV_MIX + l * 8, u3, ukey, NT, TW)
        slots = load_hgrn(l, 0)
        for hd in range(4):
            nxt = load_hgrn(l, hd + 1) if hd < 3 else load_attn(l, 0)
            hgrn_head(l, hd, slots)
            slots = nxt
        P.barrier()
        for j in range(4):
            nxt = load_attn(l, j + 1) if j < 3 else None
            attn_pair(l, j, slots)
            slots = nxt
        P.barrier()
        proj(y3, ykey, Wd["w_out"][l], D, lambda dc, tt, b: add_to_h(b, dc, tt, 1.0))

    def xattn(l, s):
        P.barrier()
        memf = AF32(0, 2048).rearrange("p (k m) -> p k m", k=KC)
        memn = ABF(2048, 2048).rearrange("p (k m) -> p k m", k=KC)
        kTm = ABF(3072, 2048).rearrange("p (k m) -> p k m", k=KC)
        vm = ABF(4096, 2048).rearrange("p (t c) -> p t c", t=2)
        pT = [ABF(5120 + i * 256, 512) for i in range(4)]
        rden = AF32(6144, 512)
        P.op("sp", [I("dma_start", out=memf, in_=memT[s].rearrange("(k p) m -> p k m", p=128))], w=[("memf",)], slot="mem")
        norm(memf, lambda kc, tt: ("memf",), CV_MEM + l * 8, memn, lambda kc, tt: ("memn", kc), 1, NMEM)
        norm(h3, hkey, CV_XQ + l * 8, u3, ukey, NT, TW)
        Wkv = Wd["w_xkv"][l]
        mkeys = [("memn", k) for k in range(KC)]
        for g in range(4):
            sw = wload([(lambda sv: v8(sv), wsrc(Wkv, g * 512, 512))])
            w8 = v8(slot_view(sw))
            if g < 2:
                for mh in range(2):
                    b = psum("a")
                    ins = []
                    for cc in range(4):
                        for k in range(KC):
                            ins.append(I("matmul", out=PS[b][:, cc * 128:cc * 128 + 128], lhsT=w8[:, k, cc * 128:(cc + 1) * 128], rhs=memn[:, k, mh * 128:(mh + 1) * 128],
                                         start=(k == 0), stop=(k == KC - 1)))
                    P.op("pe", ins, r=wkeys(sw) + mkeys, w=[("ps", b)])
                    P.op("act", [I("copy", out=kTm[:, g * 4:(g + 1) * 4, mh * 128:(mh + 1) * 128], in_=PS[b].rearrange("p (c m) -> p c m", c=4))],
                         r=[("ps", b)], w=[("kTm", g, mh)])
            else:
                for mt in range(2):
                    b = psum("a")
                    mm_group(b, [memn[:, k, mt * 128:(mt + 1) * 128] for k in range(KC)], [w8[:, k, :] for k in range(KC)], wkeys(sw) + mkeys)
                    P.op("act", [I("copy", out=vm[:, mt, (g - 2) * 512:(g - 1) * 512], in_=PS[b])], r=[("ps", b)], w=[("vm", g, mt)])
        ktkeys = [("kTm", g, mh) for g in range(2) for mh in range(2)]
        vmkeys = [("vm", g, mt) for g in (2, 3) for mt in range(2)]

        def epi_q(c, tt, b):
            P.op("act", [I("copy", out=y3[:, c, tsl_(tt)], in_=PS[b])], r=[("ps", b)], w=[ykey(c, tt)])
        proj(u3, ukey, Wd["w_xq"][l], D, epi_q)
        pi = 0
        for tt in range(NT):
            tsl = tsl_(tt)
            for m in range(4):
                pk = []
                for mt in range(2):
                    b = psum("b")
                    mm_group(b, [kTm[:, 2 * m + k, mt * 128:(mt + 1) * 128] for k in range(2)], [y3[:, 2 * m + k, tsl] for k in range(2)],
                             ktkeys + [ykey(2 * m, tt), ykey(2 * m + 1, tt)])
                    p_ = pT[pi % 4]
                    pkey = ("xp", pi % 4)
                    pi += 1
                    P.op("act", [I("activation", out=p_, in_=PS[b], func=AF.Exp, scale=1.0 / 16.0)], r=[("ps", b)], w=[pkey])
                    pk.append((p_, pkey))
                bd = psum("d")
                mm_group(bd, [ones, ones], [pk[0][0], pk[1][0]], [pk[0][1], pk[1][1], ("cb",)])
                P.op("act", [I("activation", out=rden, in_=PS[bd], func=AF.Ln)], r=[("ps", bd)], w=[("rden",)])
                P.op("act", [I("activation", out=rden, in_=rden, func=AF.Exp, scale=-1.0)], r=[("rden",)], w=[("rden",)])
                for cc in range(2):
                    bo = psum("c")
                    mm_group(bo, [vm[:, k, (2 * m + cc) * 128:(2 * m + cc + 1) * 128] for k in range(2)], [pk[0][0], pk[1][0]],
                             [pk[0][1], pk[1][1]] + vmkeys)
                    P.op("dve", [I("tensor_tensor", out=u3[:, 2 * m + cc, tsl], in0=PS[bo], in1=rden, op=ALU.mult)],
                         r=[("ps", bo), ("rden",)], w=[ukey(2 * m + cc, tt)])
        proj(u3, ukey, Wd["w_xo"][l], D, lambda dc, tt, b: add_to_h(b, dc, tt, 1.0))

    stages = []
    for l in range(DEPTH):
        stages += [("ffn1", l), ("mixer", l), ("xattn", l), ("ffn2", l)]
    if stop_after is not None:
        stages = stages[:stop_after]
    for s in range(nseq):
        for kc in range(KC):
            P.op("sp", [I("dma_start", out=h3[:, kc, :], in_=xT[s, kc * 128:(kc + 1) * 128, :])], w=[hkey(kc, tt) for tt in range(NT)], slot="x%d" % kc)
        for st, l in stages:
            if st == "ffn1":
                ffn(l, 1)
            elif st == "mixer":
                mixer(l)
            elif st == "xattn":
                xattn(l, s)
            else:
                ffn(l, 2)
        if stop_after is None:
            norm(h3, hkey, CV_FIN, h3, hkey, NT, TW)
        for kc in range(KC):
            P.op("sp", [I("dma_start", out=outT[s, kc * 128:(kc + 1) * 128, :], in_=h3[:, kc, :])], r=[hkey(kc, tt) for tt in range(NT)], slot="o%d" % kc)
    P.barrier()
    P.op("sp", None)
    P.emit()
    return nc


def host_tables():
    ct = np.zeros((128, NCT), np.float32)
    ct[:, CT_ID:CT_ID + 128] = np.eye(128, dtype=np.float32)
    ct[:, CT_ONE:CT_ONE + 128] = 1.0
    s = np.arange(128)[:, None]
    t = np.arange(128)[None, :]
    same = (s // 64) == (t // 64)
    ct[:, CT_MF:CT_MF + 128] = (same & (s <= t)).astype(np.float32)
    ct[:, CT_MB:CT_MB + 128] = (same & (s >= t)).astype(np.float32)
    j = np.arange(128)[:, None]
    i = np.arange(128)[None, :]
    xq = np.arange(256)[None, :]
    rel = np.abs(j + 64 - xq).astype(np.float32)
    ct[:, CT_VAL:CT_VAL + 256] = (rel <= 64).astype(np.float32)
    ct[:, CT_REL:CT_REL + 256] = np.minimum(rel, 80.0)
    r0 = np.abs(j - i).astype(np.float32)
    ct[:, CT_V0:CT_V0 + 128] = (r0 <= 64).astype(np.float32)
    ct[:, CT_R0:CT_R0 + 128] = np.minimum(r0, 80.0)
    sm = np.ones((128, 512), np.float32)
    sm[:, ::64] = 0.0
    ct[:, CT_SCAN:CT_SCAN + 512] = sm
    return ct


def pack_cvec(inp):
    cvv = np.zeros((128, NCV), np.float32)

    def put(col, a, inner):
        a = np.asarray(a, np.float32)
        lead = int(np.prod(a.shape[:-1])) if a.ndim > 1 else 1
        a = a.reshape(lead, inner, 128)
        cvv[:, col:col + lead * inner] = a.transpose(2, 0, 1).reshape(128, lead * inner)
    put(CV_FFN1, inp["ln_ffn1"], 8)
    put(CV_MIX, inp["ln_mix"], 8)
    put(CV_XQ, inp["ln_xq"], 8)
    put(CV_MEM, inp["ln_mem"], 8)
    put(CV_FFN2, inp["ln_ffn2"], 8)
    put(CV_FIN, inp["ln_final"], 8)
    put(CV_ON, inp["hgrn_out_norm"], 4)
    put(CV_LB, inp["hgrn_lb_logits"], 4)
    return cvv


_CACHE = {}


def run(inputs, nseq, core_ids, stop_after=None):
    key = (nseq, stop_after)
    if key not in _CACHE:
        _CACHE[key] = build(nseq, stop_after)
    nc = _CACHE[key]
    x = np.asarray(inputs["x"], np.float32)
    mem = np.asarray(inputs["mem"], np.float32)
    ct = host_tables()
    cvv = pack_cvec(inputs)
    wts = {nm: np.ascontiguousarray(np.asarray(inputs[nm], np.float32)) for nm, _ in WNAMES}
    in_maps = []
    for ci in range(len(core_ids)):
        sl = slice(ci * nseq, (ci + 1) * nseq)
        m = {"xT": np.ascontiguousarray(x[sl].transpose(0, 2, 1)), "memT": np.ascontiguousarray(mem[sl].transpose(0, 2, 1)),
             "cvec": cvv, "ctab": ct}
        m.update(wts)
        in_maps.append(m)
    res = run_bass_kernel_spmd(nc, in_maps, core_ids=core_ids)
    out = np.concatenate([np.asarray(r["outT"]).transpose(0, 2, 1) for r in res.results], axis=0)
    return np.ascontiguousarray(out.astype(np.float32))


def kernel(**inputs):
    B = inputs["x"].shape[0]
    return run(inputs, B // NCORES, list(range(NCORES)))
```

```python
import numpy as np
import concourse.bass as bass
import concourse.mybir as mybir
from concourse.bass_utils import run_bass_kernel_spmd

F32 = mybir.dt.float32
BF16 = mybir.dt.bfloat16
ALU = mybir.AluOpType
AF = mybir.ActivationFunctionType

D = 1024
S = 2048
NMEM = 256
DFF = 2816
DEPTH = 2
KC = 8
NT = 4
TW = 512
EPS = 1e-6
NCORES = 8
WNAMES = [("ffn1_w_gate", [DEPTH, D, DFF]), ("ffn1_w_up", [DEPTH, D, DFF]), ("ffn1_w_down", [DEPTH, DFF, D]),
          ("w_in", [DEPTH, D, 4096]), ("w_out", [DEPTH, D, D]), ("w_xq", [DEPTH, D, D]),
          ("w_xkv", [DEPTH, D, 2 * D]), ("w_xo", [DEPTH, D, D]),
          ("ffn2_w_gate", [DEPTH, D, DFF]), ("ffn2_w_up", [DEPTH, D, DFF]), ("ffn2_w_down", [DEPTH, DFF, D])]
CV_FFN1, CV_MIX, CV_XQ, CV_MEM, CV_FFN2, CV_FIN, CV_ON, CV_LB, NCV = 0, 16, 32, 48, 64, 80, 88, 96, 112
CT_ID, CT_ONE, CT_MF, CT_MB, CT_VAL, CT_V0, CT_REL, CT_R0, CT_SCAN, NCT = 0, 128, 256, 384, 512, 768, 896, 1152, 1280, 1792
EPOCH = 50000
SAME_ENG_SYNC = True


class Node:
    __slots__ = ("id", "eng", "fn", "deps", "slot", "sig", "cnt", "grp_last")


class Prog:
    ENGS = ["pe", "act", "dve", "pool", "sp"]

    def __init__(self, nc):
        self.nc = nc
        self.nodes = []
        self.lastw = {}
        self.readers = {}
        self.fence = {}
        self.last_on = {}
        self.last_dma = {}

    def op(self, eng, fn, r=(), w=(), slot=None):
        n = Node()
        n.id = len(self.nodes)
        n.eng = eng
        n.fn = fn
        n.slot = slot
        n.sig = slot is not None
        n.cnt = 0
        n.grp_last = None
        deps = set()
        w = list(w) + [k for k in r if k[0] == "ps" and k not in w]
        for k in r:
            if k in self.lastw:
                deps.add(self.lastw[k])
        for k in w:
            if k in self.lastw:
                deps.add(self.lastw[k])
            deps.update(self.readers.get(k, ()))
        if eng in self.fence:
            deps.update(self.fence.pop(eng))
        n.deps = deps
        for k in r:
            self.readers.setdefault(k, []).append(n.id)
        for k in w:
            self.lastw[k] = n.id
            self.readers[k] = []
        self.nodes.append(n)
        if slot is None:
            self.last_on[eng] = n.id
        else:
            self.last_dma[slot] = n.id
        return n.id

    def barrier(self):
        ids = set(self.last_on.values()) | set(self.last_dma.values())
        for e in self.ENGS:
            self.fence[e] = set(ids) | self.fence.get(e, set())

    def emit(self):
        nc = self.nc
        nodes = self.nodes
        for n in nodes:
            for d in n.deps:
                nodes[d].sig = True
        cnt = {}
        for n in nodes:
            if not n.sig:
                continue
            key = ("d", n.slot) if n.slot is not None else ("e", n.eng)
            cnt[key] = cnt.get(key, 0) + 1
            n.cnt = cnt[key]
        sems = {}

        def sem_for(key, c):
            ep = (c - 1) // EPOCH
            k = (key, ep)
            if k not in sems:
                sems[k] = nc.alloc_semaphore("s_%s_%s_%d" % (key[0], key[1], ep))
            return sems[k], (c - 1) % EPOCH + 1

        bname = {"pe": "tensor", "act": "scalar", "dve": "vector", "pool": "gpsimd", "sp": "sync"}
        with nc.Block() as block:
            for eng in self.ENGS:
                mine = [n for n in nodes if n.eng == eng]

                def body(e, mine=mine, eng=eng):
                    seen = {}
                    for n in mine:
                        need = {}
                        for d in n.deps:
                            dn = nodes[d]
                            if dn.slot is not None:
                                key = ("d", dn.slot)
                                if dn.grp_last is not None:
                                    dn = nodes[dn.grp_last]
                            else:
                                if dn.eng == eng and (eng == "pe" or not SAME_ENG_SYNC):
                                    continue
                                key = ("e", dn.eng)
                            if dn.cnt > need.get(key, 0):
                                need[key] = dn.cnt
                        for key, c in need.items():
                            if seen.get(key, 0) >= c:
                                continue
                            seen[key] = c
                            sm, v = sem_for(key, c)
                            e.wait_ge(sm, v * (16 if key[0] == "d" else 1))
                        if n.fn is None:
                            continue
                        ins = None
                        for m_, kw_ in n.fn:
                            ins = getattr(e, m_)(**kw_)
                        if n.sig:
                            key = ("d", n.slot) if n.slot is not None else ("e", n.eng)
                            sm, v = sem_for(key, n.cnt)
                            ins.then_inc(sm, 16 if n.slot is not None else 1)

                getattr(block, bname[eng])(body)


def I(m, **kw):
    return (m, kw)


def build(nseq, stop_after=None):
    nc = bass.Bass("TRN2", target_bir_lowering=False)
    P = Prog(nc)
    xT = nc.dram_tensor("xT", [nseq, D, S], F32, kind="ExternalInput").ap()
    memT = nc.dram_tensor("memT", [nseq, D, NMEM], F32, kind="ExternalInput").ap()
    Wd = {nm: nc.dram_tensor(nm, shp, F32, kind="ExternalInput").ap() for nm, shp in WNAMES}
    cvec_d = nc.dram_tensor("cvec", [128, NCV], F32, kind="ExternalInput").ap()
    ctab_d = nc.dram_tensor("ctab", [128, NCT], F32, kind="ExternalInput").ap()
    outT = nc.dram_tensor("outT", [nseq, D, S], F32, kind="ExternalOutput").ap()

    hbuf = nc.alloc_sbuf_tensor("hbuf", [128, KC * S], F32).ap()
    ubuf = nc.alloc_sbuf_tensor("ubuf", [128, KC * S], BF16).ap()
    ybuf = nc.alloc_sbuf_tensor("ybuf", [128, KC * S], BF16).ap()
    wpool = nc.alloc_sbuf_tensor("wpool", [128, 4 * 4096], BF16).ap()
    cb = nc.alloc_sbuf_tensor("cb", [128, 896], BF16).ap()
    cf = nc.alloc_sbuf_tensor("cf", [128, 896], F32).ap()
    cv = nc.alloc_sbuf_tensor("cv", [128, NCV + 48], F32).ap()
    ARENA = 10560
    arena = nc.alloc_sbuf_tensor("arena", [128, ARENA], F32).ap()
    PS = [nc.alloc_psum_tensor("ps%d" % b, [128, TW], F32).ap() for b in range(8)]

    h3 = hbuf.rearrange("p (k t) -> p k t", k=KC)
    u3 = ubuf.rearrange("p (k t) -> p k t", k=KC)
    y3 = ybuf.rearrange("p (k t) -> p k t", k=KC)
    ident = cb[:, 0:128]
    ones = cb[:, 128:256]
    mfb = [cb[:, 256:384], cb[:, 384:512]]
    valid = cb[:, 512:768]
    valid0 = cb[:, 768:896]
    relabs = cf[:, 0:256]
    rel0 = cf[:, 256:384]
    scanmask = cf[:, 384:896]
    CV_LBV, CV_OML, CV_TMP = NCV, NCV + 16, NCV + 32

    def AF32(off, n):
        assert off + n <= ARENA
        return arena[:, off:off + n]

    def ABF(off, n):
        assert off + n // 2 <= ARENA
        return arena[:, off:off + n // 2].bitcast(BF16)

    nsq = ABF(8960, 2 * TW)
    nrs = AF32(9472, TW)

    def tsl_(tt):
        return slice(tt * TW, (tt + 1) * TW)

    MAXP = 5
    wstate = {"next": 0, "nslots": 4}

    def slot_view(s):
        if s < 4:
            return wpool[:, s * 4096:(s + 1) * 4096]
        return ybuf[:, (s - 4) * 4096:(s - 3) * 4096]

    def wkeys(s):
        return [("w", s, i) for i in range(MAXP)]

    def wload(parts):
        s = wstate["next"] % wstate["nslots"]
        wstate["next"] += 1
        sv = slot_view(s)
        ids = []
        for i, (dst_fn, src) in enumerate(parts):
            wk = [("w", s, i)]
            if i == 0:
                wk += [("w", s, j) for j in range(len(parts), MAXP)]
            ids.append(P.op("pool", [I("dma_start", out=dst_fn(sv), in_=src)], w=wk, slot="w%d" % s))
        for i_ in ids:
            P.nodes[i_].grp_last = ids[-1]
        return s

    def v8(sv):
        return sv.rearrange("p (k c) -> p k c", k=8)

    def wsrc(Wl, c0, n):
        return Wl[:, c0:c0 + n].rearrange("(kc p) c -> p kc c", p=128)

    psr = {"a": [0, 1], "b": [2, 3], "c": [4, 5], "d": [6], "e": [7]}
    psn = {k: 0 for k in psr}
    psq = {"n": 0}

    def psum(cls):
        b = psr[cls][psn[cls] % len(psr[cls])]
        psn[cls] += 1
        return b

    P.op("sp", [I("dma_start", out=cv[:, 0:NCV], in_=cvec_d)], w=[("cv",)], slot="c0")
    P.op("sp", [I("dma_start", out=cf, in_=ctab_d[:, CT_REL:NCT])], w=[("cf",)], slot="c2")
    P.op("pool", [I("dma_start", out=cb, in_=ctab_d[:, 0:CT_REL])], w=[("cb",)], slot="c1")
    T0 = CV_TMP
    P.op("act", [I("activation", out=cv[:, T0:T0 + 16], in_=cv[:, CV_LB:CV_LB + 16], func=AF.Exp)], r=[("cv",)], w=[("cvt",)])
    P.op("dve", [I("tensor_tensor", out=cv[:, T0:T0 + 8], in0=cv[:, T0:T0 + 8], in1=cv[:, T0 + 8:T0 + 16], op=ALU.add)], r=[("cvt",)], w=[("cvt",)])
    P.op("dve", [I("reciprocal", out=cv[:, T0:T0 + 8], in_=cv[:, T0:T0 + 8])], r=[("cvt",)], w=[("cvt",)])
    P.op("dve", [I("memset", ap=cv[:, CV_LBV:CV_LBV + 8], constant=0.0)], w=[("lb0",)])
    P.op("dve", [I("tensor_tensor", out=cv[:, CV_LBV + 8:CV_LBV + 16], in0=cv[:, T0 + 8:T0 + 16], in1=cv[:, T0:T0 + 8], op=ALU.mult)], r=[("cvt",)], w=[("lb1",)])
    P.op("dve", [I("tensor_scalar", out=cv[:, CV_OML:CV_OML + 16], in0=cv[:, CV_LBV:CV_LBV + 16], scalar1=-1.0, scalar2=1.0, op0=ALU.mult, op1=ALU.add)],
         r=[("lb0",), ("lb1",)], w=[("lbd",)])
    CONSTR = [("cv",), ("cf",), ("cb",), ("lbd",), ("lb0",), ("lb1",)]

    hkey = lambda kc, tt: ("h", kc, tt)
    ukey = lambda kc, tt: ("u", kc, tt)
    ykey = lambda kc, tt: ("y", kc, tt)

    def norm(src3, srckey, gcol, dst3, dstkey, ntiles, tw, dn=D):
        for tt in range(ntiles):
            tsl = slice(tt * tw, (tt + 1) * tw)
            b = psum("d")
            for kc in range(KC):
                j = kc % 2
                sq = nsq[:, j * TW:j * TW + tw]
                P.op("act", [I("activation", out=sq, in_=src3[:, kc, tsl], func=AF.Square)], r=[srckey(kc, tt)], w=[("nsq", j)])
                P.op("pe", [I("matmul", out=PS[b][:, 0:tw], lhsT=ones, rhs=sq, start=(kc == 0), stop=(kc == KC - 1))], r=[("nsq", j), ("cb",)], w=[("ps", b)])
            P.op("act", [I("activation", out=nrs[:, 0:tw], in_=PS[b][:, 0:tw], func=AF.Ln, bias=EPS, scale=1.0 / dn)], r=[("ps", b)], w=[("nrs",)])
            P.op("act", [I("activation", out=nrs[:, 0:tw], in_=nrs[:, 0:tw], func=AF.Exp, scale=-0.5)], r=[("nrs",)], w=[("nrs",)])
            for kc in range(KC):
                P.op("dve", [I("scalar_tensor_tensor", out=dst3[:, kc, tsl], in0=src3[:, kc, tsl], scalar=cv[:, gcol + kc:gcol + kc + 1],
                               in1=nrs[:, 0:tw], op0=ALU.mult, op1=ALU.mult)],
                     r=[srckey(kc, tt), ("nrs",), ("cv",)], w=[dstkey(kc, tt)])

    def mm_group(b, lhs, rhs, rkeys, n=TW, col0=0):
        nk = len(lhs)
        P.op("pe", [I("matmul", out=PS[b][:, col0:col0 + n], lhsT=lhs[k], rhs=rhs[k], start=(k == 0), stop=(k == nk - 1)) for k in range(nk)],
             r=rkeys, w=[("ps", b)])

    def add_to_h(b, dc, tt, scale):
        tsl = tsl_(tt)
        P.op("dve", [I("scalar_tensor_tensor", out=h3[:, dc, tsl], in0=PS[b], scalar=scale, in1=h3[:, dc, tsl], op0=ALU.mult, op1=ALU.add)],
             r=[("ps", b), hkey(dc, tt)], w=[hkey(dc, tt)])

    def proj(src3, skey, Wl, ncols, epi, pcls="a"):
        for g0 in range(0, ncols, 512):
            gw = min(512, ncols - g0)
            s = wload([(lambda sv: v8(sv)[:, :, 0:gw], wsrc(Wl, g0, gw))])
            w8 = v8(slot_view(s))
            for tt in range(NT):
                tsl = tsl_(tt)
                for cc in range(gw // 128):
                    b = psum(pcls)
                    mm_group(b, [w8[:, k, cc * 128:(cc + 1) * 128] for k in range(KC)], [src3[:, k, tsl] for k in range(KC)],
                             wkeys(s) + [skey(k, tt) for k in range(KC)])
                    epi(g0 // 128 + cc, tt, b)

    def ffn(l, which):
        pre = "ffn%d_" % which
        gcol = (CV_FFN1 if which == 1 else CV_FFN2) + l * 8
        P.barrier()
        wstate["nslots"] = 8
        norm(h3, hkey, gcol, u3, ukey, NT, TW)
        Wg, Wu, Wdn = Wd[pre + "w_gate"][l], Wd[pre + "w_up"][l], Wd[pre + "w_down"][l]
        act = [ABF(0, 2048).rearrange("p (f t) -> p f t", f=4), ABF(1024, 2048).rearrange("p (f t) -> p f t", f=4)]
        sg = [AF32(2048, 512), AF32(2560, 512)]
        ai = 0
        si = 0
        for f0 in range(0, DFF, 512):
            gw = min(512, DFF - f0)
            nf = gw // 128
            sgt = wload([(lambda sv: v8(sv)[:, :, 0:gw], wsrc(Wg, f0, gw))])
            sup = wload([(lambda sv: v8(sv)[:, :, 0:gw], wsrc(Wu, f0, gw))])
            sdn = wload([(lambda sv: sv.rearrange("p (f d) -> p f d", f=4)[:, 0:nf, :], Wdn[f0:f0 + gw, :].rearrange("(f p) d -> p f d", p=128))])
            wg8, wu8 = v8(slot_view(sgt)), v8(slot_view(sup))
            wd4 = slot_view(sdn).rearrange("p (f d) -> p f d", f=4)
            for tt in range(NT):
                tsl = tsl_(tt)
                a = act[ai % 2]
                akey = ("act", ai % 2)
                ai += 1
                ukeys = [ukey(k, tt) for k in range(KC)]
                urhs = [u3[:, k, tsl] for k in range(KC)]
                for fc in range(nf):
                    bg = psum("a")
                    mm_group(bg, [wg8[:, k, fc * 128:(fc + 1) * 128] for k in range(KC)], urhs, wkeys(sgt) + ukeys)
                    bu = psum("b")
                    mm_group(bu, [wu8[:, k, fc * 128:(fc + 1) * 128] for k in range(KC)], urhs, wkeys(sup) + ukeys)
                    sgb = sg[si % 2]
                    sk = ("sg", si % 2)
                    si += 1
                    P.op("act", [I("activation", out=sgb, in_=PS[bg], func=AF.Silu)], r=[("ps", bg)], w=[sk])
                    P.op("dve", [I("tensor_tensor", out=a[:, fc, :], in0=sgb, in1=PS[bu], op=ALU.mult)], r=[sk, ("ps", bu)], w=[akey + (fc,)])
                for dc in range(KC):
                    bd = psum("c")
                    mm_group(bd, [wd4[:, k, dc * 128:(dc + 1) * 128] for k in range(nf)], [a[:, k, :] for k in range(nf)],
                             wkeys(sdn) + [akey + (fc,) for fc in range(nf)])
                    add_to_h(bd, dc, tt, 0.5)
        wstate["nslots"] = 4

    def run_streams(gens):
        gens = list(gens)
        while gens:
            for g in list(gens):
                try:
                    next(g)
                except StopIteration:
                    gens.remove(g)

    ext = ybuf[:, 8192:16384].bitcast(F32)

    def EF32(off, n):
        return ext[:, off:off + n]

    def EBF(off, n):
        return ext[:, off:off + n // 2].bitcast(BF16)

    def load_hgrn(l, hd):
        Win = Wd["w_in"][l]
        col = lambda j: j * 512 + hd * 128
        sA = wload([((lambda sv, j=j: v8(sv)[:, :, j * 128:(j + 1) * 128]), wsrc(Win, col(j), 128)) for j in range(4)])
        sB = wload([(lambda sv: v8(sv)[:, :, 0:128], wsrc(Win, col(4), 128))])
        return (sA, sB)

    def load_attn(l, j):
        Win = Wd["w_in"][l]
        cols = [2560 + j * 128, 3072 + j * 128, 3584 + j * 128]
        return wload([((lambda sv, i=i: v8(sv)[:, :, i * 128:(i + 1) * 128]), wsrc(Win, cols[i], 128)) for i in range(3)])

    def hgrn_head(l, hd, slots):
        sA, sB = slots
        wA, wB = v8(slot_view(sA)), v8(slot_view(sB))
        q_sb = ABF(0, 2048)
        i_tok = ABF(1024, 2048).rearrange("p (t v) -> p t v", t=16)
        o_acc = AF32(2048, 2048)
        Bs = [[AF32(4096 + i * 512, 512) for i in range(5)], [EF32(i * 512, 512) for i in range(5)]]
        QK = [[ABF(6656 + i * 256, 512) for i in range(4)], [EBF(2560 + i * 256, 512) for i in range(4)]]
        SCM = [ABF(7680, 512), EBF(3584, 512)]
        KLT = [ABF(7936, 512).rearrange("p (c k) -> p c k", c=4), EBF(3840, 512).rearrange("p (c k) -> p c k", c=4)]
        STATE = [[AF32(8192, 128), AF32(9984, 128)], [AF32(8320, 128), AF32(10112, 128)]]
        STB = [[ABF(8448, 128), ABF(8512, 128)], [ABF(8576, 128), ABF(8640, 128)]]
        AE = [AF32(8704, 8), AF32(8712, 8)]
        TL = [AF32(8720, 8), AF32(8728, 8)]
        lbc = CV_LBV + l * 8
        omc = CV_OML + l * 8
        allu = lambda tt: [ukey(k, tt) for k in range(KC)]
        for tt in range(NT):
            tsl = tsl_(tt)
            P.op("dve", [I("memset", ap=o_acc[:, tsl], constant=0.0)], w=[("oacc", tt)])
            b = psum("a")
            mm_group(b, [wA[:, k, 0:128] for k in range(KC)], [u3[:, k, tsl] for k in range(KC)], wkeys(sA) + allu(tt))
            P.op("act", [I("copy", out=q_sb[:, tsl], in_=PS[b])], r=[("ps", b)], w=[("q", tt)])
            b2 = psum("b")
            ins = []
            for t4 in range(4):
                for k in range(KC):
                    ins.append(I("matmul", out=PS[b2][:, t4 * 128:(t4 + 1) * 128], lhsT=u3[:, k, tt * TW + t4 * 128:tt * TW + (t4 + 1) * 128], rhs=wA[:, k, 128:256],
                                 start=(k == 0), stop=(k == KC - 1)))
            P.op("pe", ins, r=wkeys(sA) + allu(tt), w=[("ps", b2)])
            P.op("dve", [I("tensor_copy", out=i_tok[:, tt * 4:(tt + 1) * 4, :], in_=PS[b2].rearrange("p (t v) -> p t v", t=4))], r=[("ps", b2)], w=[("itok", tt)])

        def dir_stream(dr):
            B = Bs[dr]
            B3 = [x.rearrange("p (c t) -> p c t", t=64) for x in B]
            qa, qm, km, kl = QK[dr]
            scm, kl_tok, state2, state_bf, aE, tl = SCM[dr], KLT[dr], STATE[dr], STB[dr], AE[dr], TL[dr]
            tl3 = tl.rearrange("p (c o) -> p c o", o=1)
            K = lambda nm, *a: (nm, dr) + a
            lb_ap = cv[:, lbc + dr * 4 + hd:lbc + dr * 4 + hd + 1]
            om_ap = cv[:, omc + dr * 4 + hd:omc + dr * 4 + hd + 1]
            P.op("dve", [I("memset", ap=state2[0], constant=0.0)], w=[K("st", 0)])
            P.op("dve", [I("memset", ap=state_bf[0], constant=0.0)], w=[K("stb", 0)])
            yield
            sbi = 0
            order = list(range(NT)) if dr == 0 else list(range(NT - 1, -1, -1))
            edge = 63 if dr == 0 else 0
            for tg in order:
                tsl = tsl_(tg)
                bz = psum("a")
                mm_group(bz, [wA[:, k, (2 + dr) * 128:(3 + dr) * 128] for k in range(KC)], [u3[:, k, tsl] for k in range(KC)], wkeys(sA) + allu(tg))
                P.op("act", [I("activation", out=B[0], in_=PS[bz], func=AF.Sigmoid)], r=[("ps", bz)], w=[K("B", 0)])
                yield
                P.op("dve", [I("tensor_scalar", out=B[0], in0=B[0], scalar1=om_ap, scalar2=lb_ap, op0=ALU.mult, op1=ALU.add)], r=[K("B", 0)] + CONSTR, w=[K("B", 0)])
                yield
                P.op("act", [I("activation", out=B[1], in_=B[0], func=AF.Identity, bias=1.0, scale=-1.0)], r=[K("B", 0)], w=[K("B", 1)])
                P.op("act", [I("activation", out=B[0], in_=B[0], func=AF.Ln)], r=[K("B", 0)], w=[K("B", 0)])
                yield
                P.op("dve", [I("tensor_tensor_scan", out=B[2], data0=scanmask, data1=B[0], initial=0.0, op0=ALU.mult, op1=ALU.add)], r=[K("B", 0), ("cf",)], w=[K("B", 2)])
                yield
                if dr == 1:
                    P.op("dve", [I("tensor_copy", out=tl3, in_=B3[2][:, :, 63:64])], r=[K("B", 2)], w=[K("tl")])
                    P.op("dve", [I("tensor_tensor", out=B[0], in0=B[0], in1=B[2], op=ALU.subtract)], r=[K("B", 0), K("B", 2)], w=[K("B", 0)])
                    yield
                    P.op("dve", [I("tensor_tensor", out=B3[2], in0=B3[0], in1=tl3.broadcast_to([128, 8, 64]), op=ALU.add)], r=[K("B", 0), K("tl")], w=[K("B", 2)])
                    yield
                P.op("dve", [I("tensor_copy", out=tl3, in_=B3[2][:, :, edge:edge + 1])], r=[K("B", 2)], w=[K("tl")])
                P.op("act", [I("activation", out=aE, in_=tl, func=AF.Exp)], r=[K("tl")], w=[K("aE")])
                P.op("dve", [I("tensor_tensor", out=B3[0], in0=B3[2], in1=B3[2][:, :, 32:33].broadcast_to([128, 8, 64]), op=ALU.subtract)], r=[K("B", 2)], w=[K("B", 0)])
                yield
                P.op("dve", [I("tensor_tensor", out=B3[3], in0=tl3.broadcast_to([128, 8, 64]), in1=B3[2], op=ALU.subtract)], r=[K("B", 2), K("tl")], w=[K("B", 3)])
                yield
                P.op("act", [I("activation", out=B[2], in_=B[2], func=AF.Exp)], r=[K("B", 2)], w=[K("B", 2)])
                P.op("act", [I("activation", out=B[4], in_=B[0], func=AF.Exp)], r=[K("B", 0)], w=[K("B", 4)])
                yield
                P.op("act", [I("activation", out=B[0], in_=B[0], func=AF.Exp, scale=-1.0)], r=[K("B", 0)], w=[K("B", 0)])
                P.op("act", [I("activation", out=B[3], in_=B[3], func=AF.Exp)], r=[K("B", 3)], w=[K("B", 3)])
                yield
                P.op("dve", [I("tensor_tensor", out=qm, in0=q_sb[:, tsl], in1=B[4], op=ALU.mult)], r=[("q", tg), K("B", 4)], w=[K("qm")])
                yield
                P.op("dve", [I("tensor_tensor", out=km, in0=B[1], in1=B[0], op=ALU.mult)], r=[K("B", 1), K("B", 0)], w=[K("km")])
                yield
                bs = psum("b")
                P.op("pe", [I("matmul", out=PS[bs][:, cp * 128:(cp + 1) * 128], lhsT=km[:, cp * 128:(cp + 1) * 128], rhs=qm[:, cp * 128:(cp + 1) * 128], start=True, stop=True)
                            for cp in range(4)], r=[K("km"), K("qm")], w=[("ps", bs)])
                P.op("dve", [I("tensor_tensor", out=kl, in0=B[1], in1=B[3], op=ALU.mult)], r=[K("B", 1), K("B", 3)], w=[K("kl")])
                yield
                P.op("dve", [I("tensor_tensor", out=scm.rearrange("p (c t) -> p c t", c=4), in0=PS[bs].rearrange("p (c t) -> p c t", c=4),
                               in1=mfb[dr].rearrange("p (o t) -> p o t", o=1).broadcast_to([128, 4, 128]), op=ALU.mult)],
                     r=[("ps", bs), ("cb",)], w=[K("scm")])
                yield
                P.op("dve", [I("tensor_tensor", out=qa, in0=q_sb[:, tsl], in1=B[2], op=ALU.mult)], r=[("q", tg), K("B", 2)], w=[K("qa")])
                yield
                bt = psum("b")
                ptb = PS[bt].bitcast(BF16)
                P.op("pe", [I("transpose", out=ptb[:, cp * 128:(cp + 1) * 128], in_=kl[:, cp * 128:(cp + 1) * 128], identity=ident) for cp in range(4)],
                     r=[K("kl"), ("cb",)], w=[("ps", bt)])
                P.op("act", [I("copy", out=kl_tok, in_=ptb[:, 0:512].rearrange("p (c k) -> p c k", c=4))], r=[("ps", bt)], w=[K("kltok")])
                yield
                bo = psum("c")
                corder = list(range(8)) if dr == 0 else list(range(7, -1, -1))
                for c in corder:
                    cp, hf = c // 2, c % 2
                    rows = slice(hf * 64, hf * 64 + 64)
                    tile = tg * 4 + cp
                    sb_cur = state_bf[sbi % 2]
                    sb_nxt = state_bf[(sbi + 1) % 2]
                    kcur, knxt = K("stb", sbi % 2), K("stb", (sbi + 1) % 2)
                    sbi += 1
                    pq = PS[6 + dr][:, 0:128]
                    P.op("pe", [I("matmul", out=pq, lhsT=kl_tok[rows, cp, :], rhs=i_tok[rows, tile, :], start=True, stop=True)],
                         r=[K("kltok"), ("itok", tg)], w=[("ps", 6 + dr)])
                    P.op("pe", [I("matmul", out=PS[bo][:, c * 64:(c + 1) * 64], lhsT=i_tok[rows, tile, :], rhs=scm[rows, cp * 128 + hf * 64:cp * 128 + hf * 64 + 64], start=True, stop=False),
                                I("matmul", out=PS[bo][:, c * 64:(c + 1) * 64], lhsT=sb_cur, rhs=qa[:, c * 64:(c + 1) * 64], start=False, stop=True)],
                         r=[("itok", tg), K("scm"), kcur, K("qa")], w=[("ps", bo)])
                    st_cur, st_nxt = state2[(sbi - 1) % 2], state2[sbi % 2]
                    P.op("dve", [I("scalar_tensor_tensor", out=sb_nxt, in0=st_cur, scalar=aE[:, c:c + 1], in1=pq, op0=ALU.mult, op1=ALU.add),
                                 I("scalar_tensor_tensor", out=st_nxt, in0=st_cur, scalar=aE[:, c:c + 1], in1=pq, op0=ALU.mult, op1=ALU.add)],
                         r=[K("st", (sbi - 1) % 2), K("aE"), ("ps", 6 + dr)], w=[K("st", sbi % 2), knxt])
                    yield
                P.op("dve", [I("tensor_tensor", out=o_acc[:, tsl], in0=o_acc[:, tsl], in1=PS[bo], op=ALU.add)], r=[("ps", bo), ("oacc", tg)], w=[("oacc", tg)])
                yield

        run_streams([dir_stream(0), dir_stream(1)])
        B = Bs[0]
        osq = ABF(4096, 512)
        K0 = lambda i: ("B", 0, i)
        gcol = CV_ON + l * 4 + hd
        for tg in range(NT):
            tsl = tsl_(tg)
            P.op("act", [I("activation", out=osq, in_=o_acc[:, tsl], func=AF.Square)], r=[("oacc", tg)], w=[K0(0)])
            bn = psum("d")
            P.op("pe", [I("matmul", out=PS[bn], lhsT=ones, rhs=osq, start=True, stop=True)], r=[K0(0), ("cb",)], w=[("ps", bn)])
            P.op("act", [I("activation", out=B[1], in_=PS[bn], func=AF.Ln, bias=EPS, scale=1.0 / 128)], r=[("ps", bn)], w=[K0(1)])
            P.op("act", [I("activation", out=B[1], in_=B[1], func=AF.Exp, scale=-0.5)], r=[K0(1)], w=[K0(1)])
            bg = psum("a")
            mm_group(bg, [wB[:, k, 0:128] for k in range(KC)], [u3[:, k, tsl] for k in range(KC)], wkeys(sB) + allu(tg))
            P.op("act", [I("activation", out=B[2], in_=PS[bg], func=AF.Silu)], r=[("ps", bg)], w=[K0(2)])
            P.op("dve", [I("scalar_tensor_tensor", out=B[3], in0=o_acc[:, tsl], scalar=cv[:, gcol:gcol + 1], in1=B[1], op0=ALU.mult, op1=ALU.mult)],
                 r=[("oacc", tg), K0(1)] + CONSTR, w=[K0(3)])
            P.op("dve", [I("tensor_tensor", out=y3[:, hd, tsl], in0=B[3], in1=B[2], op=ALU.mult)], r=[K0(3), K0(2)], w=[ykey(hd, tg)])

    def attn_pair(l, j, sA):
        wA = v8(slot_view(sA))
        qkv = [ABF(i * 1024, 2048) for i in range(3)]
        vtok = ABF(3072, 3072).rearrange("p (t v) -> p t v", t=16)
        acc = AF32(4608, 4096).rearrange("p (a t) -> p a t", a=2)
        NS = 2
        pTs = [ABF(8704 + i * 512, 1024) for i in range(NS)]
        Wp = [ABF(9728 + i * 256, 512) for i in range(2)]
        dtmp = AF32(9216, 512)
        sbanks = [(2, 3), (6, 7)]
        pvbanks = [(4, 0), (5, 1)]
        allu = lambda tt: [ukey(k, tt) for k in range(KC)]
        for i in range(3):
            for tt in range(NT):
                tsl = tsl_(tt)
                b = psum("a")
                mm_group(b, [wA[:, k, i * 128:(i + 1) * 128] for k in range(KC)], [u3[:, k, tsl] for k in range(KC)], wkeys(sA) + allu(tt))
                P.op("act", [I("copy", out=qkv[i][:, tsl], in_=PS[b])], r=[("ps", b)], w=[("qkv", i, tt)])
        qT, kT, vT = qkv
        qk_keys = [("qkv", i, tt) for i in range(2) for tt in range(NT)]
        v_keys = [("qkv", 2, tt) for tt in range(NT)]
        for tt in range(NT):
            P.op("dve", [I("memset", ap=acc[:, :, tsl_(tt)], constant=0.0)], w=[("acc", tt)])
        P.op("dve", [I("memset", ap=vtok[:, :, 64:128], constant=1.0)], w=[("vones",)])
        wi = 0
        for dil in (1, 4, 16):
            L = S // dil
            nkt = L // 128
            Wc = Wp[wi % 2]
            wkey = ("Wp", wi % 2)
            wi += 1
            for hh in range(2):
                c = (2.0 ** (-(2 * j + hh + 1))) * dil
                P.op("act", [I("activation", out=Wc[:, hh * 256:(hh + 1) * 256], in_=relabs, func=AF.Exp, scale=-c)], r=[("cf",)], w=[wkey + (hh,)])
                P.op("dve", [I("tensor_tensor", out=Wc[:, hh * 256:(hh + 1) * 256], in0=Wc[:, hh * 256:(hh + 1) * 256], in1=valid, op=ALU.mult)],
                     r=[wkey + (hh,), ("cb",)], w=[wkey + (hh,)])
            wkeys_ = [wkey + (0,), wkey + (1,)]
            for r_ in range(dil):
                for kt0 in range(0, nkt, 4):
                    nk = min(4, nkt - kt0)
                    bt = psum("e")
                    ptb = PS[bt].bitcast(BF16)
                    ins = []
                    for q in range(nk):
                        t0 = r_ + dil * 128 * (kt0 + q)
                        ins.append(I("transpose", out=ptb[:, q * 128:(q + 1) * 128], in_=vT[:, t0:t0 + 127 * dil + 1:dil], identity=ident))
                    P.op("pe", ins, r=v_keys + [("cb",)], w=[("ps", bt)])
                    ti = r_ * nkt + kt0
                    pt4 = ptb[:, 0:nk * 128].rearrange("p (t h e) -> p t h e", h=2, e=64)
                    P.op("act", [I("copy", out=vtok[:, ti:ti + nk, 0:64], in_=pt4[:, :, 0, :]),
                                 I("copy", out=vtok[:, ti:ti + nk, 128:192], in_=pt4[:, :, 1, :])], r=[("ps", bt)], w=[("vtok", ti)])
            vt_keys = [("vtok", r_ * nkt + kt0) for r_ in range(dil) for kt0 in range(0, nkt, 4)]
            units = []
            for r_ in range(dil):
                for kt in range(nkt):
                    x0 = 64 if kt == 0 else 0
                    x1 = 192 if kt == nkt - 1 else 256
                    base = 128 * kt - 64
                    a, bnd = base + x0, base + x1
                    pieces = []
                    if dil == 1:
                        for u in range((a + 64) // 512, (bnd - 1 + 64) // 512 + 1):
                            ma, mb = max(a, 512 * u - 64), min(bnd, 512 * u + 448)
                            pieces.append((("b", u), ma + 64 - 512 * u, ma - base, mb - base))
                    elif dil == 4:
                        pieces.append((("b", r_), a, x0, x1))
                    else:
                        pieces.append((("b", r_ // 4), (r_ % 4) * 128 + a, x0, x1))
                    units.append(dict(r=r_, kt=kt, x0=x0, x1=x1, base=base, pieces=pieces))
            inst_order = []
            contrib = {}
            for ui, un in enumerate(units):
                for pi_, pc in enumerate(un["pieces"]):
                    if pc[0] not in contrib:
                        contrib[pc[0]] = []
                        inst_order.append(pc[0])
                    contrib[pc[0]].append((ui, pi_))
            inst_idx = {k: n for n, k in enumerate(inst_order)}
            upairs = [units[i:i + 2] for i in range(0, len(units), 2)]

            def evac(inst, hh, dil=dil, L=L):
                b = pvbanks[hh][inst_idx[inst] % 2]
                if dil == 1:
                    u = inst[1]
                    m0, m1 = max(0, 512 * u - 64), min(L, 512 * u + 448)
                    c0 = m0 + 64 - 512 * u
                    dst = acc[:, hh, m0:m1]
                    src = PS[b][:, c0:c0 + (m1 - m0)]
                    akeys = [("acc", t) for t in range(m0 // TW, (m1 - 1) // TW + 1)]
                elif dil == 4:
                    dst = acc[:, hh, inst[1]:inst[1] + 4 * 511 + 1:4]
                    src = PS[b]
                    akeys = [("acc", t) for t in range(NT)]
                else:
                    g = inst[1]
                    dst = acc[:, hh, :].rearrange("p (m r) -> p r m", r=16)[:, 4 * g:4 * g + 4, :]
                    src = PS[b].rearrange("p (r m) -> p r m", r=4)
                    akeys = [("acc", t) for t in range(NT)]
                P.op("dve", [I("tensor_tensor", out=dst, in0=dst, in1=src, op=ALU.add)], r=[("ps", b)] + akeys, w=akeys)

            def blk_stream(si, hh, dil=dil, nkt=nkt, Wc=Wc, wkey=wkey, vt_keys=vt_keys, units=units, upairs=upairs, contrib=contrib, inst_idx=inst_idx):
                p_ = pTs[si][:, hh * 512:(hh + 1) * 512]
                pkey = ("pT", si, hh)
                rows = slice(hh * 64, hh * 64 + 64)
                sb = sbanks[si][hh]
                for up in upairs[si::NS]:
                    ncol = 256 * len(up)
                    ins = []
                    for slot, un in enumerate(up):
                        k0 = un["r"] + dil * 128 * un["kt"]
                        q0 = un["r"] + dil * (un["base"] + un["x0"])
                        nq = un["x1"] - un["x0"]
                        ins.append(I("matmul", out=PS[sb][:, slot * 256 + un["x0"]:slot * 256 + un["x1"]], lhsT=kT[rows, k0:k0 + 127 * dil + 1:dil],
                                     rhs=qT[rows, q0:q0 + (nq - 1) * dil + 1:dil], start=True, stop=True))
                    P.op("pe", ins, r=qk_keys, w=[("ps", sb)])
                    P.op("act", [I("activation", out=p_[:, 0:ncol], in_=PS[sb][:, 0:ncol], func=AF.Exp, scale=0.125)], r=[("ps", sb)], w=[pkey])
                    yield
                    nsl = len(up)
                    pv = p_.rearrange("p (s x) -> p s x", s=2)[:, 0:nsl, :]
                    wv = Wc[:, hh * 256:(hh + 1) * 256].rearrange("p (o x) -> p o x", o=1).broadcast_to([128, nsl, 256])
                    P.op("dve", [I("tensor_tensor", out=pv, in0=pv, in1=wv, op=ALU.mult)], r=[pkey, wkey + (hh,)], w=[pkey])
                    yield
                    ins = []
                    wb = set()
                    closing = []
                    for slot, un in enumerate(up):
                        ui = units.index(un)
                        tile = un["r"] * nkt + un["kt"]
                        for pi_, (inst, col0, xa, xb) in enumerate(un["pieces"]):
                            first = contrib[inst][0] == (ui, pi_)
                            last = contrib[inst][-1] == (ui, pi_)
                            b = pvbanks[hh][inst_idx[inst] % 2]
                            wb.add(b)
                            ins.append(I("matmul", out=PS[b][:, col0:col0 + (xb - xa)], lhsT=vtok[:, tile, hh * 64:hh * 64 + 128],
                                         rhs=p_[:, slot * 256 + xa:slot * 256 + xb], start=first, stop=last, skip_group_check=True))
                            if last:
                                closing.append(inst)
                    P.op("pe", ins, r=[pkey] + vt_keys + [("vones",)], w=[("ps", b) for b in sorted(wb)])
                    for inst in closing:
                        evac(inst, hh)
                    yield

            run_streams([blk_stream(si, hh) for si in range(NS) for hh in range(2)])
        for tt in range(NT):
            tsl = tsl_(tt)
            dk = [("pT", 1, 0), ("pT", 1, 1)]
            P.op("act", [I("activation", out=dtmp[0:64, :], in_=acc[64:128, 0, tsl], func=AF.Ln), I("activation", out=dtmp[64:128, :], in_=acc[0:64, 1, tsl], func=AF.Ln)],
                 r=[("acc", tt)], w=dk)
            P.op("act", [I("activation", out=dtmp, in_=dtmp, func=AF.Exp, scale=-1.0)], r=dk, w=dk)
            P.op("dve", [I("tensor_tensor", out=y3[0:64, 4 + j, tsl], in0=acc[0:64, 0, tsl], in1=dtmp[0:64, :], op=ALU.mult),
                         I("tensor_tensor", out=y3[64:128, 4 + j, tsl], in0=acc[64:128, 1, tsl], in1=dtmp[64:128, :], op=ALU.mult)],
                 r=[("acc", tt)] + dk, w=[ykey(4 + j, tt)])

    def mixer(l):
        P.barrier()
        norm(h3, hkey, CV_MIX + l * 8, u3, ukey, NT, TW)
        slots = load_hgrn(l, 0)
        for hd in range(4):
            nxt = load_hgrn(l, hd + 1) if hd < 3 else load_attn(l, 0)
            hgrn_head(l, hd, slots)
            slots = nxt
        P.barrier()
        for j in range(4):
            nxt = load_attn(l, j + 1) if j < 3 else None
            attn_pair(l, j, slots)
            slots = nxt
        P.barrier()
        proj(y3, ykey, Wd["w_out"][l], D, lambda dc, tt, b: add_to_h(b, dc, tt, 1.0))

    def xattn(l, s):
        P.barrier()
        memf = AF32(0, 2048).rearrange("p (k m) -> p k m", k=KC)
        memn = ABF(2048, 2048).rearrange("p (k m) -> p k m", k=KC)
        kTm = ABF(3072, 2048).rearrange("p (k m) -> p k m", k=KC)
        vm = ABF(4096, 2048).rearrange("p (t c) -> p t c", t=2)
        pT = [ABF(5120 + i * 256, 512) for i in range(4)]
        rden = AF32(6144, 512)
        P.op("sp", [I("dma_start", out=memf, in_=memT[s].rearrange("(k p) m -> p k m", p=128))], w=[("memf",)], slot="mem")
        norm(memf, lambda kc, tt: ("memf",), CV_MEM + l * 8, memn, lambda kc, tt: ("memn", kc), 1, NMEM)
        norm(h3, hkey, CV_XQ + l * 8, u3, ukey, NT, TW)
        Wkv = Wd["w_xkv"][l]
        mkeys = [("memn", k) for k in range(KC)]
        for g in range(4):
            sw = wload([(lambda sv: v8(sv), wsrc(Wkv, g * 512, 512))])
            w8 = v8(slot_view(sw))
            if g < 2:
                for mh in range(2):
                    b = psum("a")
                    ins = []
                    for cc in range(4):
                        for k in range(KC):
                            ins.append(I("matmul", out=PS[b][:, cc * 128:cc * 128 + 128], lhsT=w8[:, k, cc * 128:(cc + 1) * 128], rhs=memn[:, k, mh * 128:(mh + 1) * 128],
                                         start=(k == 0), stop=(k == KC - 1)))
                    P.op("pe", ins, r=wkeys(sw) + mkeys, w=[("ps", b)])
                    P.op("act", [I("copy", out=kTm[:, g * 4:(g + 1) * 4, mh * 128:(mh + 1) * 128], in_=PS[b].rearrange("p (c m) -> p c m", c=4))],
                         r=[("ps", b)], w=[("kTm", g, mh)])
            else:
                for mt in range(2):
                    b = psum("a")
                    mm_group(b, [memn[:, k, mt * 128:(mt + 1) * 128] for k in range(KC)], [w8[:, k, :] for k in range(KC)], wkeys(sw) + mkeys)
                    P.op("act", [I("copy", out=vm[:, mt, (g - 2) * 512:(g - 1) * 512], in_=PS[b])], r=[("ps", b)], w=[("vm", g, mt)])
        ktkeys = [("kTm", g, mh) for g in range(2) for mh in range(2)]
        vmkeys = [("vm", g, mt) for g in (2, 3) for mt in range(2)]

        def epi_q(c, tt, b):
            P.op("act", [I("copy", out=y3[:, c, tsl_(tt)], in_=PS[b])], r=[("ps", b)], w=[ykey(c, tt)])
        proj(u3, ukey, Wd["w_xq"][l], D, epi_q)
        pi = 0
        for tt in range(NT):
            tsl = tsl_(tt)
            for m in range(4):
                pk = []
                for mt in range(2):
                    b = psum("b")
                    mm_group(b, [kTm[:, 2 * m + k, mt * 128:(mt + 1) * 128] for k in range(2)], [y3[:, 2 * m + k, tsl] for k in range(2)],
                             ktkeys + [ykey(2 * m, tt), ykey(2 * m + 1, tt)])
                    p_ = pT[pi % 4]
                    pkey = ("xp", pi % 4)
                    pi += 1
                    P.op("act", [I("activation", out=p_, in_=PS[b], func=AF.Exp, scale=1.0 / 16.0)], r=[("ps", b)], w=[pkey])
                    pk.append((p_, pkey))
                bd = psum("d")
                mm_group(bd, [ones, ones], [pk[0][0], pk[1][0]], [pk[0][1], pk[1][1], ("cb",)])
                P.op("act", [I("activation", out=rden, in_=PS[bd], func=AF.Ln)], r=[("ps", bd)], w=[("rden",)])
                P.op("act", [I("activation", out=rden, in_=rden, func=AF.Exp, scale=-1.0)], r=[("rden",)], w=[("rden",)])
                for cc in range(2):
                    bo = psum("c")
                    mm_group(bo, [vm[:, k, (2 * m + cc) * 128:(2 * m + cc + 1) * 128] for k in range(2)], [pk[0][0], pk[1][0]],
                             [pk[0][1], pk[1][1]] + vmkeys)
                    P.op("dve", [I("tensor_tensor", out=u3[:, 2 * m + cc, tsl], in0=PS[bo], in1=rden, op=ALU.mult)],
                         r=[("ps", bo), ("rden",)], w=[ukey(2 * m + cc, tt)])
        proj(u3, ukey, Wd["w_xo"][l], D, lambda dc, tt, b: add_to_h(b, dc, tt, 1.0))

    stages = []
    for l in range(DEPTH):
        stages += [("ffn1", l), ("mixer", l), ("xattn", l), ("ffn2", l)]
    if stop_after is not None:
        stages = stages[:stop_after]
    for s in range(nseq):
        for kc in range(KC):
            P.op("sp", [I("dma_start", out=h3[:, kc, :], in_=xT[s, kc * 128:(kc + 1) * 128, :])], w=[hkey(kc, tt) for tt in range(NT)], slot="x%d" % kc)
        for st, l in stages:
            if st == "ffn1":
                ffn(l, 1)
            elif st == "mixer":
                mixer(l)
            elif st == "xattn":
                xattn(l, s)
            else:
                ffn(l, 2)
        if stop_after is None:
            norm(h3, hkey, CV_FIN, h3, hkey, NT, TW)
        for kc in range(KC):
            P.op("sp", [I("dma_start", out=outT[s, kc * 128:(kc + 1) * 128, :], in_=h3[:, kc, :])], r=[hkey(kc, tt) for tt in range(NT)], slot="o%d" % kc)
    P.barrier()
    P.op("sp", None)
    P.emit()
    return nc


def host_tables():
    ct = np.zeros((128, NCT), np.float32)
    ct[:, CT_ID:CT_ID + 128] = np.eye(128, dtype=np.float32)
    ct[:, CT_ONE:CT_ONE + 128] = 1.0
    s = np.arange(128)[:, None]
    t = np.arange(128)[None, :]
    same = (s // 64) == (t // 64)
    ct[:, CT_MF:CT_MF + 128] = (same & (s <= t)).astype(np.float32)
    ct[:, CT_MB:CT_MB + 128] = (same & (s >= t)).astype(np.float32)
    j = np.arange(128)[:, None]
    i = np.arange(128)[None, :]
    xq = np.arange(256)[None, :]
    rel = np.abs(j + 64 - xq).astype(np.float32)
    ct[:, CT_VAL:CT_VAL + 256] = (rel <= 64).astype(np.float32)
    ct[:, CT_REL:CT_REL + 256] = np.minimum(rel, 80.0)
    r0 = np.abs(j - i).astype(np.float32)
    ct[:, CT_V0:CT_V0 + 128] = (r0 <= 64).astype(np.float32)
    ct[:, CT_R0:CT_R0 + 128] = np.minimum(r0, 80.0)
    sm = np.ones((128, 512), np.float32)
    sm[:, ::64] = 0.0
    ct[:, CT_SCAN:CT_SCAN + 512] = sm
    return ct


def pack_cvec(inp):
    cvv = np.zeros((128, NCV), np.float32)

    def put(col, a, inner):
        a = np.asarray(a, np.float32)
        lead = int(np.prod(a.shape[:-1])) if a.ndim > 1 else 1
        a = a.reshape(lead, inner, 128)
        cvv[:, col:col + lead * inner] = a.transpose(2, 0, 1).reshape(128, lead * inner)
    put(CV_FFN1, inp["ln_ffn1"], 8)
    put(CV_MIX, inp["ln_mix"], 8)
    put(CV_XQ, inp["ln_xq"], 8)
    put(CV_MEM, inp["ln_mem"], 8)
    put(CV_FFN2, inp["ln_ffn2"], 8)
    put(CV_FIN, inp["ln_final"], 8)
    put(CV_ON, inp["hgrn_out_norm"], 4)
    put(CV_LB, inp["hgrn_lb_logits"], 4)
    return cvv


_CACHE = {}


def run(inputs, nseq, core_ids, stop_after=None):
    key = (nseq, stop_after)
    if key not in _CACHE:
        _CACHE[key] = build(nseq, stop_after)
    nc = _CACHE[key]
    x = np.asarray(inputs["x"], np.float32)
    mem = np.asarray(inputs["mem"], np.float32)
    ct = host_tables()
    cvv = pack_cvec(inputs)
    wts = {nm: np.ascontiguousarray(np.asarray(inputs[nm], np.float32)) for nm, _ in WNAMES}
    in_maps = []
    for ci in range(len(core_ids)):
        sl = slice(ci * nseq, (ci + 1) * nseq)
        m = {"xT": np.ascontiguousarray(x[sl].transpose(0, 2, 1)), "memT": np.ascontiguousarray(mem[sl].transpose(0, 2, 1)),
             "cvec": cvv, "ctab": ct}
        m.update(wts)
        in_maps.append(m)
    res = run_bass_kernel_spmd(nc, in_maps, core_ids=core_ids)
    out = np.concatenate([np.asarray(r["outT"]).transpose(0, 2, 1) for r in res.results], axis=0)
    return np.ascontiguousarray(out.astype(np.float32))


def kernel(**inputs):
    B = inputs["x"].shape[0]
    return run(inputs, B // NCORES, list(range(NCORES)))
```

```python
import numpy as np
import concourse.bass as bass
import concourse.mybir as mybir
from concourse.bass_utils import run_bass_kernel_spmd

F32 = mybir.dt.float32
BF16 = mybir.dt.bfloat16
ALU = mybir.AluOpType
AF = mybir.ActivationFunctionType

D = 1024
S = 2048
NMEM = 256
DFF = 2816
DEPTH = 2
KC = 8
NT = 4
TW = 512
EPS = 1e-6
NCORES = 8
WNAMES = [("ffn1_w_gate", [DEPTH, D, DFF]), ("ffn1_w_up", [DEPTH, D, DFF]), ("ffn1_w_down", [DEPTH, DFF, D]),
          ("w_in", [DEPTH, D, 4096]), ("w_out", [DEPTH, D, D]), ("w_xq", [DEPTH, D, D]),
          ("w_xkv", [DEPTH, D, 2 * D]), ("w_xo", [DEPTH, D, D]),
          ("ffn2_w_gate", [DEPTH, D, DFF]), ("ffn2_w_up", [DEPTH, D, DFF]), ("ffn2_w_down", [DEPTH, DFF, D])]
CV_FFN1, CV_MIX, CV_XQ, CV_MEM, CV_FFN2, CV_FIN, CV_ON, CV_LB, NCV = 0, 16, 32, 48, 64, 80, 88, 96, 112
CT_ID, CT_ONE, CT_MF, CT_MB, CT_VAL, CT_V0, CT_REL, CT_R0, CT_SCAN, NCT = 0, 128, 256, 384, 512, 768, 896, 1152, 1280, 1792
EPOCH = 50000
SAME_ENG_SYNC = True


class Node:
    __slots__ = ("id", "eng", "fn", "deps", "slot", "sig", "cnt", "grp_last")


class Prog:
    ENGS = ["pe", "act", "dve", "pool", "sp"]

    def __init__(self, nc):
        self.nc = nc
        self.nodes = []
        self.lastw = {}
        self.readers = {}
        self.fence = {}
        self.last_on = {}
        self.last_dma = {}

    def op(self, eng, fn, r=(), w=(), slot=None):
        n = Node()
        n.id = len(self.nodes)
        n.eng = eng
        n.fn = fn
        n.slot = slot
        n.sig = slot is not None
        n.cnt = 0
        n.grp_last = None
        deps = set()
        w = list(w) + [k for k in r if k[0] == "ps" and k not in w]
        for k in r:
            if k in self.lastw:
                deps.add(self.lastw[k])
        for k in w:
            if k in self.lastw:
                deps.add(self.lastw[k])
            deps.update(self.readers.get(k, ()))
        if eng in self.fence:
            deps.update(self.fence.pop(eng))
        n.deps = deps
        for k in r:
            self.readers.setdefault(k, []).append(n.id)
        for k in w:
            self.lastw[k] = n.id
            self.readers[k] = []
        self.nodes.append(n)
        if slot is None:
            self.last_on[eng] = n.id
        else:
            self.last_dma[slot] = n.id
        return n.id

    def barrier(self):
        ids = set(self.last_on.values()) | set(self.last_dma.values())
        for e in self.ENGS:
            self.fence[e] = set(ids) | self.fence.get(e, set())

    def emit(self):
        nc = self.nc
        nodes = self.nodes
        for n in nodes:
            for d in n.deps:
                nodes[d].sig = True
        cnt = {}
        for n in nodes:
            if not n.sig:
                continue
            key = ("d", n.slot) if n.slot is not None else ("e", n.eng)
            cnt[key] = cnt.get(key, 0) + 1
            n.cnt = cnt[key]
        sems = {}

        def sem_for(key, c):
            ep = (c - 1) // EPOCH
            k = (key, ep)
            if k not in sems:
                sems[k] = nc.alloc_semaphore("s_%s_%s_%d" % (key[0], key[1], ep))
            return sems[k], (c - 1) % EPOCH + 1

        bname = {"pe": "tensor", "act": "scalar", "dve": "vector", "pool": "gpsimd", "sp": "sync"}
        with nc.Block() as block:
            for eng in self.ENGS:
                mine = [n for n in nodes if n.eng == eng]

                def body(e, mine=mine, eng=eng):
                    seen = {}
                    for n in mine:
                        need = {}
                        for d in n.deps:
                            dn = nodes[d]
                            if dn.slot is not None:
                                key = ("d", dn.slot)
                                if dn.grp_last is not None:
                                    dn = nodes[dn.grp_last]
                            else:
                                if dn.eng == eng and (eng == "pe" or not SAME_ENG_SYNC):
                                    continue
                                key = ("e", dn.eng)
                            if dn.cnt > need.get(key, 0):
                                need[key] = dn.cnt
                        for key, c in need.items():
                            if seen.get(key, 0) >= c:
                                continue
                            seen[key] = c
                            sm, v = sem_for(key, c)
                            e.wait_ge(sm, v * (16 if key[0] == "d" else 1))
                        if n.fn is None:
                            continue
                        ins = None
                        for m_, kw_ in n.fn:
                            ins = getattr(e, m_)(**kw_)
                        if n.sig:
                            key = ("d", n.slot) if n.slot is not None else ("e", n.eng)
                            sm, v = sem_for(key, n.cnt)
                            ins.then_inc(sm, 16 if n.slot is not None else 1)

                getattr(block, bname[eng])(body)


def I(m, **kw):
    return (m, kw)


def build(nseq, stop_after=None):
    nc = bass.Bass("TRN2", target_bir_lowering=False)
    P = Prog(nc)
    xT = nc.dram_tensor("xT", [nseq, D, S], F32, kind="ExternalInput").ap()
    memT = nc.dram_tensor("memT", [nseq, D, NMEM], F32, kind="ExternalInput").ap()
    Wd = {nm: nc.dram_tensor(nm, shp, F32, kind="ExternalInput").ap() for nm, shp in WNAMES}
    cvec_d = nc.dram_tensor("cvec", [128, NCV], F32, kind="ExternalInput").ap()
    ctab_d = nc.dram_tensor("ctab", [128, NCT], F32, kind="ExternalInput").ap()
    outT = nc.dram_tensor("outT", [nseq, D, S], F32, kind="ExternalOutput").ap()

    hbuf = nc.alloc_sbuf_tensor("hbuf", [128, KC * S], F32).ap()
    ubuf = nc.alloc_sbuf_tensor("ubuf", [128, KC * S], BF16).ap()
    ybuf = nc.alloc_sbuf_tensor("ybuf", [128, KC * S], BF16).ap()
    wpool = nc.alloc_sbuf_tensor("wpool", [128, 4 * 4096], BF16).ap()
    cb = nc.alloc_sbuf_tensor("cb", [128, 896], BF16).ap()
    cf = nc.alloc_sbuf_tensor("cf", [128, 896], F32).ap()
    cv = nc.alloc_sbuf_tensor("cv", [128, NCV + 48], F32).ap()
    ARENA = 10560
    arena = nc.alloc_sbuf_tensor("arena", [128, ARENA], F32).ap()
    PS = [nc.alloc_psum_tensor("ps%d" % b, [128, TW], F32).ap() for b in range(8)]

    h3 = hbuf.rearrange("p (k t) -> p k t", k=KC)
    u3 = ubuf.rearrange("p (k t) -> p k t", k=KC)
    y3 = ybuf.rearrange("p (k t) -> p k t", k=KC)
    ident = cb[:, 0:128]
    ones = cb[:, 128:256]
    mfb = [cb[:, 256:384], cb[:, 384:512]]
    valid = cb[:, 512:768]
    valid0 = cb[:, 768:896]
    relabs = cf[:, 0:256]
    rel0 = cf[:, 256:384]
    scanmask = cf[:, 384:896]
    CV_LBV, CV_OML, CV_TMP = NCV, NCV + 16, NCV + 32

    def AF32(off, n):
        assert off + n <= ARENA
        return arena[:, off:off + n]

    def ABF(off, n):
        assert off + n // 2 <= ARENA
        return arena[:, off:off + n // 2].bitcast(BF16)

    nsq = ABF(8960, 2 * TW)
    nrs = AF32(9472, TW)

    def tsl_(tt):
        return slice(tt * TW, (tt + 1) * TW)

    MAXP = 5
    wstate = {"next": 0, "nslots": 4}

    def slot_view(s):
        if s < 4:
            return wpool[:, s * 4096:(s + 1) * 4096]
        return ybuf[:, (s - 4) * 4096:(s - 3) * 4096]

    def wkeys(s):
        return [("w", s, i) for i in range(MAXP)]

    def wload(parts):
        s = wstate["next"] % wstate["nslots"]
        wstate["next"] += 1
        sv = slot_view(s)
        ids = []
        for i, (dst_fn, src) in enumerate(parts):
            wk = [("w", s, i)]
            if i == 0:
                wk += [("w", s, j) for j in range(len(parts), MAXP)]
            ids.append(P.op("pool", [I("dma_start", out=dst_fn(sv), in_=src)], w=wk, slot="w%d" % s))
        for i_ in ids:
            P.nodes[i_].grp_last = ids[-1]
        return s

    def v8(sv):
        return sv.rearrange("p (k c) -> p k c", k=8)

    def wsrc(Wl, c0, n):
        return Wl[:, c0:c0 + n].rearrange("(kc p) c -> p kc c", p=128)

    psr = {"a": [0, 1], "b": [2, 3], "c": [4, 5], "d": [6], "e": [7]}
    psn = {k: 0 for k in psr}
    psq = {"n": 0}

    def psum(cls):
        b = psr[cls][psn[cls] % len(psr[cls])]
        psn[cls] += 1
        return b

    P.op("sp", [I("dma_start", out=cv[:, 0:NCV], in_=cvec_d)], w=[("cv",)], slot="c0")
    P.op("sp", [I("dma_start", out=cf, in_=ctab_d[:, CT_REL:NCT])], w=[("cf",)], slot="c2")
    P.op("pool", [I("dma_start", out=cb, in_=ctab_d[:, 0:CT_REL])], w=[("cb",)], slot="c1")
    T0 = CV_TMP
    P.op("act", [I("activation", out=cv[:, T0:T0 + 16], in_=cv[:, CV_LB:CV_LB + 16], func=AF.Exp)], r=[("cv",)], w=[("cvt",)])
    P.op("dve", [I("tensor_tensor", out=cv[:, T0:T0 + 8], in0=cv[:, T0:T0 + 8], in1=cv[:, T0 + 8:T0 + 16], op=ALU.add)], r=[("cvt",)], w=[("cvt",)])
    P.op("dve", [I("reciprocal", out=cv[:, T0:T0 + 8], in_=cv[:, T0:T0 + 8])], r=[("cvt",)], w=[("cvt",)])
    P.op("dve", [I("memset", ap=cv[:, CV_LBV:CV_LBV + 8], constant=0.0)], w=[("lb0",)])
    P.op("dve", [I("tensor_tensor", out=cv[:, CV_LBV + 8:CV_LBV + 16], in0=cv[:, T0 + 8:T0 + 16], in1=cv[:, T0:T0 + 8], op=ALU.mult)], r=[("cvt",)], w=[("lb1",)])
    P.op("dve", [I("tensor_scalar", out=cv[:, CV_OML:CV_OML + 16], in0=cv[:, CV_LBV:CV_LBV + 16], scalar1=-1.0, scalar2=1.0, op0=ALU.mult, op1=ALU.add)],
         r=[("lb0",), ("lb1",)], w=[("lbd",)])
    CONSTR = [("cv",), ("cf",), ("cb",), ("lbd",), ("lb0",), ("lb1",)]

    hkey = lambda kc, tt: ("h", kc, tt)
    ukey = lambda kc, tt: ("u", kc, tt)
    ykey = lambda kc, tt: ("y", kc, tt)

    def norm(src3, srckey, gcol, dst3, dstkey, ntiles, tw, dn=D):
        for tt in range(ntiles):
            tsl = slice(tt * tw, (tt + 1) * tw)
            b = psum("d")
            for kc in range(KC):
                j = kc % 2
                sq = nsq[:, j * TW:j * TW + tw]
                P.op("act", [I("activation", out=sq, in_=src3[:, kc, tsl], func=AF.Square)], r=[srckey(kc, tt)], w=[("nsq", j)])
                P.op("pe", [I("matmul", out=PS[b][:, 0:tw], lhsT=ones, rhs=sq, start=(kc == 0), stop=(kc == KC - 1))], r=[("nsq", j), ("cb",)], w=[("ps", b)])
            P.op("act", [I("activation", out=nrs[:, 0:tw], in_=PS[b][:, 0:tw], func=AF.Ln, bias=EPS, scale=1.0 / dn)], r=[("ps", b)], w=[("nrs",)])
            P.op("act", [I("activation", out=nrs[:, 0:tw], in_=nrs[:, 0:tw], func=AF.Exp, scale=-0.5)], r=[("nrs",)], w=[("nrs",)])
            for kc in range(KC):
                P.op("dve", [I("scalar_tensor_tensor", out=dst3[:, kc, tsl], in0=src3[:, kc, tsl], scalar=cv[:, gcol + kc:gcol + kc + 1],
                               in1=nrs[:, 0:tw], op0=ALU.mult, op1=ALU.mult)],
                     r=[srckey(kc, tt), ("nrs",), ("cv",)], w=[dstkey(kc, tt)])

    def mm_group(b, lhs, rhs, rkeys, n=TW, col0=0):
        nk = len(lhs)
        P.op("pe", [I("matmul", out=PS[b][:, col0:col0 + n], lhsT=lhs[k], rhs=rhs[k], start=(k == 0), stop=(k == nk - 1)) for k in range(nk)],
             r=rkeys, w=[("ps", b)])

    def add_to_h(b, dc, tt, scale):
        tsl = tsl_(tt)
        P.op("dve", [I("scalar_tensor_tensor", out=h3[:, dc, tsl], in0=PS[b], scalar=scale, in1=h3[:, dc, tsl], op0=ALU.mult, op1=ALU.add)],
             r=[("ps", b), hkey(dc, tt)], w=[hkey(dc, tt)])

    def proj(src3, skey, Wl, ncols, epi, pcls="a"):
        for g0 in range(0, ncols, 512):
            gw = min(512, ncols - g0)
            s = wload([(lambda sv: v8(sv)[:, :, 0:gw], wsrc(Wl, g0, gw))])
            w8 = v8(slot_view(s))
            for tt in range(NT):
                tsl = tsl_(tt)
                for cc in range(gw // 128):
                    b = psum(pcls)
                    mm_group(b, [w8[:, k, cc * 128:(cc + 1) * 128] for k in range(KC)], [src3[:, k, tsl] for k in range(KC)],
                             wkeys(s) + [skey(k, tt) for k in range(KC)])
                    epi(g0 // 128 + cc, tt, b)

    def ffn(l, which):
        pre = "ffn%d_" % which
        gcol = (CV_FFN1 if which == 1 else CV_FFN2) + l * 8
        P.barrier()
        wstate["nslots"] = 8
        norm(h3, hkey, gcol, u3, ukey, NT, TW)
        Wg, Wu, Wdn = Wd[pre + "w_gate"][l], Wd[pre + "w_up"][l], Wd[pre + "w_down"][l]
        act = [ABF(0, 2048).rearrange("p (f t) -> p f t", f=4), ABF(1024, 2048).rearrange("p (f t) -> p f t", f=4)]
        sg = [AF32(2048, 512), AF32(2560, 512)]
        ai = 0
        si = 0
        for f0 in range(0, DFF, 512):
            gw = min(512, DFF - f0)
            nf = gw // 128
            sgt = wload([(lambda sv: v8(sv)[:, :, 0:gw], wsrc(Wg, f0, gw))])
            sup = wload([(lambda sv: v8(sv)[:, :, 0:gw], wsrc(Wu, f0, gw))])
            sdn = wload([(lambda sv: sv.rearrange("p (f d) -> p f d", f=4)[:, 0:nf, :], Wdn[f0:f0 + gw, :].rearrange("(f p) d -> p f d", p=128))])
            wg8, wu8 = v8(slot_view(sgt)), v8(slot_view(sup))
            wd4 = slot_view(sdn).rearrange("p (f d) -> p f d", f=4)
            for tt in range(NT):
                tsl = tsl_(tt)
                a = act[ai % 2]
                akey = ("act", ai % 2)
                ai += 1
                ukeys = [ukey(k, tt) for k in range(KC)]
                urhs = [u3[:, k, tsl] for k in range(KC)]
                for fc in range(nf):
                    bg = psum("a")
                    mm_group(bg, [wg8[:, k, fc * 128:(fc + 1) * 128] for k in range(KC)], urhs, wkeys(sgt) + ukeys)
                    bu = psum("b")
                    mm_group(bu, [wu8[:, k, fc * 128:(fc + 1) * 128] for k in range(KC)], urhs, wkeys(sup) + ukeys)
                    sgb = sg[si % 2]
                    sk = ("sg", si % 2)
                    si += 1
                    P.op("act", [I("activation", out=sgb, in_=PS[bg], func=AF.Silu)], r=[("ps", bg)], w=[sk])
                    P.op("dve", [I("tensor_tensor", out=a[:, fc, :], in0=sgb, in1=PS[bu], op=ALU.mult)], r=[sk, ("ps", bu)], w=[akey + (fc,)])
                for dc in range(KC):
                    bd = psum("c")
                    mm_group(bd, [wd4[:, k, dc * 128:(dc + 1) * 128] for k in range(nf)], [a[:, k, :] for k in range(nf)],
                             wkeys(sdn) + [akey + (fc,) for fc in range(nf)])
                    add_to_h(bd, dc, tt, 0.5)
        wstate["nslots"] = 4

    def run_streams(gens):
        gens = list(gens)
        while gens:
            for g in list(gens):
                try:
                    next(g)
                except StopIteration:
                    gens.remove(g)

    ext = ybuf[:, 8192:16384].bitcast(F32)

    def EF32(off, n):
        return ext[:, off:off + n]

    def EBF(off, n):
        return ext[:, off:off + n // 2].bitcast(BF16)

    def load_hgrn(l, hd):
        Win = Wd["w_in"][l]
        col = lambda j: j * 512 + hd * 128
        sA = wload([((lambda sv, j=j: v8(sv)[:, :, j * 128:(j + 1) * 128]), wsrc(Win, col(j), 128)) for j in range(4)])
        sB = wload([(lambda sv: v8(sv)[:, :, 0:128], wsrc(Win, col(4), 128))])
        return (sA, sB)

    def load_attn(l, j):
        Win = Wd["w_in"][l]
        cols = [2560 + j * 128, 3072 + j * 128, 3584 + j * 128]
        return wload([((lambda sv, i=i: v8(sv)[:, :, i * 128:(i + 1) * 128]), wsrc(Win, cols[i], 128)) for i in range(3)])

    def hgrn_head(l, hd, slots):
        sA, sB = slots
        wA, wB = v8(slot_view(sA)), v8(slot_view(sB))
        q_sb = ABF(0, 2048)
        i_tok = ABF(1024, 2048).rearrange("p (t v) -> p t v", t=16)
        o_acc = AF32(2048, 2048)
        Bs = [[AF32(4096 + i * 512, 512) for i in range(5)], [EF32(i * 512, 512) for i in range(5)]]
        QK = [[ABF(6656 + i * 256, 512) for i in range(4)], [EBF(2560 + i * 256, 512) for i in range(4)]]
        SCM = [ABF(7680, 512), EBF(3584, 512)]
        KLT = [ABF(7936, 512).rearrange("p (c k) -> p c k", c=4), EBF(3840, 512).rearrange("p (c k) -> p c k", c=4)]
        STATE = [[AF32(8192, 128), AF32(9984, 128)], [AF32(8320, 128), AF32(10112, 128)]]
        STB = [[ABF(8448, 128), ABF(8512, 128)], [ABF(8576, 128), ABF(8640, 128)]]
        AE = [AF32(8704, 8), AF32(8712, 8)]
        TL = [AF32(8720, 8), AF32(8728, 8)]
        lbc = CV_LBV + l * 8
        omc = CV_OML + l * 8
        allu = lambda tt: [ukey(k, tt) for k in range(KC)]
        for tt in range(NT):
            tsl = tsl_(tt)
            P.op("dve", [I("memset", ap=o_acc[:, tsl], constant=0.0)], w=[("oacc", tt)])
            b = psum("a")
            mm_group(b, [wA[:, k, 0:128] for k in range(KC)], [u3[:, k, tsl] for k in range(KC)], wkeys(sA) + allu(tt))
            P.op("act", [I("copy", out=q_sb[:, tsl], in_=PS[b])], r=[("ps", b)], w=[("q", tt)])
            b2 = psum("b")
            ins = []
            for t4 in range(4):
                for k in range(KC):
                    ins.append(I("matmul", out=PS[b2][:, t4 * 128:(t4 + 1) * 128], lhsT=u3[:, k, tt * TW + t4 * 128:tt * TW + (t4 + 1) * 128], rhs=wA[:, k, 128:256],
                                 start=(k == 0), stop=(k == KC - 1)))
            P.op("pe", ins, r=wkeys(sA) + allu(tt), w=[("ps", b2)])
            P.op("dve", [I("tensor_copy", out=i_tok[:, tt * 4:(tt + 1) * 4, :], in_=PS[b2].rearrange("p (t v) -> p t v", t=4))], r=[("ps", b2)], w=[("itok", tt)])

        def dir_stream(dr):
            B = Bs[dr]
            B3 = [x.rearrange("p (c t) -> p c t", t=64) for x in B]
            qa, qm, km, kl = QK[dr]
            scm, kl_tok, state2, state_bf, aE, tl = SCM[dr], KLT[dr], STATE[dr], STB[dr], AE[dr], TL[dr]
            tl3 = tl.rearrange("p (c o) -> p c o", o=1)
            K = lambda nm, *a: (nm, dr) + a
            lb_ap = cv[:, lbc + dr * 4 + hd:lbc + dr * 4 + hd + 1]
            om_ap = cv[:, omc + dr * 4 + hd:omc + dr * 4 + hd + 1]
            P.op("dve", [I("memset", ap=state2[0], constant=0.0)], w=[K("st", 0)])
            P.op("dve", [I("memset", ap=state_bf[0], constant=0.0)], w=[K("stb", 0)])
            yield
            sbi = 0
            order = list(range(NT)) if dr == 0 else list(range(NT - 1, -1, -1))
            edge = 63 if dr == 0 else 0
            for tg in order:
                tsl = tsl_(tg)
                bz = psum("a")
                mm_group(bz, [wA[:, k, (2 + dr) * 128:(3 + dr) * 128] for k in range(KC)], [u3[:, k, tsl] for k in range(KC)], wkeys(sA) + allu(tg))
                P.op("act", [I("activation", out=B[0], in_=PS[bz], func=AF.Exp, scale=-1.0)], r=[("ps", bz)], w=[K("B", 0)])
                yield
                P.op("act", [I("activation", out=B[0], in_=B[0], func=AF.Ln, bias=1.0)], r=[K("B", 0)], w=[K("B", 0)])
                yield
                P.op("act", [I("activation", out=B[0], in_=B[0], func=AF.Exp, scale=-1.0)], r=[K("B", 0)], w=[K("B", 0)])
                yield
                P.op("dve", [I("tensor_scalar", out=B[0], in0=B[0], scalar1=om_ap, scalar2=lb_ap, op0=ALU.mult, op1=ALU.add)], r=[K("B", 0)] + CONSTR, w=[K("B", 0)])
                yield
                P.op("act", [I("activation", out=B[1], in_=B[0], func=AF.Identity, bias=1.0, scale=-1.0)], r=[K("B", 0)], w=[K("B", 1)])
                P.op("act", [I("activation", out=B[0], in_=B[0], func=AF.Ln)], r=[K("B", 0)], w=[K("B", 0)])
                yield
                P.op("dve", [I("tensor_tensor_scan", out=B[2], data0=scanmask, data1=B[0], initial=0.0, op0=ALU.mult, op1=ALU.add)], r=[K("B", 0), ("cf",)], w=[K("B", 2)])
                yield
                if dr == 1:
                    P.op("dve", [I("tensor_copy", out=tl3, in_=B3[2][:, :, 63:64])], r=[K("B", 2)], w=[K("tl")])
                    P.op("dve", [I("tensor_tensor", out=B[0], in0=B[0], in1=B[2], op=ALU.subtract)], r=[K("B", 0), K("B", 2)], w=[K("B", 0)])
                    yield
                    P.op("dve", [I("tensor_tensor", out=B3[2], in0=B3[0], in1=tl3.broadcast_to([128, 8, 64]), op=ALU.add)], r=[K("B", 0), K("tl")], w=[K("B", 2)])
                    yield
                P.op("dve", [I("tensor_copy", out=tl3, in_=B3[2][:, :, edge:edge + 1])], r=[K("B", 2)], w=[K("tl")])
                P.op("act", [I("activation", out=aE, in_=tl, func=AF.Exp)], r=[K("tl")], w=[K("aE")])
                P.op("dve", [I("tensor_tensor", out=B3[0], in0=B3[2], in1=B3[2][:, :, 32:33].broadcast_to([128, 8, 64]), op=ALU.subtract)], r=[K("B", 2)], w=[K("B", 0)])
                yield
                P.op("dve", [I("tensor_tensor", out=B3[3], in0=tl3.broadcast_to([128, 8, 64]), in1=B3[2], op=ALU.subtract)], r=[K("B", 2), K("tl")], w=[K("B", 3)])
                yield
                P.op("act", [I("activation", out=B[2], in_=B[2], func=AF.Exp)], r=[K("B", 2)], w=[K("B", 2)])
                P.op("act", [I("activation", out=B[4], in_=B[0], func=AF.Exp)], r=[K("B", 0)], w=[K("B", 4)])
                yield
                P.op("act", [I("activation", out=B[0], in_=B[0], func=AF.Exp, scale=-1.0)], r=[K("B", 0)], w=[K("B", 0)])
                P.op("act", [I("activation", out=B[3], in_=B[3], func=AF.Exp)], r=[K("B", 3)], w=[K("B", 3)])
                yield
                P.op("dve", [I("tensor_tensor", out=qm, in0=q_sb[:, tsl], in1=B[4], op=ALU.mult)], r=[("q", tg), K("B", 4)], w=[K("qm")])
                yield
                P.op("dve", [I("tensor_tensor", out=km, in0=B[1], in1=B[0], op=ALU.mult)], r=[K("B", 1), K("B", 0)], w=[K("km")])
                yield
                bs = psum("b")
                P.op("pe", [I("matmul", out=PS[bs][:, cp * 128:(cp + 1) * 128], lhsT=km[:, cp * 128:(cp + 1) * 128], rhs=qm[:, cp * 128:(cp + 1) * 128], start=True, stop=True)
                            for cp in range(4)], r=[K("km"), K("qm")], w=[("ps", bs)])
                P.op("dve", [I("tensor_tensor", out=kl, in0=B[1], in1=B[3], op=ALU.mult)], r=[K("B", 1), K("B", 3)], w=[K("kl")])
                yield
                P.op("dve", [I("tensor_tensor", out=scm.rearrange("p (c t) -> p c t", c=4), in0=PS[bs].rearrange("p (c t) -> p c t", c=4),
                               in1=mfb[dr].rearrange("p (o t) -> p o t", o=1).broadcast_to([128, 4, 128]), op=ALU.mult)],
                     r=[("ps", bs), ("cb",)], w=[K("scm")])
                yield
                P.op("dve", [I("tensor_tensor", out=qa, in0=q_sb[:, tsl], in1=B[2], op=ALU.mult)], r=[("q", tg), K("B", 2)], w=[K("qa")])
                yield
                bt = psum("b")
                ptb = PS[bt].bitcast(BF16)
                P.op("pe", [I("transpose", out=ptb[:, cp * 128:(cp + 1) * 128], in_=kl[:, cp * 128:(cp + 1) * 128], identity=ident) for cp in range(4)],
                     r=[K("kl"), ("cb",)], w=[("ps", bt)])
                P.op("act", [I("copy", out=kl_tok, in_=ptb[:, 0:512].rearrange("p (c k) -> p c k", c=4))], r=[("ps", bt)], w=[K("kltok")])
                yield
                bo = psum("c")
                corder = list(range(8)) if dr == 0 else list(range(7, -1, -1))
                for c in corder:
                    cp, hf = c // 2, c % 2
                    rows = slice(hf * 64, hf * 64 + 64)
                    tile = tg * 4 + cp
                    sb_cur = state_bf[sbi % 2]
                    sb_nxt = state_bf[(sbi + 1) % 2]
                    kcur, knxt = K("stb", sbi % 2), K("stb", (sbi + 1) % 2)
                    sbi += 1
                    pq = PS[6 + dr][:, 0:128]
                    P.op("pe", [I("matmul", out=pq, lhsT=kl_tok[rows, cp, :], rhs=i_tok[rows, tile, :], start=True, stop=True)],
                         r=[K("kltok"), ("itok", tg)], w=[("ps", 6 + dr)])
                    P.op("pe", [I("matmul", out=PS[bo][:, c * 64:(c + 1) * 64], lhsT=i_tok[rows, tile, :], rhs=scm[rows, cp * 128 + hf * 64:cp * 128 + hf * 64 + 64], start=True, stop=False),
                                I("matmul", out=PS[bo][:, c * 64:(c + 1) * 64], lhsT=sb_cur, rhs=qa[:, c * 64:(c + 1) * 64], start=False, stop=True)],
                         r=[("itok", tg), K("scm"), kcur, K("qa")], w=[("ps", bo)])
                    st_cur, st_nxt = state2[(sbi - 1) % 2], state2[sbi % 2]
                    P.op("dve", [I("scalar_tensor_tensor", out=sb_nxt, in0=st_cur, scalar=aE[:, c:c + 1], in1=pq, op0=ALU.mult, op1=ALU.add),
                                 I("scalar_tensor_tensor", out=st_nxt, in0=st_cur, scalar=aE[:, c:c + 1], in1=pq, op0=ALU.mult, op1=ALU.add)],
                         r=[K("st", (sbi - 1) % 2), K("aE"), ("ps", 6 + dr)], w=[K("st", sbi % 2), knxt])
                    yield
                P.op("dve", [I("tensor_tensor", out=o_acc[:, tsl], in0=o_acc[:, tsl], in1=PS[bo], op=ALU.add)], r=[("ps", bo), ("oacc", tg)], w=[("oacc", tg)])
                yield

        run_streams([dir_stream(0), dir_stream(1)])
        B = Bs[0]
        osq = ABF(4096, 512)
        K0 = lambda i: ("B", 0, i)
        gcol = CV_ON + l * 4 + hd
        for tg in range(NT):
            tsl = tsl_(tg)
            P.op("act", [I("activation", out=osq, in_=o_acc[:, tsl], func=AF.Square)], r=[("oacc", tg)], w=[K0(0)])
            bn = psum("d")
            P.op("pe", [I("matmul", out=PS[bn], lhsT=ones, rhs=osq, start=True, stop=True)], r=[K0(0), ("cb",)], w=[("ps", bn)])
            P.op("act", [I("activation", out=B[1], in_=PS[bn], func=AF.Ln, bias=EPS, scale=1.0 / 128)], r=[("ps", bn)], w=[K0(1)])
            P.op("act", [I("activation", out=B[1], in_=B[1], func=AF.Exp, scale=-0.5)], r=[K0(1)], w=[K0(1)])
            bg = psum("a")
            mm_group(bg, [wB[:, k, 0:128] for k in range(KC)], [u3[:, k, tsl] for k in range(KC)], wkeys(sB) + allu(tg))
            P.op("act", [I("activation", out=B[2], in_=PS[bg], func=AF.Exp, scale=-1.0)], r=[("ps", bg)], w=[K0(2)])
            P.op("act", [I("activation", out=B[2], in_=B[2], func=AF.Ln, bias=1.0)], r=[K0(2)], w=[K0(2)])
            P.op("act", [I("activation", out=B[2], in_=B[2], func=AF.Exp, scale=-1.0)], r=[K0(2)], w=[K0(2)])
            P.op("dve", [I("tensor_tensor", out=B[2], in0=B[2], in1=PS[bg], op=ALU.mult)], r=[K0(2), ("ps", bg)], w=[K0(2)])
            P.op("dve", [I("scalar_tensor_tensor", out=B[3], in0=o_acc[:, tsl], scalar=cv[:, gcol:gcol + 1], in1=B[1], op0=ALU.mult, op1=ALU.mult)],
                 r=[("oacc", tg), K0(1)] + CONSTR, w=[K0(3)])
            P.op("dve", [I("tensor_tensor", out=y3[:, hd, tsl], in0=B[3], in1=B[2], op=ALU.mult)], r=[K0(3), K0(2)], w=[ykey(hd, tg)])

    def attn_pair(l, j, sA):
        wA = v8(slot_view(sA))
        qkv = [ABF(i * 1024, 2048) for i in range(3)]
        vtok = ABF(3072, 3072).rearrange("p (t v) -> p t v", t=16)
        acc = AF32(4608, 4096).rearrange("p (a t) -> p a t", a=2)
        NS = 2
        pTs = [ABF(8704 + i * 512, 1024) for i in range(NS)]
        Wp = [ABF(9728 + i * 256, 512) for i in range(2)]
        dtmp = AF32(9216, 512)
        sbanks = [(2, 3), (6, 7)]
        pvbanks = [(4, 0), (5, 1)]
        allu = lambda tt: [ukey(k, tt) for k in range(KC)]
        for i in range(3):
            for tt in range(NT):
                tsl = tsl_(tt)
                b = psum("a")
                mm_group(b, [wA[:, k, i * 128:(i + 1) * 128] for k in range(KC)], [u3[:, k, tsl] for k in range(KC)], wkeys(sA) + allu(tt))
                P.op("act", [I("copy", out=qkv[i][:, tsl], in_=PS[b])], r=[("ps", b)], w=[("qkv", i, tt)])
        qT, kT, vT = qkv
        qk_keys = [("qkv", i, tt) for i in range(2) for tt in range(NT)]
        v_keys = [("qkv", 2, tt) for tt in range(NT)]
        for tt in range(NT):
            P.op("dve", [I("memset", ap=acc[:, :, tsl_(tt)], constant=0.0)], w=[("acc", tt)])
        P.op("dve", [I("memset", ap=vtok[:, :, 64:128], constant=1.0)], w=[("vones",)])
        wi = 0
        for dil in (1, 4, 16):
            L = S // dil
            nkt = L // 128
            Wc = Wp[wi % 2]
            wkey = ("Wp", wi % 2)
            wi += 1
            for hh in range(2):
                c = (2.0 ** (-(2 * j + hh + 1))) * dil
                P.op("act", [I("activation", out=Wc[:, hh * 256:(hh + 1) * 256], in_=relabs, func=AF.Exp, scale=-c)], r=[("cf",)], w=[wkey + (hh,)])
                P.op("dve", [I("tensor_tensor", out=Wc[:, hh * 256:(hh + 1) * 256], in0=Wc[:, hh * 256:(hh + 1) * 256], in1=valid, op=ALU.mult)],
                     r=[wkey + (hh,), ("cb",)], w=[wkey + (hh,)])
            wkeys_ = [wkey + (0,), wkey + (1,)]
            for r_ in range(dil):
                for kt0 in range(0, nkt, 4):
                    nk = min(4, nkt - kt0)
                    bt = psum("e")
                    ptb = PS[bt].bitcast(BF16)
                    ins = []
                    for q in range(nk):
                        t0 = r_ + dil * 128 * (kt0 + q)
                        ins.append(I("transpose", out=ptb[:, q * 128:(q + 1) * 128], in_=vT[:, t0:t0 + 127 * dil + 1:dil], identity=ident))
                    P.op("pe", ins, r=v_keys + [("cb",)], w=[("ps", bt)])
                    ti = r_ * nkt + kt0
                    pt4 = ptb[:, 0:nk * 128].rearrange("p (t h e) -> p t h e", h=2, e=64)
                    P.op("act", [I("copy", out=vtok[:, ti:ti + nk, 0:64], in_=pt4[:, :, 0, :]),
                                 I("copy", out=vtok[:, ti:ti + nk, 128:192], in_=pt4[:, :, 1, :])], r=[("ps", bt)], w=[("vtok", ti)])
            vt_keys = [("vtok", r_ * nkt + kt0) for r_ in range(dil) for kt0 in range(0, nkt, 4)]
            units = []
            for r_ in range(dil):
                for kt in range(nkt):
                    x0 = 64 if kt == 0 else 0
                    x1 = 192 if kt == nkt - 1 else 256
                    base = 128 * kt - 64
                    a, bnd = base + x0, base + x1
                    pieces = []
                    if dil == 1:
                        for u in range((a + 64) // 512, (bnd - 1 + 64) // 512 + 1):
                            ma, mb = max(a, 512 * u - 64), min(bnd, 512 * u + 448)
                            pieces.append((("b", u), ma + 64 - 512 * u, ma - base, mb - base))
                    elif dil == 4:
                        pieces.append((("b", r_), a, x0, x1))
                    else:
                        pieces.append((("b", r_ // 4), (r_ % 4) * 128 + a, x0, x1))
                    units.append(dict(r=r_, kt=kt, x0=x0, x1=x1, base=base, pieces=pieces))
            inst_order = []
            contrib = {}
            for ui, un in enumerate(units):
                for pi_, pc in enumerate(un["pieces"]):
                    if pc[0] not in contrib:
                        contrib[pc[0]] = []
                        inst_order.append(pc[0])
                    contrib[pc[0]].append((ui, pi_))
            inst_idx = {k: n for n, k in enumerate(inst_order)}
            upairs = [units[i:i + 2] for i in range(0, len(units), 2)]

            def evac(inst, hh, dil=dil, L=L):
                b = pvbanks[hh][inst_idx[inst] % 2]
                if dil == 1:
                    u = inst[1]
                    m0, m1 = max(0, 512 * u - 64), min(L, 512 * u + 448)
                    c0 = m0 + 64 - 512 * u
                    dst = acc[:, hh, m0:m1]
                    src = PS[b][:, c0:c0 + (m1 - m0)]
                    akeys = [("acc", t) for t in range(m0 // TW, (m1 - 1) // TW + 1)]
                elif dil == 4:
                    dst = acc[:, hh, inst[1]:inst[1] + 4 * 511 + 1:4]
                    src = PS[b]
                    akeys = [("acc", t) for t in range(NT)]
                else:
                    g = inst[1]
                    dst = acc[:, hh, :].rearrange("p (m r) -> p r m", r=16)[:, 4 * g:4 * g + 4, :]
                    src = PS[b].rearrange("p (r m) -> p r m", r=4)
                    akeys = [("acc", t) for t in range(NT)]
                P.op("dve", [I("tensor_tensor", out=dst, in0=dst, in1=src, op=ALU.add)], r=[("ps", b)] + akeys, w=akeys)

            def blk_stream(si, hh, dil=dil, nkt=nkt, Wc=Wc, wkey=wkey, vt_keys=vt_keys, units=units, upairs=upairs, contrib=contrib, inst_idx=inst_idx):
                p_ = pTs[si][:, hh * 512:(hh + 1) * 512]
                pkey = ("pT", si, hh)
                rows = slice(hh * 64, hh * 64 + 64)
                sb = sbanks[si][hh]
                for up in upairs[si::NS]:
                    ncol = 256 * len(up)
                    ins = []
                    for slot, un in enumerate(up):
                        k0 = un["r"] + dil * 128 * un["kt"]
                        q0 = un["r"] + dil * (un["base"] + un["x0"])
                        nq = un["x1"] - un["x0"]
                        ins.append(I("matmul", out=PS[sb][:, slot * 256 + un["x0"]:slot * 256 + un["x1"]], lhsT=kT[rows, k0:k0 + 127 * dil + 1:dil],
                                     rhs=qT[rows, q0:q0 + (nq - 1) * dil + 1:dil], start=True, stop=True))
                    P.op("pe", ins, r=qk_keys, w=[("ps", sb)])
                    P.op("act", [I("activation", out=p_[:, 0:ncol], in_=PS[sb][:, 0:ncol], func=AF.Exp, scale=0.125)], r=[("ps", sb)], w=[pkey])
                    yield
                    nsl = len(up)
                    pv = p_.rearrange("p (s x) -> p s x", s=2)[:, 0:nsl, :]
                    wv = Wc[:, hh * 256:(hh + 1) * 256].rearrange("p (o x) -> p o x", o=1).broadcast_to([128, nsl, 256])
                    P.op("dve", [I("tensor_tensor", out=pv, in0=pv, in1=wv, op=ALU.mult)], r=[pkey, wkey + (hh,)], w=[pkey])
                    yield
                    ins = []
                    wb = set()
                    closing = []
                    for slot, un in enumerate(up):
                        ui = units.index(un)
                        tile = un["r"] * nkt + un["kt"]
                        for pi_, (inst, col0, xa, xb) in enumerate(un["pieces"]):
                            first = contrib[inst][0] == (ui, pi_)
                            last = contrib[inst][-1] == (ui, pi_)
                            b = pvbanks[hh][inst_idx[inst] % 2]
                            wb.add(b)
                            ins.append(I("matmul", out=PS[b][:, col0:col0 + (xb - xa)], lhsT=vtok[:, tile, hh * 64:hh * 64 + 128],
                                         rhs=p_[:, slot * 256 + xa:slot * 256 + xb], start=first, stop=last, skip_group_check=True))
                            if last:
                                closing.append(inst)
                    P.op("pe", ins, r=[pkey] + vt_keys + [("vones",)], w=[("ps", b) for b in sorted(wb)])
                    for inst in closing:
                        evac(inst, hh)
                    yield

            run_streams([blk_stream(si, hh) for si in range(NS) for hh in range(2)])
        for tt in range(NT):
            tsl = tsl_(tt)
            dk = [("pT", 1, 0), ("pT", 1, 1)]
            P.op("act", [I("activation", out=dtmp[0:64, :], in_=acc[64:128, 0, tsl], func=AF.Ln), I("activation", out=dtmp[64:128, :], in_=acc[0:64, 1, tsl], func=AF.Ln)],
                 r=[("acc", tt)], w=dk)
            P.op("act", [I("activation", out=dtmp, in_=dtmp, func=AF.Exp, scale=-1.0)], r=dk, w=dk)
            P.op("dve", [I("tensor_tensor", out=y3[0:64, 4 + j, tsl], in0=acc[0:64, 0, tsl], in1=dtmp[0:64, :], op=ALU.mult),
                         I("tensor_tensor", out=y3[64:128, 4 + j, tsl], in0=acc[64:128, 1, tsl], in1=dtmp[64:128, :], op=ALU.mult)],
                 r=[("acc", tt)] + dk, w=[ykey(4 + j, tt)])

    def mixer(l):
        P.barrier()
        norm(h3, hkey, CV_MIX + l * 8, u3, ukey, NT, TW)
        slots = load_hgrn(l, 0)
        for hd in range(4):
            nxt = load_hgrn(l, hd + 1) if hd < 3 else load_attn(l, 0)
            hgrn_head(l, hd, slots)
            slots = nxt
        P.barrier()
        for j in range(4):
            nxt = load_attn(l, j + 1) if j < 3 else None
            attn_pair(l, j, slots)
            slots = nxt
        P.barrier()
        proj(y3, ykey, Wd["w_out"][l], D, lambda dc, tt, b: add_to_h(b, dc, tt, 1.0))

    def xattn(l, s):
        P.barrier()
        memf = AF32(0, 2048).rearrange("p (k m) -> p k m", k=KC)
        memn = ABF(2048, 2048).rearrange("p (k m) -> p k m", k=KC)
        kTm = ABF(3072, 2048).rearrange("p (k m) -> p k m", k=KC)
        vm = ABF(4096, 2048).rearrange("p (t c) -> p t c", t=2)
        pT = [ABF(5120 + i * 256, 512) for i in range(4)]
        rden = AF32(6144, 512)
        P.op("sp", [I("dma_start", out=memf, in_=memT[s].rearrange("(k p) m -> p k m", p=128))], w=[("memf",)], slot="mem")
        norm(memf, lambda kc, tt: ("memf",), CV_MEM + l * 8, memn, lambda kc, tt: ("memn", kc), 1, NMEM)
        norm(h3, hkey, CV_XQ + l * 8, u3, ukey, NT, TW)
        Wkv = Wd["w_xkv"][l]
        mkeys = [("memn", k) for k in range(KC)]
        for g in range(4):
            sw = wload([(lambda sv: v8(sv), wsrc(Wkv, g * 512, 512))])
            w8 = v8(slot_view(sw))
            if g < 2:
                for mh in range(2):
                    b = psum("a")
                    ins = []
                    for cc in range(4):
                        for k in range(KC):
                            ins.append(I("matmul", out=PS[b][:, cc * 128:cc * 128 + 128], lhsT=w8[:, k, cc * 128:(cc + 1) * 128], rhs=memn[:, k, mh * 128:(mh + 1) * 128],
                                         start=(k == 0), stop=(k == KC - 1)))
                    P.op("pe", ins, r=wkeys(sw) + mkeys, w=[("ps", b)])
                    P.op("act", [I("copy", out=kTm[:, g * 4:(g + 1) * 4, mh * 128:(mh + 1) * 128], in_=PS[b].rearrange("p (c m) -> p c m", c=4))],
                         r=[("ps", b)], w=[("kTm", g, mh)])
            else:
                for mt in range(2):
                    b = psum("a")
                    mm_group(b, [memn[:, k, mt * 128:(mt + 1) * 128] for k in range(KC)], [w8[:, k, :] for k in range(KC)], wkeys(sw) + mkeys)
                    P.op("act", [I("copy", out=vm[:, mt, (g - 2) * 512:(g - 1) * 512], in_=PS[b])], r=[("ps", b)], w=[("vm", g, mt)])
        ktkeys = [("kTm", g, mh) for g in range(2) for mh in range(2)]
        vmkeys = [("vm", g, mt) for g in (2, 3) for mt in range(2)]

        def epi_q(c, tt, b):
            P.op("act", [I("copy", out=y3[:, c, tsl_(tt)], in_=PS[b])], r=[("ps", b)], w=[ykey(c, tt)])
        proj(u3, ukey, Wd["w_xq"][l], D, epi_q)
        pi = 0
        for tt in range(NT):
            tsl = tsl_(tt)
            for m in range(4):
                pk = []
                for mt in range(2):
                    b = psum("b")
                    mm_group(b, [kTm[:, 2 * m + k, mt * 128:(mt + 1) * 128] for k in range(2)], [y3[:, 2 * m + k, tsl] for k in range(2)],
                             ktkeys + [ykey(2 * m, tt), ykey(2 * m + 1, tt)])
                    p_ = pT[pi % 4]
                    pkey = ("xp", pi % 4)
                    pi += 1
                    P.op("act", [I("activation", out=p_, in_=PS[b], func=AF.Exp, scale=1.0 / 16.0)], r=[("ps", b)], w=[pkey])
                    pk.append((p_, pkey))
                bd = psum("d")
                mm_group(bd, [ones, ones], [pk[0][0], pk[1][0]], [pk[0][1], pk[1][1], ("cb",)])
                P.op("act", [I("activation", out=rden, in_=PS[bd], func=AF.Ln)], r=[("ps", bd)], w=[("rden",)])
                P.op("act", [I("activation", out=rden, in_=rden, func=AF.Exp, scale=-1.0)], r=[("rden",)], w=[("rden",)])
                for cc in range(2):
                    bo = psum("c")
                    mm_group(bo, [vm[:, k, (2 * m + cc) * 128:(2 * m + cc + 1) * 128] for k in range(2)], [pk[0][0], pk[1][0]],
                             [pk[0][1], pk[1][1]] + vmkeys)
                    P.op("dve", [I("tensor_tensor", out=u3[:, 2 * m + cc, tsl], in0=PS[bo], in1=rden, op=ALU.mult)],
                         r=[("ps", bo), ("rden",)], w=[ukey(2 * m + cc, tt)])
        proj(u3, ukey, Wd["w_xo"][l], D, lambda dc, tt, b: add_to_h(b, dc, tt, 1.0))

    stages = []
    for l in range(DEPTH):
        stages += [("ffn1", l), ("mixer", l), ("xattn", l), ("ffn2", l)]
    if stop_after is not None:
        stages = stages[:stop_after]
    for s in range(nseq):
        for kc in range(KC):
            P.op("sp", [I("dma_start", out=h3[:, kc, :], in_=xT[s, kc * 128:(kc + 1) * 128, :])], w=[hkey(kc, tt) for tt in range(NT)], slot="x%d" % kc)
        for st, l in stages:
            if st == "ffn1":
                ffn(l, 1)
            elif st == "mixer":
                mixer(l)
            elif st == "xattn":
                xattn(l, s)
            else:
                ffn(l, 2)
        if stop_after is None:
            norm(h3, hkey, CV_FIN, h3, hkey, NT, TW)
        for kc in range(KC):
            P.op("sp", [I("dma_start", out=outT[s, kc * 128:(kc + 1) * 128, :], in_=h3[:, kc, :])], r=[hkey(kc, tt) for tt in range(NT)], slot="o%d" % kc)
    P.barrier()
    P.op("sp", None)
    P.emit()
    return nc


def host_tables():
    ct = np.zeros((128, NCT), np.float32)
    ct[:, CT_ID:CT_ID + 128] = np.eye(128, dtype=np.float32)
    ct[:, CT_ONE:CT_ONE + 128] = 1.0
    s = np.arange(128)[:, None]
    t = np.arange(128)[None, :]
    same = (s // 64) == (t // 64)
    ct[:, CT_MF:CT_MF + 128] = (same & (s <= t)).astype(np.float32)
    ct[:, CT_MB:CT_MB + 128] = (same & (s >= t)).astype(np.float32)
    j = np.arange(128)[:, None]
    i = np.arange(128)[None, :]
    xq = np.arange(256)[None, :]
    rel = np.abs(j + 64 - xq).astype(np.float32)
    ct[:, CT_VAL:CT_VAL + 256] = (rel <= 64).astype(np.float32)
    ct[:, CT_REL:CT_REL + 256] = np.minimum(rel, 80.0)
    r0 = np.abs(j - i).astype(np.float32)
    ct[:, CT_V0:CT_V0 + 128] = (r0 <= 64).astype(np.float32)
    ct[:, CT_R0:CT_R0 + 128] = np.minimum(r0, 80.0)
    sm = np.ones((128, 512), np.float32)
    sm[:, ::64] = 0.0
    ct[:, CT_SCAN:CT_SCAN + 512] = sm
    return ct


def pack_cvec(inp):
    cvv = np.zeros((128, NCV), np.float32)

    def put(col, a, inner):
        a = np.asarray(a, np.float32)
        lead = int(np.prod(a.shape[:-1])) if a.ndim > 1 else 1
        a = a.reshape(lead, inner, 128)
        cvv[:, col:col + lead * inner] = a.transpose(2, 0, 1).reshape(128, lead * inner)
    put(CV_FFN1, inp["ln_ffn1"], 8)
    put(CV_MIX, inp["ln_mix"], 8)
    put(CV_XQ, inp["ln_xq"], 8)
    put(CV_MEM, inp["ln_mem"], 8)
    put(CV_FFN2, inp["ln_ffn2"], 8)
    put(CV_FIN, inp["ln_final"], 8)
    put(CV_ON, inp["hgrn_out_norm"], 4)
    put(CV_LB, inp["hgrn_lb_logits"], 4)
    return cvv


_CACHE = {}


def run(inputs, nseq, core_ids, stop_after=None):
    key = (nseq, stop_after)
    if key not in _CACHE:
        _CACHE[key] = build(nseq, stop_after)
    nc = _CACHE[key]
    x = np.asarray(inputs["x"], np.float32)
    mem = np.asarray(inputs["mem"], np.float32)
    ct = host_tables()
    cvv = pack_cvec(inputs)
    wts = {nm: np.ascontiguousarray(np.asarray(inputs[nm], np.float32)) for nm, _ in WNAMES}
    in_maps = []
    for ci in range(len(core_ids)):
        sl = slice(ci * nseq, (ci + 1) * nseq)
        m = {"xT": np.ascontiguousarray(x[sl].transpose(0, 2, 1)), "memT": np.ascontiguousarray(mem[sl].transpose(0, 2, 1)),
             "cvec": cvv, "ctab": ct}
        m.update(wts)
        in_maps.append(m)
    res = run_bass_kernel_spmd(nc, in_maps, core_ids=core_ids)
    out = np.concatenate([np.asarray(r["outT"]).transpose(0, 2, 1) for r in res.results], axis=0)
    return np.ascontiguousarray(out.astype(np.float32))


def kernel(**inputs):
    B = inputs["x"].shape[0]
    return run(inputs, B // NCORES, list(range(NCORES)))
```
